# Optimizing a Trainium2 kernel written in Bass

```python
import math
import jax
import jax.numpy as jnp
from jax import lax
import numpy as np

D_MODEL = 1024
BATCH = 8
SEQ = 4096
DEPTH = 2

N_MIXERS = 2
EPS = 1e-6
NEG_INF = -1e30

MIX_WIDTH = D_MODEL
TOK_WIDTH = 3 * MIX_WIDTH // 4
MEM_WIDTH = MIX_WIDTH // 4

DILATED_GROUPS = ((128, 1), (512, 4), (2048, 16))
ATT_HEAD_DIM = 64
ATT_HEADS = TOK_WIDTH // ATT_HEAD_DIM
HEADS_PER_GROUP = ATT_HEADS // len(DILATED_GROUPS)
BAND_BLOCK = 64
REL_BUCKETS = 32
REL_MAX_DIST = 1024

DN_HEAD_DIM = 128
DN_HEADS = TOK_WIDTH // DN_HEAD_DIM
DN_CONV = 5
DN_CHUNK = 64

MEM_LEN = 256
MEM_HEADS = 4
MEM_HEAD_DIM = MEM_WIDTH // MEM_HEADS

_FF_RAW = -(-8 * D_MODEL // 3)
D_FF = -(-_FF_RAW // 256) * 256

ATT_IN = 3 * TOK_WIDTH + MEM_WIDTH
DN_IN = 4 * TOK_WIDTH + 4 * DN_HEADS + MEM_WIDTH

kernel_name = "hybrid_dilated_attn_gated_deltanet_encoder"


def _rms_norm(x, gain):
    xf = x.astype(jnp.float32)
    y = xf * lax.rsqrt(jnp.mean(xf * xf, axis=-1, keepdims=True) + EPS) * gain.astype(jnp.float32)
    return y.astype(x.dtype)


def _l2norm(t):
    return t * lax.rsqrt(jnp.sum(t * t, axis=-1, keepdims=True) + EPS)


def _t5_bucket(rel):
    half = REL_BUCKETS // 2
    max_exact = half // 2
    n = np.abs(rel)
    large = max_exact + (np.log(np.maximum(n, 1) / max_exact) / math.log(REL_MAX_DIST / max_exact)
                         * (half - max_exact)).astype(np.int64)
    large = np.minimum(large, half - 1)
    return ((rel > 0) * half + np.where(n < max_exact, n, large)).astype(np.int32)


def _dilated_band_attention(q, k, v, bias, dil, half):
    b, s, h, dh = q.shape
    L = s // dil
    nb = -(-L // BAND_BLOCK)
    lp = nb * BAND_BLOCK
    n = b * dil

    def to_sub(t):
        return t.reshape(b, L, dil, h, dh).transpose(0, 2, 1, 3, 4).reshape(n, L, h, dh)

    def key_windows(t):
        t = jnp.pad(to_sub(t), ((0, 0), (BAND_BLOCK, lp - L + BAND_BLOCK), (0, 0), (0, 0)))
        t = t.reshape(n, nb + 2, BAND_BLOCK, h, dh)
        return jnp.concatenate([t[:, :-2], t[:, 1:-1], t[:, 2:]], axis=2)

    qs = jnp.pad(to_sub(q), ((0, 0), (0, lp - L), (0, 0), (0, 0))).reshape(n, nb, BAND_BLOCK, h, dh)
    kw = key_windows(k)
    vw = key_windows(v)

    rel = np.arange(3 * BAND_BLOCK)[None, :] - BAND_BLOCK - np.arange(BAND_BLOCK)[:, None]
    in_band = np.abs(rel) <= half
    kpos = (np.arange(nb)[:, None] - 1) * BAND_BLOCK + np.arange(3 * BAND_BLOCK)[None, :]
    valid = in_band[None] & ((kpos >= 0) & (kpos < L))[:, None, :]
    bias_band = jnp.transpose(bias[np.clip(rel + half, 0, 2 * half)], (2, 0, 1)).astype(jnp.float32)

    logits = jnp.einsum('nbqhd,nbkhd->nbhqk', qs, kw, preferred_element_type=jnp.float32)
    logits = jnp.where(valid[None, :, None], logits + bias_band[None, None], NEG_INF)
    m = jnp.max(logits, axis=-1)
    p = jnp.exp(logits - m[..., None])
    den = jnp.sum(p, axis=-1)
    o = jnp.einsum('nbhqk,nbkhd->nbqhd', p.astype(v.dtype), vw, preferred_element_type=jnp.float32)
    o = o / jnp.swapaxes(den, -1, -2)[..., None]

    def from_sub(t):
        rest = t.shape[3:]
        t = t.reshape((b, dil, lp) + rest)[:, :, :L]
        return jnp.swapaxes(t, 1, 2).reshape((b, s) + rest)

    return from_sub(o), from_sub(jnp.swapaxes(m, -1, -2)), from_sub(jnp.swapaxes(den, -1, -2))


def _dilated_mixture(q, k, v, rel_bias):
    b, s = q.shape[:2]
    outs, lses = [], []
    for gi, (window, dil) in enumerate(DILATED_GROUPS):
        half = window // (2 * dil)
        heads = slice(gi * HEADS_PER_GROUP, (gi + 1) * HEADS_PER_GROUP)
        bias = rel_bias[_t5_bucket(np.arange(-half, half + 1) * dil)][:, heads]
        o, m, den = _dilated_band_attention(q[:, :, heads], k[:, :, heads], v[:, :, heads], bias, dil, half)
        outs.append(o)
        lses.append(m + jnp.log(den))
    wts = jax.nn.softmax(jnp.stack(lses), axis=0)
    mixed = jnp.concatenate([o * w[..., None] for o, w in zip(outs, wts)], axis=2)
    return mixed.reshape(b, s, TOK_WIDTH)


def _memory_attention(q_mem, mem_n, w_kv):
    b, s, _ = q_mem.shape
    q = q_mem.reshape(b, s, MEM_HEADS, MEM_HEAD_DIM) * (MEM_HEAD_DIM ** -0.5)
    k, v = jnp.split(mem_n @ w_kv, 2, axis=-1)
    k = k.reshape(b, -1, MEM_HEADS, MEM_HEAD_DIM)
    v = v.reshape(b, -1, MEM_HEADS, MEM_HEAD_DIM)
    logits = jnp.einsum('bshd,bmhd->bhsm', q, k, preferred_element_type=jnp.float32)
    p = jax.nn.softmax(logits, axis=-1)
    o = jnp.einsum('bhsm,bmhd->bshd', p.astype(v.dtype), v)
    return o.reshape(b, s, MEM_WIDTH)


def _gated_delta_chunked(q, k, v, g, beta):
    b, s, h, dk = q.shape
    dv = v.shape[-1]
    nc = s // DN_CHUNK

    def chunks(t):
        t = t.reshape((b, nc, DN_CHUNK, h) + t.shape[3:])
        return jnp.moveaxis(t, 3, 1)

    q, k, v, g, beta = (chunks(t) for t in (q, k, v, g, beta))
    g = jnp.cumsum(g, axis=-1)
    tril = np.tril(np.ones((DN_CHUNK, DN_CHUNK), dtype=bool))
    strict = np.tril(np.ones((DN_CHUNK, DN_CHUNK), dtype=bool), -1)
    decay = jnp.exp(jnp.where(tril, g[..., :, None] - g[..., None, :], NEG_INF))
    k_beta = k * beta[..., None]
    lmat = jnp.where(strict, jnp.einsum('bhnik,bhnjk->bhnij', k_beta, k) * decay, 0.0)
    a = lmat + jnp.eye(DN_CHUNK, dtype=jnp.float32)
    rhs = jnp.concatenate([v * beta[..., None], k_beta * jnp.exp(g)[..., None]], axis=-1)
    sol = lax.linalg.triangular_solve(a, rhs, left_side=True, lower=True, unit_diagonal=True)
    u, w = sol[..., :dv], sol[..., dv:]
    intra = jnp.where(tril, jnp.einsum('bhnik,bhnjk->bhnij', q, k) * decay, 0.0)

    def step(state, inp):
        qc, kc, uc, wc, gc, ac = inp
        v_new = uc - jnp.einsum('bhck,bhkv->bhcv', wc, state)
        out = (jnp.einsum('bhck,bhkv->bhcv', qc * jnp.exp(gc)[..., None], state)
               + jnp.einsum('bhij,bhjv->bhiv', ac, v_new))
        g_last = gc[..., -1]
        state = (state * jnp.exp(g_last)[..., None, None]
                 + jnp.einsum('bhck,bhcv->bhkv', kc * jnp.exp(g_last[..., None] - gc)[..., None], v_new))
        return state, out

    xs = tuple(jnp.moveaxis(t, 2, 0) for t in (q, k, u, w, g, intra))
    state0 = jnp.zeros((b, h, dk, dv), jnp.float32)
    _, out = lax.scan(step, state0, xs)
    out = jnp.moveaxis(out, 0, 2)
    return jnp.moveaxis(out, 1, 3).reshape(b, s, h, dv)


def _attention_sublayer(h, mem_n, w_in, w_out, rel_bias, w_mem_kv):
    b, s, _ = h.shape
    q, k, v, q_mem = jnp.split(h @ w_in, [TOK_WIDTH, 2 * TOK_WIDTH, 3 * TOK_WIDTH], axis=-1)
    q = q.reshape(b, s, ATT_HEADS, ATT_HEAD_DIM) * (ATT_HEAD_DIM ** -0.5)
    k = k.reshape(b, s, ATT_HEADS, ATT_HEAD_DIM)
    v = v.reshape(b, s, ATT_HEADS, ATT_HEAD_DIM)
    mixed = _dilated_mixture(q, k, v, rel_bias).astype(h.dtype)
    mem_out = _memory_attention(q_mem, mem_n, w_mem_kv)
    return jnp.concatenate([mixed, mem_out], axis=-1) @ w_out


def _deltanet_sublayer(h, mem_n, w_in, conv_w, a_log, dt_bias, out_norm, w_out, w_mem_kv):
    b, s, _ = h.shape
    qkv, z, gate_in, q_mem = jnp.split(
        h @ w_in, [3 * TOK_WIDTH, 4 * TOK_WIDTH, 4 * TOK_WIDTH + 4 * DN_HEADS], axis=-1)
    qkv = lax.conv_general_dilated(
        qkv, conv_w[:, None, :], window_strides=(1,), padding=[(DN_CONV // 2, DN_CONV // 2)],
        dimension_numbers=('NWC', 'WIO', 'NWC'), feature_group_count=3 * TOK_WIDTH)
    q, k, v = jnp.split(jax.nn.silu(qkv).astype(jnp.float32), 3, axis=-1)
    q = _l2norm(q.reshape(b, s, DN_HEADS, DN_HEAD_DIM)) * (DN_HEAD_DIM ** -0.5)
    k = _l2norm(k.reshape(b, s, DN_HEADS, DN_HEAD_DIM))
    v = v.reshape(b, s, DN_HEADS, DN_HEAD_DIM)
    gate_in = gate_in.astype(jnp.float32).reshape(b, s, 2, 2, DN_HEADS)
    g = -jnp.exp(a_log.astype(jnp.float32)) * jax.nn.softplus(gate_in[:, :, :, 0] + dt_bias.astype(jnp.float32))
    beta = jax.nn.sigmoid(gate_in[:, :, :, 1])
    o_fwd = _gated_delta_chunked(q, k, v, g[:, :, 0], beta[:, :, 0])
    rev = lambda t: jnp.flip(t, axis=1)
    o_bwd = rev(_gated_delta_chunked(rev(q), rev(k), rev(v), rev(g[:, :, 1]), rev(beta[:, :, 1])))
    o = o_fwd + o_bwd
    zf = z.astype(jnp.float32).reshape(b, s, DN_HEADS, DN_HEAD_DIM)
    o = (o * lax.rsqrt(jnp.mean(o * o, axis=-1, keepdims=True) + EPS)
         * out_norm.astype(jnp.float32) * jax.nn.silu(zf))
    o = o.reshape(b, s, TOK_WIDTH).astype(h.dtype)
    mem_out = _memory_attention(q_mem, mem_n, w_mem_kv)
    return jnp.concatenate([o, mem_out], axis=-1) @ w_out


def _swiglu(h, w_gate_up, w_down):
    gate, up = jnp.split(h @ w_gate_up, 2, axis=-1)
    return (jax.nn.silu(gate) * up) @ w_down


def setup_inputs(seed: int = 0) -> dict:
    key = jax.random.key(seed)
    ks = jax.random.split(key, 24)
    n_att = (DEPTH + N_MIXERS - 1) // N_MIXERS
    n_dn = DEPTH // N_MIXERS

    def nrm(k, shape, scale):
        return jax.random.normal(k, shape, jnp.float32) * scale

    def gain(k, shape):
        return 1.0 + nrm(k, shape, 0.05)

    dt = jnp.exp(jax.random.uniform(ks[8], (n_dn, 2, DN_HEADS), jnp.float32,
                                    minval=math.log(1e-3), maxval=math.log(1e-1)))
    return {
        "x": nrm(ks[0], (BATCH, SEQ, D_MODEL), 1.0),
        "mem": nrm(ks[1], (BATCH, MEM_LEN, D_MODEL), 1.0),
        "rel_bias": nrm(ks[2], (REL_BUCKETS, ATT_HEADS), 0.5),
        "att_w_in": nrm(ks[3], (n_att, D_MODEL, ATT_IN), D_MODEL ** -0.5),
        "att_w_out": nrm(ks[4], (n_att, MIX_WIDTH, D_MODEL), MIX_WIDTH ** -0.5),
        "dn_w_in": nrm(ks[5], (n_dn, D_MODEL, DN_IN), D_MODEL ** -0.5),
        "dn_conv": nrm(ks[6], (n_dn, DN_CONV, 3 * TOK_WIDTH), DN_CONV ** -0.5),
        "dn_a_log": jnp.log(jax.random.uniform(ks[7], (n_dn, 2, DN_HEADS), jnp.float32, minval=1.0, maxval=16.0)),
        "dn_dt_bias": dt + jnp.log(-jnp.expm1(-dt)),
        "dn_out_norm": gain(ks[9], (n_dn, DN_HEAD_DIM)),
        "dn_w_out": nrm(ks[10], (n_dn, MIX_WIDTH, D_MODEL), MIX_WIDTH ** -0.5),
        "mem_norm": gain(ks[11], (DEPTH, D_MODEL)),
        "mem_w_kv": nrm(ks[12], (DEPTH, D_MODEL, 2 * MEM_WIDTH), D_MODEL ** -0.5),
        "norm_mix_pre": gain(ks[13], (DEPTH, D_MODEL)),
        "norm_mix_post": gain(ks[14], (DEPTH, D_MODEL)),
        "norm_ffn_pre": gain(ks[15], (DEPTH, D_MODEL)),
        "norm_ffn_post": gain(ks[16], (DEPTH, D_MODEL)),
        "ffn_w_gate_up": nrm(ks[17], (DEPTH, D_MODEL, 2 * D_FF), D_MODEL ** -0.5),
        "ffn_w_down": nrm(ks[18], (DEPTH, D_FF, D_MODEL), D_FF ** -0.5),
    }


def reference(x, mem, rel_bias, att_w_in, att_w_out, dn_w_in, dn_conv, dn_a_log, dn_dt_bias,
              dn_out_norm, dn_w_out, mem_norm, mem_w_kv, norm_mix_pre, norm_mix_post,
              norm_ffn_pre, norm_ffn_post, ffn_w_gate_up, ffn_w_down):
    for i in range(DEPTH):
        j = i // N_MIXERS
        h = _rms_norm(x, norm_mix_pre[i])
        mem_n = _rms_norm(mem, mem_norm[i])
        if i % N_MIXERS == 0:
            mixed = _attention_sublayer(h, mem_n, att_w_in[j], att_w_out[j], rel_bias, mem_w_kv[i])
        else:
            mixed = _deltanet_sublayer(h, mem_n, dn_w_in[j], dn_conv[j], dn_a_log[j], dn_dt_bias[j],
                                       dn_out_norm[j], dn_w_out[j], mem_w_kv[i])
        x = x + _rms_norm(mixed, norm_mix_post[i])
        h = _rms_norm(x, norm_ffn_pre[i])
        x = x + _rms_norm(_swiglu(h, ffn_w_gate_up[i], ffn_w_down[i]), norm_ffn_post[i])
    return x
```

```python
import math
import types
from contextlib import ExitStack
import numpy as np
import concourse.bass as bass
import concourse.mybir as mybir
from concourse.bass_utils import run_bass_kernel_spmd

F32 = mybir.dt.float32
F32R = mybir.dt.float32r
BF16 = mybir.dt.bfloat16
AF = mybir.ActivationFunctionType
ALU = mybir.AluOpType
AX = mybir.AxisListType

ENGS = ("pe", "act", "dve", "pool", "sp")
N_DMA_SLOTS = 12

S = 4096
D = 1024
NT = S // 128
DFF = 2816
EPS = 1e-6
BIG = 1.0e30
GROUPS = ((128, 1), (512, 4), (2048, 16))
PSUM_NAMES = ("pT", "pF", "pV", "pL", "pO", "pY", "pG", "pU", "bk")
DBG = {"b1_heads": 6, "b1_steps": 32, "b1_stage": 9}


class Reg:
    __slots__ = ("name", "w", "r", "key")

    def __init__(self, name="", key=()):
        self.name = name
        self.key = key
        self.w = None
        self.r = []


class RegMap(dict):
    def __call__(self, *key):
        r = self.get(key)
        if r is None:
            r = Reg(str(key), key)
            self[key] = r
        return r


class Ctx:
    def __init__(self, nc):
        self.nc = nc
        self.sem = {}
        self.cnt = {}
        for e in ("pe", "act", "dve", "pool"):
            self.sem[e] = nc.alloc_semaphore("c_" + e)
            self.cnt[e] = 0
        self.dq = ("sp", "act", "pool")
        self.dslots = {}
        for q in self.dq:
            self.dslots[q] = []
            for i in range(N_DMA_SLOTS):
                k = "d_%s_%d" % (q, i)
                self.sem[k] = nc.alloc_semaphore(k)
                self.cnt[k] = 0
                self.dslots[q].append(k)
        self.dnext = {q: 0 for q in self.dq}
        self.known = {e: {} for e in ENGS}
        self.snap = {}


def _freeze(fn):
    if fn.__closure__ is None:
        return fn
    cells = []
    for c in fn.__closure__:
        try:
            cells.append(types.CellType(c.cell_contents))
        except ValueError:
            cells.append(c)
    return types.FunctionType(fn.__code__, fn.__globals__, fn.__name__, fn.__defaults__, tuple(cells))


class Phase:
    def __init__(self, ctx, name="ph"):
        self.ctx = ctx
        self.name = name
        self.q = {e: [] for e in ENGS}
        self.nops = 0
        self.excl = set()

    def _waits(self, e, reads, writes):
        ctx = self.ctx
        deps = {}

        def add(tok):
            if tok is None:
                return
            k, v = tok
            if deps.get(k, 0) < v:
                deps[k] = v
        for r in reads:
            add(r.w)
        for w in writes:
            if w.w is not None and w.w[0] != e:
                add(w.w)
            for t in w.r:
                if t[0] != e:
                    add(t)
        known = ctx.known[e]
        waits = []
        for k, v in deps.items():
            if k == e and e == "pe":
                continue
            if known.get(k, 0) >= v:
                continue
            waits.append((k, v))
            known[k] = v
            sn = ctx.snap.get((k, v))
            if sn is not None:
                for k2, v2 in sn.items():
                    if known.get(k2, 0) < v2:
                        known[k2] = v2
        return waits

    def op(self, e, fn, reads=(), writes=()):
        ctx = self.ctx
        ex = [r for r in reads if (len(r.key) > 1 and r.key[1] in PSUM_NAMES) or r in self.excl]
        if ex:
            writes = list(writes) + ex
        waits = self._waits(e, reads, writes)
        ctx.cnt[e] += 1
        tok = (e, ctx.cnt[e])
        ctx.snap[tok] = dict(ctx.known[e])
        for r in reads:
            r.r.append(tok)
        for w in writes:
            w.w = tok
            w.r = []
        self.q[e].append((waits, _freeze(fn), (e, 1)))
        self.nops += 1
        return tok

    def dma(self, q, out, in_, reads=(), writes=(), **kw):
        ctx = self.ctx
        slots = ctx.dslots[q]
        k = slots[ctx.dnext[q] % len(slots)]
        ctx.dnext[q] += 1
        waits = self._waits(q, reads, writes)
        known = ctx.known[q]
        if ctx.cnt[k] > 0 and known.get(k, 0) < ctx.cnt[k]:
            waits.append((k, ctx.cnt[k]))
            known[k] = ctx.cnt[k]
        ctx.cnt[k] += 16
        tok = (k, ctx.cnt[k])
        for r in reads:
            r.r.append(tok)
        for w in writes:
            w.w = tok
            w.r = []

        def fn(eng, out=out, in_=in_, kw=kw):
            return eng.dma_start(out=out, in_=in_, **kw)
        self.q[q].append((waits, fn, (k, 16)))
        self.nops += 1
        return tok

    def emit(self):
        ctx = self.ctx
        nc = ctx.nc
        fin = []
        for q in ctx.dq:
            for k in ctx.dslots[q]:
                if ctx.cnt[k] > 0 and ctx.known["sp"].get(k, 0) < ctx.cnt[k]:
                    fin.append((k, ctx.cnt[k]))
                    ctx.known["sp"][k] = ctx.cnt[k]
        qs = self.q
        sem = ctx.sem

        def body(e):
            def f(eng):
                for waits, fn, inc in qs[e]:
                    for k, v in waits:
                        eng.wait_ge(sem[k], v)
                    ins = fn(eng)
                    ins.then_inc(sem[inc[0]], inc[1])
                if e == "sp":
                    for k, v in fin:
                        eng.wait_ge(sem[k], v)
            return f

        with nc.Block() as block:
            block.tensor(body("pe"))
            block.scalar(body("act"))
            block.vector(body("dve"))
            block.gpsimd(body("pool"))
            block.sync(body("sp"))
        full = {}
        for e in ("pe", "act", "dve", "pool"):
            full[e] = ctx.cnt[e]
        for q in ctx.dq:
            for k in ctx.dslots[q]:
                full[k] = ctx.cnt[k]
        for e in ENGS:
            ctx.known[e] = dict(full)
        ctx.snap = {}


def clear_sems(ctx):
    nc = ctx.nc
    sems = list(ctx.sem.values())
    with nc.Block() as block:
        def f(eng):
            for s in sems:
                eng.sem_clear(s)
        block.gpsimd(f)


def _t5_bucket(rel):
    half = 16
    max_exact = 8
    n = np.abs(rel)
    large = max_exact + (np.log(np.maximum(n, 1) / max_exact) / math.log(1024 / max_exact)
                         * (half - max_exact)).astype(np.int64)
    large = np.minimum(large, half - 1)
    return ((rel > 0) * half + np.where(n < max_exact, n, large)).astype(np.int32)


C_ID = 0
C_MNEG = 128
C_MF = 384
C_MB = 640
C_TRIL = 896
C_TRIU = 1024
C_SELL = 1152
C_SELF = 1280
C_ID2 = 1408
NCONST = 1664


def make_consts():
    c = np.zeros((128, NCONST), np.float32)
    a = np.arange(128)[:, None]
    b = np.arange(128)[None, :]
    c[:, C_ID:C_ID + 128] = np.eye(128)
    col = np.arange(256)[None, :]
    rel = col - 64 - a
    c[:, C_MNEG:C_MNEG + 256] = np.where(np.abs(rel) <= 64, 0.0, -BIG)
    c[:, C_MF:C_MF + 128] = np.where(b >= a, BIG, 0.0)
    c[:, C_MF + 128:C_MF + 256] = np.where(b < a, -BIG, 0.0)
    c[:, C_MB:C_MB + 128] = np.where(b <= a, BIG, 0.0)
    c[:, C_MB + 128:C_MB + 256] = np.where(b > a, -BIG, 0.0)
    c[:, C_TRIL:C_TRIL + 128] = (a <= b)
    c[:, C_TRIU:C_TRIU + 128] = (a >= b)
    c[127, C_SELL:C_SELL + 128] = 1.0
    c[0, C_SELF:C_SELF + 128] = 1.0
    c[:, C_ID2:C_ID2 + 128] = np.eye(128)
    c[:, C_ID2 + 128:C_ID2 + 256] = np.eye(128)
    return c


def gather_bias(rel_bias):
    out = np.zeros((12, 128, 256), np.float32)
    a = np.arange(128)[:, None]
    col = np.arange(256)[None, :]
    rel = col - 64 - a
    inb = np.abs(rel) <= 64
    relc = np.clip(rel, -64, 64)
    for gi, (window, dil) in enumerate(GROUPS):
        bk = _t5_bucket(relc * dil)
        for hh in range(4):
            h = gi * 4 + hh
            out[h] = np.where(inb, rel_bias[bk, h], 0.0)
    return out


def build(debug=False, phases=None):
    nc = bass.Bass("TRN2", target_bir_lowering=False)

    def din(name, shape, dt=F32):
        return nc.dram_tensor(name, list(shape), dt, kind="ExternalInput").ap()

    def dscr(name, shape, dt):
        kind = "ExternalOutput" if (debug and (DBG.get("outs") is None or name in DBG["outs"])) else "Internal"
        return nc.dram_tensor(name, list(shape), dt, kind=kind).ap()

    def want(p):
        return phases is None or p in phases

    x_in = din("x", [S, D])
    mem_in = din("mem", [256, D])
    biasg = din("biasg", [12, 128, 256])
    consts_in = din("consts", [128, NCONST])
    gains_pc = din("gains_pc", [128, 6, 8])
    att_w_in = din("att_w_in", [D, 2560])
    att_w_out = din("att_w_out", [D, D])
    dn_w_in = din("dn_w_in", [D, 3352])
    dn_convT = din("dn_convT", [2304, 5])
    dn_a_log = din("dn_a_log", [12])
    dn_dt_bias = din("dn_dt_bias", [12])
    dn_out_norm = din("dn_out_norm", [128])
    dn_w_out = din("dn_w_out", [D, D])
    mem_w_kv = din("mem_w_kv", [2, D, 512])
    norm_mix_post = din("norm_mix_post", [2, D])
    norm_ffn_post = din("norm_ffn_post", [2, D])
    ffn_wgu = din("ffn_wgu", [2, D, 2 * DFF])
    ffn_wd = din("ffn_wd", [2, DFF, D])
    y_out = nc.dram_tensor("y", [S, D], F32, kind="ExternalOutput").ap()

    QTd = dscr("QTd", [6, 128, S], BF16)
    KTd = dscr("KTd", [6, 128, S], BF16)
    V0 = dscr("V0", [S, 768], BF16)
    OATT = dscr("OATT", [S, 768], F32)
    LSE = dscr("LSE", [S, 12], F32)
    X1 = dscr("X1", [S, D], F32)
    X2 = dscr("X2", [S, D], F32)
    X3 = dscr("X3", [S, D], F32)
    QKVT = dscr("QKVT", [2304, S], BF16)
    SZ = dscr("SZ", [S, 768], BF16)
    GD = dscr("GD", [S, 12], F32)
    BETA = dscr("BETA", [S, 12], F32)
    OF = dscr("OF", [S, 768], F32)
    OB = dscr("OB", [S, 768], F32)

    ctx = Ctx(nc)
    clear_sems(ctx)
    R = RegMap()

    with ExitStack() as gs:
        def gsb(name, shape, dt):
            return gs.enter_context(nc.sbuf_tensor(name, list(shape), dt))

        cst = gsb("cst", [128, NCONST], F32)
        identb = gsb("identb", [128, 128], BF16)
        onesb = gsb("onesb", [128, 128], BF16)
        ones32 = gsb("ones32", [128, 128], F32)
        gpc = gsb("gpc", [128, 6, 8], F32)
        qm_box = [None]
        kmT = gsb("kmT", [128, 2, 256], BF16)
        vm = gsb("vm", [128, 2, 256], BF16)
        ident32 = cst[:, C_ID:C_ID + 128]

        P = Phase(ctx, "setup")
        P.dma("sp", cst[:], consts_in, writes=[R("cst")])
        P.dma("sp", gpc[:], gains_pc, writes=[R("gpc")])
        P.op("dve", lambda e: e.tensor_copy(identb[:], cst[:, C_ID:C_ID + 128]), reads=[R("cst")], writes=[R("identb")])
        P.op("pool", lambda e: e.memset(onesb[:], 1.0), writes=[R("onesb")])
        P.op("pool", lambda e: e.memset(ones32[:], 1.0), writes=[R("ones32")])
        P.emit()

        def load_weight(P, tag, dst, src, KC, N, stg, gain=None, PW=512, engs=("dve", "pool")):
            srcv = src.rearrange("(c p) n -> p c n", p=128)
            i = 0
            for c0 in range(0, N, PW):
                w = min(PW, N - c0)
                b = i % len(stg)
                P.dma("sp", stg[b][:, :, 0:w], srcv[:, :, c0:c0 + w], writes=[R("stg", id(stg[b]))])
                eng = engs[i % len(engs)]
                if gain is not None:
                    P.op(eng, lambda e, b=b, c0=c0, w=w: e.tensor_tensor(
                        dst[:, :, c0:c0 + w], stg[b][:, :, 0:w],
                        gain.unsqueeze(2).broadcast_to([128, KC, w]), ALU.mult),
                        reads=[R("stg", id(stg[b])), R("gpc")], writes=[R(tag, "w")])
                else:
                    P.op(eng, lambda e, b=b, c0=c0, w=w: e.tensor_copy(dst[:, :, c0:c0 + w], stg[b][:, :, 0:w]),
                         reads=[R("stg", id(stg[b]))], writes=[R(tag, "w")])
                i += 1

        def rms_prep(P, tag, b, xt, sqj, ssq, rstd, xn, r_x):
            P.op("act", lambda e: e.activation(sqj, xt, AF.Square, accum_out=ssq),
                 reads=[r_x], writes=[R(tag, "sqj"), R(tag, "ssq", b)])
            P.op("act", lambda e: e.activation(rstd, ssq, AF.Sqrt, bias=EPS, scale=1.0 / D),
                 reads=[R(tag, "ssq", b)], writes=[R(tag, "rstd", b)])
            P.op("dve", lambda e: e.reciprocal(rstd, rstd), reads=[R(tag, "rstd", b)], writes=[R(tag, "rstd", b)])
            P.op("dve", lambda e: e.tensor_scalar(xn, xt, rstd, None, ALU.mult),
                 reads=[r_x, R(tag, "rstd", b)], writes=[R(tag, "xn", b)])

        def transpose8(P, tag, b, xn, pT, r_pT, dst, r_dst, evac="act"):
            for c in range(8):
                P.op("pe", lambda e, c=c: e.transpose(pT[:, c, :], xn[:, c * 128:(c + 1) * 128], identb[:]),
                     reads=[R(tag, "xn", b), R("identb")], writes=[r_pT])
            if evac == "act":
                P.op("act", lambda e: e.copy(dst, pT[:]), reads=[r_pT], writes=[r_dst])
            else:
                P.op("dve", lambda e: e.tensor_copy(dst, pT[:]), reads=[r_pT], writes=[r_dst])

        def post_norm_residual(P, tag, b, pY, r_pY, gpost, xres, r_xres, sqj, ssq2, rstd, tmp, xo, r_xo):
            rpy = r_pY if isinstance(r_pY, list) else [r_pY]
            for hf in range(2):
                P.op("act", lambda e, hf=hf: e.activation(sqj[:, hf * 512:(hf + 1) * 512], pY[:, hf * 512:(hf + 1) * 512],
                                                          AF.Square, accum_out=ssq2[:, hf:hf + 1]),
                     reads=rpy, writes=[R(tag, "psqj"), R(tag, "pssq", b, hf)])
            P.op("dve", lambda e: e.tensor_tensor(rstd, ssq2[:, 0:1], ssq2[:, 1:2], ALU.add),
                 reads=[R(tag, "pssq", b, 0), R(tag, "pssq", b, 1)], writes=[R(tag, "prstd", b)])
            P.op("act", lambda e: e.activation(rstd, rstd, AF.Sqrt, bias=EPS, scale=1.0 / D),
                 reads=[R(tag, "prstd", b)], writes=[R(tag, "prstd", b)])
            P.op("dve", lambda e: e.reciprocal(rstd, rstd), reads=[R(tag, "prstd", b)], writes=[R(tag, "prstd", b)])
            P.op("dve", lambda e: e.scalar_tensor_tensor(tmp, pY, rstd, gpost, ALU.mult, ALU.mult),
                 reads=rpy + [R(tag, "prstd", b), R(tag, "gpost")], writes=[R(tag, "ptmp")])
            P.op("pool", lambda e: e.tensor_tensor(xo, tmp, xres, ALU.add),
                 reads=[R(tag, "ptmp"), r_xres], writes=[r_xo])

        def mem_kv(P, es, li, pbank, stg, mt, sqj, ssq, rstd, xn):
            tag = "mkv%d" % li
            sb = lambda n, s, d: es.enter_context(nc.sbuf_tensor(tag + n, list(s), d))
            wkv = sb("w", [128, 8, 512], BF16)
            mT = sb("mT", [128, 8, 256], BF16)
            load_weight(P, tag, wkv, mem_w_kv[li], 8, 512, stg, gain=gpc[:, 4 + li, :], PW=256)
            for t in range(2):
                ptag = "A%d" % li
                P.dma("sp", mt[:], mem_in[t * 128:(t + 1) * 128, :], writes=[R(ptag, "xt", 0)])
                rms_prep(P, ptag, 0, mt[:], sqj[:], ssq[:], rstd[:], xn[:], R(ptag, "xt", 0))
                transpose8(P, ptag, 0, xn, pbank["T"], R(ptag, "pT", 0), mT[:, :, t * 128:(t + 1) * 128], R(tag, "mT"))
            pk = pbank["F"]
            for cb in range(2):
                for c in range(8):
                    P.op("pe", lambda e, cb=cb, c=c: e.matmul(pk[:, 0:256], wkv[:, c, cb * 128:(cb + 1) * 128], mT[:, c, :],
                                                              start=(c == 0), stop=(c == 7)),
                         reads=[R(tag, "w"), R(tag, "mT")], writes=[R("A%d" % li, "pF", 0)])
                P.op("act", lambda e, cb=cb: e.copy(kmT[:, cb, :], pk[:, 0:256]), reads=[R("A%d" % li, "pF", 0)], writes=[R("kmT")])
            for t in range(2):
                for c in range(8):
                    P.op("pe", lambda e, t=t, c=c: e.matmul(pk[:, 0:256], mT[:, c, t * 128:(t + 1) * 128], wkv[:, c, 256:512],
                                                            start=(c == 0), stop=(c == 7)),
                         reads=[R(tag, "w"), R(tag, "mT")], writes=[R("A%d" % li, "pF", 0)])
                P.op("act", lambda e, t=t: e.copy(vm[:, t, :], pk[:, 0:256]), reads=[R("A%d" % li, "pF", 0)], writes=[R("vm")])

        def phase_A(li):
            tag = "A%d" % li
            P = Phase(ctx, tag)
            with ExitStack() as es:
                sb = lambda n, s, d: es.enter_context(nc.sbuf_tensor(tag + n, list(s), d))
                ps = lambda n, s, d: es.enter_context(nc.psum_tensor(tag + n, list(s), d))
                NW = 2560 if li == 0 else 3352
                w_src = att_w_in if li == 0 else dn_w_in
                x_src = x_in if li == 0 else X2
                wb = sb("wb", [128, 8, NW], BF16)
                stg = [sb("stg%d" % i, [128, 8, 256], F32) for i in range(2)]
                xt = [sb("xt%d" % i, [128, D], F32) for i in range(2)]
                sqj = sb("sqj", [128, D], BF16)
                ssq = [sb("ssq%d" % i, [128, 1], F32) for i in range(2)]
                rstd = [sb("rstd%d" % i, [128, 1], F32) for i in range(2)]
                xn = [sb("xn%d" % i, [128, D], BF16) for i in range(2)]
                hT = [sb("hT%d" % i, [128, 8, 512], BF16) for i in range(2)]
                fo = [sb("fo%d" % i, [128, 512], BF16) for i in range(3)]
                pT = [ps("pT%d" % i, [128, 8, 128], BF16) for i in range(2)]
                pF = [ps("pF%d" % i, [128, 512], F32) for i in range(2)]
                pV = [ps("pV%d" % i, [128, 1024], F32) for i in range(1)]
                mem_kv(P, es, li, {"T": pT[0], "F": pF[0]}, stg, xt[0], sqj, ssq[0], rstd[0], xn[0])
                load_weight(P, tag, wb, w_src, 8, NW, stg, gain=gpc[:, li, :], PW=256)
                if li == 0:
                    vo = [sb("vo%d" % i, [128, 768], BF16) for i in range(2)]
                    feat_blocks = [("q", i) for i in range(6)] + [("k", i) for i in range(6)] + [("m", i) for i in range(2)]
                else:
                    vo = [sb("vo%d" % i, [128, 768], BF16) for i in range(2)]
                    feat_blocks = [("c", i) for i in range(18)] + [("m", i) for i in range(2)]
                    gi = [sb("gi%d" % i, [128, 24], F32) for i in range(2)]
                    gw = [sb("gw%d" % i, [128, 6, 12], F32) for i in range(2)]
                    go = [sb("go%d" % i, [128, 2, 12], F32) for i in range(2)]
                    alog_b = sb("alogb", [128, 12], F32)
                    dtb_b = sb("dtbb", [128, 12], F32)
                    P.dma("sp", alog_b[:], dn_a_log.partition_broadcast(128), writes=[R(tag, "alog")])
                    P.dma("sp", dtb_b[:], dn_dt_bias.partition_broadcast(128), writes=[R(tag, "dtb")])
                    P.op("act", lambda e: e.activation(alog_b[:], alog_b[:], AF.Exp), reads=[R(tag, "alog")], writes=[R(tag, "alog")])
                    P.op("dve", lambda e: e.tensor_scalar(alog_b[:], alog_b[:], -1.0, None, ALU.mult),
                         reads=[R(tag, "alog")], writes=[R(tag, "alog")])
                nfo = 0
                for st in range(S // 512):
                    hb = st % 2
                    for s4 in range(4):
                        t = st * 4 + s4
                        b = t % 2
                        P.dma("sp", xt[b][:], x_src[t * 128:(t + 1) * 128, :], writes=[R(tag, "xt", b)])
                        rms_prep(P, tag, b, xt[b][:], sqj[:], ssq[b][:], rstd[b][:], xn[b][:], R(tag, "xt", b))
                        transpose8(P, tag, b, xn[b], pT[b], R(tag, "pT", b),
                                   hT[hb][:, :, s4 * 128:(s4 + 1) * 128], R(tag, "hT", hb))
                    for kind, i in feat_blocks:
                        if li == 0:
                            col0 = {"q": 0, "k": 768, "m": 2304}[kind] + i * 128
                        else:
                            col0 = {"c": 0, "m": 3096}[kind] + i * 128
                        pb = nfo % 2
                        for c in range(8):
                            P.op("pe", lambda e, c=c, col0=col0, pb=pb: e.matmul(pF[pb][:], wb[:, c, col0:col0 + 128], hT[hb][:, c, :],
                                                                                  start=(c == 0), stop=(c == 7)),
                                 reads=[R(tag, "w"), R(tag, "hT", hb)], writes=[R(tag, "pF", pb)])
                        sc = 0.125 if kind in ("q", "m") else 1.0
                        if kind == "m":
                            P.op("act", lambda e, pb=pb, i=i, sc=sc: e.activation(qm_box[0][:, i, st * 512:(st + 1) * 512], pF[pb][:], AF.Copy, scale=sc),
                                 reads=[R(tag, "pF", pb)], writes=[R("QmT")])
                        elif kind in ("q", "k"):
                            dst_t = QTd if kind == "q" else KTd
                            dil = GROUPS[i // 2][1]
                            m0 = st * 512 // dil
                            nm = 512 // dil
                            fb = nfo % 3
                            dsts = fo[fb][:].rearrange("p (r m) -> p r m", r=dil)
                            src = pF[pb][:].rearrange("p (m r) -> p r m", r=dil)
                            P.op("act", lambda e, dsts=dsts, src=src, sc=sc: e.activation(dsts, src, AF.Copy, scale=sc),
                                 reads=[R(tag, "pF", pb)], writes=[R(tag, "fo", fb)])
                            P.dma("pool", dst_t[i].rearrange("p (r l) -> p r l", r=dil)[:, :, m0:m0 + nm], dsts,
                                  reads=[R(tag, "fo", fb)])
                        else:
                            fb = nfo % 3
                            P.op("act", lambda e, pb=pb, fb=fb: e.copy(fo[fb][:], pF[pb][:]),
                                 reads=[R(tag, "pF", pb)], writes=[R(tag, "fo", fb)])
                            P.dma("pool", QKVT[i * 128:(i + 1) * 128, st * 512:(st + 1) * 512], fo[fb][:], reads=[R(tag, "fo", fb)])
                        nfo += 1
                    vcol = 1536 if li == 0 else 2304
                    for s4 in range(4):
                        t = st * 4 + s4
                        b = t % 2
                        for (n0, nn) in ((0, 512), (512, 256)):
                            for c in range(8):
                                P.op("pe", lambda e, c=c, n0=n0, nn=nn, s4=s4: e.matmul(
                                    pV[0][:, n0:n0 + nn], hT[hb][:, c, s4 * 128:(s4 + 1) * 128], wb[:, c, vcol + n0:vcol + n0 + nn],
                                    start=(c == 0), stop=(c == 7)),
                                    reads=[R(tag, "w"), R(tag, "hT", hb)], writes=[R(tag, "pV", n0)])
                        if li == 0:
                            P.op("dve", lambda e, b=b: e.tensor_copy(vo[b][:], pV[0][:, 0:768]),
                                 reads=[R(tag, "pV", 0), R(tag, "pV", 512)], writes=[R(tag, "vo", b)])
                            P.dma("pool", V0[t * 128:(t + 1) * 128, :], vo[b][:], reads=[R(tag, "vo", b)])
                        else:
                            for c in range(8):
                                P.op("pe", lambda e, c=c, s4=s4: e.matmul(
                                    pV[0][:, 768:792], hT[hb][:, c, s4 * 128:(s4 + 1) * 128], wb[:, c, 3072:3096],
                                    start=(c == 0), stop=(c == 7)),
                                    reads=[R(tag, "w"), R(tag, "hT", hb)], writes=[R(tag, "pV", 512)])
                            P.op("act", lambda e, b=b: e.activation(vo[b][:], pV[0][:, 0:768], AF.Silu),
                                 reads=[R(tag, "pV", 0), R(tag, "pV", 512)], writes=[R(tag, "vo", b)])
                            P.dma("pool", SZ[t * 128:(t + 1) * 128, :], vo[b][:], reads=[R(tag, "vo", b)])
                            P.op("dve", lambda e, b=b: e.tensor_copy(gi[b][:], pV[0][:, 768:792]),
                                 reads=[R(tag, "pV", 512)], writes=[R(tag, "gi", b)])
                            giv = gi[b][:].rearrange("p (d a h) -> p d a h", d=2, a=2)
                            w0 = gw[b][:, 0, :].rearrange("p (d h) -> p d h", d=2)
                            w1 = gw[b][:, 1, :].rearrange("p (d h) -> p d h", d=2)
                            rg = [R(tag, "gw", b)]
                            P.op("dve", lambda e, w0=w0, giv=giv: e.tensor_tensor(
                                w0, giv[:, :, 0, :], dtb_b[:].rearrange("p (d h) -> p d h", d=2), ALU.add),
                                reads=[R(tag, "gi", b), R(tag, "dtb")], writes=rg)
                            P.op("dve", lambda e, w1=w1, giv=giv: e.tensor_copy(w1, giv[:, :, 1, :]),
                                 reads=[R(tag, "gi", b)], writes=rg)
                            P.op("act", lambda e, b=b: e.activation(gw[b][:, 2, :], gw[b][:, 0, :], AF.Abs),
                                 reads=rg, writes=rg)
                            P.op("act", lambda e, b=b: e.activation(gw[b][:, 2, :], gw[b][:, 2, :], AF.Exp, scale=-1.0),
                                 reads=rg, writes=rg)
                            P.op("act", lambda e, b=b: e.activation(gw[b][:, 2, :], gw[b][:, 2, :], AF.Ln, bias=1.0),
                                 reads=rg, writes=rg)
                            P.op("dve", lambda e, b=b: e.scalar_tensor_tensor(gw[b][:, 3, :], gw[b][:, 0, :], 0.0, gw[b][:, 2, :], ALU.max, ALU.add),
                                 reads=rg, writes=rg)
                            P.op("dve", lambda e, b=b: e.tensor_tensor(go[b][:, 0, :], gw[b][:, 3, :], alog_b[:], ALU.mult),
                                 reads=rg + [R(tag, "alog")], writes=[R(tag, "go", b)])
                            P.op("act", lambda e, b=b: e.activation(gw[b][:, 4, :], gw[b][:, 1, :], AF.Exp, scale=-1.0),
                                 reads=rg, writes=rg)
                            P.op("dve", lambda e, b=b: e.tensor_scalar(gw[b][:, 4, :], gw[b][:, 4, :], 1.0, None, ALU.add),
                                 reads=rg, writes=rg)
                            P.op("dve", lambda e, b=b: e.reciprocal(go[b][:, 1, :], gw[b][:, 4, :]),
                                 reads=rg, writes=[R(tag, "go", b)])
                            P.dma("pool", GD[t * 128:(t + 1) * 128, :], go[b][:, 0, :], reads=[R(tag, "go", b)])
                            P.dma("pool", BETA[t * 128:(t + 1) * 128, :], go[b][:, 1, :], reads=[R(tag, "go", b)])
                P.emit()
            return P.nops

        def phase_B0():
            tag = "B0"
            P = Phase(ctx, tag)
            with ExitStack() as es:
                sb = lambda n, s, d: es.enter_context(nc.sbuf_tensor(tag + n, list(s), d))
                ps = lambda n, s, d: es.enter_context(nc.psum_tensor(tag + n, list(s), d))
                bm = sb("bm", [128, 12, 256], F32)
                P.dma("sp", bm[:], biasg.rearrange("h q k -> q h k"), writes=[R(tag, "bm")])
                P.op("dve", lambda e: e.tensor_tensor(bm[:], bm[:], cst[:, C_MNEG:C_MNEG + 256].unsqueeze(1).broadcast_to([128, 12, 256]), ALU.add),
                     reads=[R(tag, "bm"), R("cst")], writes=[R(tag, "bm")])
                NB = 3
                vw = [sb("vw%d" % i, [128, 2, 256], BF16) for i in range(NB)]
                sl = [sb("sl%d" % i, [128, 256], F32) for i in range(NB)]
                pe_ = [sb("p%d" % i, [128, 256], BF16) for i in range(NB)]
                pTs = [sb("pTs%d" % i, [128, 2, 128], BF16) for i in range(NB)]
                st_ = [sb("st%d" % i, [128, 3, 4], F32) for i in range(NB)]
                on = [sb("on%d" % i, [128, 4, 64], F32) for i in range(NB)]
                pL = [ps("pL%d" % i, [128, 256], F32) for i in range(2)]
                pT = [ps("pT%d" % i, [128, 2, 128], BF16) for i in range(2)]
                pO = [ps("pO%d" % i, [128, 4, 64], F32) for i in range(2)]
                qs = [sb("qs%d" % i, [128, 2, S], BF16) for i in range(2)]
                ks = [sb("ks%d" % i, [128, 2, S], BF16) for i in range(2)]
                it = 0
                ih = 0
                for g, (window, dil) in enumerate(GROUPS):
                    L = S // dil
                    ntile = L // 128
                    gb = g % 2
                    QT = qs[gb]
                    KT = ks[gb]
                    for b2 in range(2):
                        P.dma("sp", QT[:, b2, :], QTd[2 * g + b2], writes=[R(tag, "QK", gb)])
                        P.dma("sp", KT[:, b2, :], KTd[2 * g + b2], writes=[R(tag, "QK", gb)])
                    for r in range(dil):
                        for qt in range(ntile):
                            b = it % NB
                            ob = it % 2
                            base = r * L + qt * 128
                            first = (qt == 0)
                            last = (qt == ntile - 1)
                            c_lo = 64 if first else 0
                            c_hi = 192 if last else 256
                            for c in range(2):
                                m0 = qt * 128 - 64 + 128 * c
                                p_lo = 64 if (first and c == 0) else 0
                                p_hi = 64 if (last and c == 1) else 128
                                row0 = (m0 + p_lo) * dil + r
                                nrow = p_hi - p_lo
                                srcv = V0[row0:row0 + (nrow - 1) * dil + 1:dil, g * 256:(g + 1) * 256] if dil > 1 else \
                                    V0[row0:row0 + nrow, g * 256:(g + 1) * 256]
                                P.dma("sp", vw[b][p_lo:p_hi, c, :], srcv, writes=[R(tag, "vw", b, c)])
                            for hh in range(4):
                                h = g * 4 + hh
                                blk = hh // 2
                                pl = (hh % 2) * 64
                                lb = ih % 2
                                P.op("pe", lambda e, lb=lb, blk=blk, pl=pl, base=base, c_lo=c_lo, c_hi=c_hi, QT=QT, KT=KT: e.matmul(
                                    pL[lb][:, c_lo:c_hi], QT[pl:pl + 64, blk, base:base + 128],
                                    KT[pl:pl + 64, blk, base - 64 + c_lo:base - 64 + c_hi], start=True, stop=True),
                                    reads=[R(tag, "QK", gb)], writes=[R(tag, "pL", lb)])
                                P.op("dve", lambda e, lb=lb, b=b, h=h, c_lo=c_lo, c_hi=c_hi: e.tensor_tensor(
                                    sl[b][:, c_lo:c_hi], pL[lb][:, c_lo:c_hi], bm[:, h, c_lo:c_hi], ALU.add),
                                    reads=[R(tag, "pL", lb), R(tag, "bm")], writes=[R(tag, "sl", b)])
                                P.op("dve", lambda e, b=b, hh=hh, c_lo=c_lo, c_hi=c_hi: e.tensor_reduce(
                                    st_[b][:, 0, hh:hh + 1], sl[b][:, c_lo:c_hi], AX.X, ALU.max, negate=True),
                                    reads=[R(tag, "sl", b)], writes=[R(tag, "nmx", b, hh)])
                                P.op("act", lambda e, b=b, hh=hh, c_lo=c_lo, c_hi=c_hi: e.activation(
                                    pe_[b][:, c_lo:c_hi], sl[b][:, c_lo:c_hi], AF.Exp, bias=st_[b][:, 0, hh:hh + 1],
                                    accum_out=st_[b][:, 1, hh:hh + 1]),
                                    reads=[R(tag, "sl", b), R(tag, "nmx", b, hh)], writes=[R(tag, "p", b), R(tag, "den", b, hh)])
                                for c in range(2):
                                    P.op("pe", lambda e, lb=lb, b=b, c=c: e.transpose(pT[lb][:, c, :], pe_[b][:, c * 128:(c + 1) * 128], identb[:]),
                                         reads=[R(tag, "p", b), R("identb")], writes=[R(tag, "pT", lb)])
                                P.op("dve" if hh % 2 else "act",
                                     (lambda e, lb=lb, b=b: e.tensor_copy(pTs[b][:], pT[lb][:])) if hh % 2 else
                                     (lambda e, lb=lb, b=b: e.copy(pTs[b][:], pT[lb][:])),
                                     reads=[R(tag, "pT", lb)], writes=[R(tag, "pTs", b)])
                                for c in range(2):
                                    p_lo = 64 if (first and c == 0) else 0
                                    p_hi = 64 if (last and c == 1) else 128
                                    P.op("pe", lambda e, ob=ob, b=b, c=c, hh=hh, p_lo=p_lo, p_hi=p_hi: e.matmul(
                                        pO[ob][:, hh, :], pTs[b][p_lo:p_hi, c, :], vw[b][p_lo:p_hi, c, hh * 64:(hh + 1) * 64],
                                        start=(c == 0), stop=(c == 1)),
                                        reads=[R(tag, "pTs", b), R(tag, "vw", b, c)], writes=[R(tag, "pO", ob, hh)])
                                ih += 1
                            rden = [R(tag, "den", b, hh) for hh in range(4)]
                            rnmx = [R(tag, "nmx", b, hh) for hh in range(4)]
                            P.op("act", lambda e, b=b: e.activation(st_[b][:, 2, :], st_[b][:, 1, :], AF.Ln),
                                 reads=rden, writes=[R(tag, "lse", b)])
                            P.op("dve", lambda e, b=b: e.tensor_tensor(st_[b][:, 2, :], st_[b][:, 2, :], st_[b][:, 0, :], ALU.subtract),
                                 reads=[R(tag, "lse", b)] + rnmx, writes=[R(tag, "lse", b)])
                            P.op("dve", lambda e, b=b: e.reciprocal(st_[b][:, 1, :], st_[b][:, 1, :]),
                                 reads=rden + [R(tag, "lse", b)], writes=rden)
                            P.op("dve", lambda e, b=b, ob=ob: e.tensor_tensor(
                                on[b][:], pO[ob][:], st_[b][:, 1, :].unsqueeze(2).broadcast_to([128, 4, 64]), ALU.mult),
                                reads=rden + [R(tag, "pO", ob, hh) for hh in range(4)], writes=[R(tag, "on", b)])
                            row0 = (qt * 128) * dil + r
                            if dil > 1:
                                dsto = OATT[row0:row0 + 127 * dil + 1:dil, g * 256:(g + 1) * 256]
                                dstl = LSE[row0:row0 + 127 * dil + 1:dil, g * 4:(g + 1) * 4]
                            else:
                                dsto = OATT[row0:row0 + 128, g * 256:(g + 1) * 256]
                                dstl = LSE[row0:row0 + 128, g * 4:(g + 1) * 4]
                            P.dma("pool", dsto, on[b][:].rearrange("p h d -> p (h d)"), reads=[R(tag, "on", b)])
                            P.dma("pool", dstl, st_[b][:, 2, :], reads=[R(tag, "lse", b)])
                            it += 1
                P.emit()
            return P.nops

        def phase_C(li):
            tag = "C%d" % li
            P = Phase(ctx, tag)
            x_src = x_in if li == 0 else X2
            x_dst = X1 if li == 0 else X3
            w_src = att_w_out if li == 0 else dn_w_out
            with ExitStack() as es:
                sb = lambda n, s, d: es.enter_context(nc.sbuf_tensor(tag + n, list(s), d))
                ps = lambda n, s, d: es.enter_context(nc.psum_tensor(tag + n, list(s), d))
                wo = sb("wo", [128, 8, D], BF16)
                stg = [sb("stg%d" % i, [128, 8, 256], F32) for i in range(2)]
                gpost = sb("gpost", [128, D], F32)
                P.dma("sp", gpost[:], norm_mix_post[li].partition_broadcast(128), writes=[R(tag, "gpost")])
                load_weight(P, tag, wo, w_src, 8, D, stg, PW=256)
                NB = 2
                ot = [sb("ot%d" % i, [128, 768], F32) for i in range(NB)]
                cat = [sb("cat%d" % i, [128, D], BF16) for i in range(NB)]
                catT = [sb("catT%d" % i, [128, 8, 128], BF16) for i in range(NB)]
                xr = [sb("xr%d" % i, [128, D], F32) for i in range(NB)]
                xo = [sb("xo%d" % i, [128, D], F32) for i in range(NB)]
                tmp = [sb("tmp%d" % i, [128, D], F32) for i in range(NB)]
                sqj = sb("sqj", [128, D], BF16)
                ssq2 = [sb("ssq2%d" % i, [128, 2], F32) for i in range(NB)]
                rstd = [sb("rstd%d" % i, [128, 1], F32) for i in range(NB)]
                pm = [sb("pm%d" % i, [128, 256], BF16) for i in range(NB)]
                pmT = [sb("pmT%d" % i, [128, 2, 128], BF16) for i in range(NB)]
                ms = [sb("ms%d" % i, [128, 2, 4], F32) for i in range(NB)]
                if li == 0:
                    ls = [sb("ls%d" % i, [128, 4, 12], F32) for i in range(NB)]
                else:
                    ot2 = [sb("ot2%d" % i, [128, 768], F32) for i in range(NB)]
                    szt = [sb("szt%d" % i, [128, 768], BF16) for i in range(NB)]
                    os_ = [sb("os%d" % i, [128, 2, 6], F32) for i in range(NB)]
                    onb = sb("onb", [128, 128], F32)
                    P.dma("sp", onb[:], dn_out_norm.partition_broadcast(128), writes=[R(tag, "onb")])
                pL = [ps("pL%d" % i, [128, 256], F32) for i in range(2)]
                pT = [ps("pT%d" % i, [128, 8, 128], BF16) for i in range(1)]
                pO = [ps("pO%d" % i, [128, 4, 64], F32) for i in range(1)]
                pY = [ps("pY%d" % i, [128, D], F32) for i in range(2)]
                ih = 0
                for t in range(NT):
                    b = t % NB
                    rows = slice(t * 128, (t + 1) * 128)
                    P.dma("sp", xr[b][:], x_src[rows, :], writes=[R(tag, "xr", b)])
                    if li == 0:
                        P.dma("sp", ot[b][:], OATT[rows, :], writes=[R(tag, "ot", b)])
                        P.dma("sp", ls[b][:, 0, :], LSE[rows, :], writes=[R(tag, "ls", b)])
                        rl = [R(tag, "ls", b)]
                        l3 = ls[b][:, 0, :].rearrange("p (g h) -> p g h", g=3)
                        e3 = ls[b][:, 1, :].rearrange("p (g h) -> p g h", g=3)
                        P.op("dve", lambda e, b=b, l3=l3: e.tensor_tensor(ls[b][:, 2, 0:4], l3[:, 0, :], l3[:, 1, :], ALU.max), reads=rl, writes=rl)
                        P.op("dve", lambda e, b=b, l3=l3: e.tensor_tensor(ls[b][:, 2, 0:4], ls[b][:, 2, 0:4], l3[:, 2, :], ALU.max), reads=rl, writes=rl)
                        P.op("dve", lambda e, b=b, l3=l3, e3=e3: e.tensor_tensor(
                            e3, l3, ls[b][:, 2, 0:4].unsqueeze(1).broadcast_to([128, 3, 4]), ALU.subtract), reads=rl, writes=rl)
                        P.op("act", lambda e, b=b: e.activation(ls[b][:, 1, :], ls[b][:, 1, :], AF.Exp), reads=rl, writes=rl)
                        P.op("dve", lambda e, b=b, e3=e3: e.tensor_tensor(ls[b][:, 2, 4:8], e3[:, 0, :], e3[:, 1, :], ALU.add), reads=rl, writes=rl)
                        P.op("dve", lambda e, b=b, e3=e3: e.tensor_tensor(ls[b][:, 2, 4:8], ls[b][:, 2, 4:8], e3[:, 2, :], ALU.add), reads=rl, writes=rl)
                        P.op("dve", lambda e, b=b: e.reciprocal(ls[b][:, 2, 4:8], ls[b][:, 2, 4:8]), reads=rl, writes=rl)
                        P.op("dve", lambda e, b=b, e3=e3: e.tensor_tensor(
                            e3, e3, ls[b][:, 2, 4:8].unsqueeze(1).broadcast_to([128, 3, 4]), ALU.mult), reads=rl, writes=rl)
                        P.op("pool", lambda e, b=b: e.tensor_tensor(
                            cat[b][:, 0:768].rearrange("p (h d) -> p h d", h=12), ot[b][:].rearrange("p (h d) -> p h d", h=12),
                            ls[b][:, 1, :].unsqueeze(2).broadcast_to([128, 12, 64]), ALU.mult),
                            reads=rl + [R(tag, "ot", b)], writes=[R(tag, "cat", b)])
                    else:
                        P.dma("sp", ot[b][:], OF[rows, :], writes=[R(tag, "ot", b)])
                        P.dma("sp", ot2[b][:], OB[rows, :], writes=[R(tag, "ot2", b)])
                        P.dma("sp", szt[b][:], SZ[rows, :], writes=[R(tag, "szt", b)])
                        P.op("pool", lambda e, b=b: e.tensor_tensor(ot[b][:], ot[b][:], ot2[b][:], ALU.add),
                             reads=[R(tag, "ot", b), R(tag, "ot2", b)], writes=[R(tag, "ot", b)])
                        for h in range(6):
                            P.op("act", lambda e, b=b, h=h: e.activation(ot2[b][:, h * 128:(h + 1) * 128], ot[b][:, h * 128:(h + 1) * 128],
                                                                         AF.Square, accum_out=os_[b][:, 0, h:h + 1]),
                                 reads=[R(tag, "ot", b)], writes=[R(tag, "ot2", b), R(tag, "os", b)])
                        P.op("act", lambda e, b=b: e.activation(os_[b][:, 1, :], os_[b][:, 0, :], AF.Sqrt, bias=EPS, scale=1.0 / 128),
                             reads=[R(tag, "os", b)], writes=[R(tag, "os", b)])
                        P.op("dve", lambda e, b=b: e.reciprocal(os_[b][:, 1, :], os_[b][:, 1, :]), reads=[R(tag, "os", b)], writes=[R(tag, "os", b)])
                        o3 = ot[b][:].rearrange("p (h d) -> p h d", h=6)
                        P.op("dve", lambda e, b=b, o3=o3: e.tensor_tensor(o3, o3, os_[b][:, 1, :].unsqueeze(2).broadcast_to([128, 6, 128]), ALU.mult),
                             reads=[R(tag, "ot", b), R(tag, "os", b)], writes=[R(tag, "ot", b)])
                        P.op("pool", lambda e, b=b, o3=o3: e.tensor_tensor(o3, o3, onb[:].unsqueeze(1).broadcast_to([128, 6, 128]), ALU.mult),
                             reads=[R(tag, "ot", b), R(tag, "onb")], writes=[R(tag, "ot", b)])
                        P.op("pool", lambda e, b=b: e.tensor_tensor(cat[b][:, 0:768], ot[b][:], szt[b][:], ALU.mult),
                             reads=[R(tag, "ot", b), R(tag, "szt", b)], writes=[R(tag, "cat", b)])
                    for hd in range(4):
                        blk = hd // 2
                        pl = (hd % 2) * 64
                        lb = ih % 2
                        P.op("pe", lambda e, lb=lb, blk=blk, pl=pl, t=t: e.matmul(
                            pL[lb][:], qm_box[0][pl:pl + 64, blk, t * 128:(t + 1) * 128], kmT[pl:pl + 64, blk, :], start=True, stop=True),
                            reads=[R("QmT"), R("kmT")], writes=[R(tag, "pL", lb)])
                        P.op("dve", lambda e, lb=lb, b=b, hd=hd: e.tensor_reduce(ms[b][:, 0, hd:hd + 1], pL[lb][:], AX.X, ALU.max, negate=True),
                             reads=[R(tag, "pL", lb)], writes=[R(tag, "mnmx", b, hd)])
                        P.op("act", lambda e, lb=lb, b=b, hd=hd: e.activation(pm[b][:], pL[lb][:], AF.Exp, bias=ms[b][:, 0, hd:hd + 1],
                                                                             accum_out=ms[b][:, 1, hd:hd + 1]),
                             reads=[R(tag, "pL", lb), R(tag, "mnmx", b, hd)], writes=[R(tag, "pm", b), R(tag, "mden", b, hd)])
                        for c in range(2):
                            P.op("pe", lambda e, b=b, c=c: e.transpose(pT[0][:, c, :], pm[b][:, c * 128:(c + 1) * 128], identb[:]),
                                 reads=[R(tag, "pm", b), R("identb")], writes=[R(tag, "pT")])
                        P.op("dve", lambda e, b=b: e.tensor_copy(pmT[b][:], pT[0][:, 0:2, :]), reads=[R(tag, "pT")], writes=[R(tag, "pmT", b)])
                        for c in range(2):
                            P.op("pe", lambda e, b=b, c=c, hd=hd: e.matmul(pO[0][:, hd, :], pmT[b][:, c, :], vm[:, c, hd * 64:(hd + 1) * 64],
                                                                           start=(c == 0), stop=(c == 1)),
                                 reads=[R(tag, "pmT", b), R("vm")], writes=[R(tag, "pO", hd)])
                        ih += 1
                    rden = [R(tag, "mden", b, hd) for hd in range(4)]
                    P.op("dve", lambda e, b=b: e.reciprocal(ms[b][:, 1, :], ms[b][:, 1, :]), reads=rden, writes=rden)
                    P.op("dve", lambda e, b=b: e.tensor_tensor(
                        cat[b][:, 768:1024].rearrange("p (h d) -> p h d", h=4), pO[0][:],
                        ms[b][:, 1, :].unsqueeze(2).broadcast_to([128, 4, 64]), ALU.mult),
                        reads=rden + [R(tag, "pO", hd) for hd in range(4)], writes=[R(tag, "cat", b)])
                    for c in range(8):
                        P.op("pe", lambda e, b=b, c=c: e.transpose(pT[0][:, c, :], cat[b][:, c * 128:(c + 1) * 128], identb[:]),
                             reads=[R(tag, "cat", b), R("identb")], writes=[R(tag, "pT")])
                    P.op("act", lambda e, b=b: e.copy(catT[b][:], pT[0][:]), reads=[R(tag, "pT")], writes=[R(tag, "catT", b)])
                    yb = t % 2
                    for hf in range(2):
                        for c in range(8):
                            P.op("pe", lambda e, b=b, c=c, hf=hf, yb=yb: e.matmul(
                                pY[yb][:, hf * 512:(hf + 1) * 512], catT[b][:, c, :], wo[:, c, hf * 512:(hf + 1) * 512],
                                start=(c == 0), stop=(c == 7)),
                                reads=[R(tag, "catT", b), R(tag, "w")], writes=[R(tag, "pY", yb)])
                    post_norm_residual(P, tag, b, pY[yb][:], R(tag, "pY", yb), gpost[:], xr[b][:], R(tag, "xr", b),
                                       sqj, ssq2[b], rstd[b][:], tmp[b][:], xo[b][:], R(tag, "xo", b))
                    P.dma("pool", x_dst[rows, :], xo[b][:], reads=[R(tag, "xo", b)])
                P.emit()
            return P.nops

        def phase_D(li):
            tag = "D%d" % li
            P = Phase(ctx, tag)
            x_src = X1 if li == 0 else X3
            x_dst = X2 if li == 0 else y_out
            TS = 256
            with ExitStack() as es:
                sb = lambda n, s, d: es.enter_context(nc.sbuf_tensor(tag + n, list(s), d))
                ps = lambda n, s, d: es.enter_context(nc.psum_tensor(tag + n, list(s), d))
                wgu = sb("wgu", [128, 8, 2 * DFF], BF16)
                wd = sb("wd", [128, 22, D], BF16)
                stg = [sb("stg%d" % i, [128, 8, 256], F32) for i in range(1)]
                gpost = sb("gpost", [128, D], F32)
                P.dma("sp", gpost[:], norm_ffn_post[li].partition_broadcast(128), writes=[R(tag, "gpost")])
                load_weight(P, tag + "gu", wgu, ffn_wgu[li], 8, 2 * DFF, stg, gain=gpc[:, 2 + li, :], PW=256)
                wdv = ffn_wd[li].rearrange("(c p) n -> p c n", p=128)
                stv = stg[0][:].rearrange("p c n -> p (c n)").rearrange("p (c n) -> p c n", c=2)
                for i in range(11):
                    P.dma("sp", stv, wdv[:, 2 * i:2 * i + 2, :], writes=[R("stg", id(stg[0]))])
                    P.op("pool" if i % 2 else "dve", lambda e, i=i: e.tensor_copy(wd[:, 2 * i:2 * i + 2, :], stv),
                         reads=[R("stg", id(stg[0]))], writes=[R(tag, "wd")])
                xt = [sb("xt%d" % i, [128, D], F32) for i in range(2)]
                tmp = sb("tmp", [128, D], F32)
                sqj = sb("sqj", [128, D], BF16)
                ssq = [sb("ssq%d" % i, [128, 1], F32) for i in range(2)]
                rstd = [sb("rstd%d" % i, [128, 1], F32) for i in range(2)]
                ssq2 = [sb("ssq2%d" % i, [128, 2], F32) for i in range(2)]
                prstd = [sb("prstd%d" % i, [128, 1], F32) for i in range(2)]
                xn = [sb("xn%d" % i, [128, D], BF16) for i in range(2)]
                hT = [sb("hT%d" % i, [128, 8, TS], BF16) for i in range(2)]
                actT = sb("actT", [128, 22, TS], BF16)
                sg = [sb("sg%d" % i, [128, TS], BF16) for i in range(2)]
                pT = [ps("pT%d" % i, [128, 8, 128], BF16) for i in range(1)]
                pG = [ps("pG%d" % i, [128, 512], F32) for i in range(2)]
                pU = [ps("pU%d" % i, [128, 512], F32) for i in range(2)]
                pY = [ps("pY%d" % i, [128, D], F32) for i in range(1)]
                nsub = TS // 128
                for st in range(S // TS):
                    hb = st % 2
                    for s4 in range(nsub):
                        t = st * nsub + s4
                        b = t % 2
                        P.dma("sp", xt[b][:], x_src[t * 128:(t + 1) * 128, :], writes=[R(tag, "xt", b)])
                        rms_prep(P, tag, b, xt[b][:], sqj[:], ssq[b][:], rstd[b][:], xn[b][:], R(tag, "xt", b))
                        transpose8(P, tag, b, xn[b], pT[0], R(tag, "pT"), hT[hb][:, :, s4 * 128:(s4 + 1) * 128], R(tag, "hT", hb))
                    for j in range(22):
                        pb = j % 2
                        for c in range(8):
                            P.op("pe", lambda e, c=c, j=j, pb=pb, hb=hb: e.matmul(pG[pb][:, 0:TS], wgu[:, c, j * 128:(j + 1) * 128], hT[hb][:, c, :],
                                                                                  start=(c == 0), stop=(c == 7)),
                                 reads=[R(tag + "gu", "w"), R(tag, "hT", hb)], writes=[R(tag, "pG", pb)])
                        for c in range(8):
                            P.op("pe", lambda e, c=c, j=j, pb=pb, hb=hb: e.matmul(pU[pb][:, 0:TS], wgu[:, c, DFF + j * 128:DFF + (j + 1) * 128], hT[hb][:, c, :],
                                                                                  start=(c == 0), stop=(c == 7)),
                                 reads=[R(tag + "gu", "w"), R(tag, "hT", hb)], writes=[R(tag, "pU", pb)])
                        P.op("act", lambda e, pb=pb: e.activation(sg[pb][:], pG[pb][:, 0:TS], AF.Silu),
                             reads=[R(tag, "pG", pb)], writes=[R(tag, "sg", pb)])
                        P.op("dve", lambda e, pb=pb, j=j: e.tensor_tensor(actT[:, j, :], sg[pb][:], pU[pb][:, 0:TS], ALU.mult),
                             reads=[R(tag, "sg", pb), R(tag, "pU", pb)], writes=[R(tag, "actT", j)])
                    for s4 in range(nsub):
                        t = st * nsub + s4
                        b = t % 2
                        P.dma("sp", xt[b][:], x_src[t * 128:(t + 1) * 128, :], writes=[R(tag, "xt", b)])
                        for hf in range(2):
                            for j in range(22):
                                P.op("pe", lambda e, j=j, hf=hf, s4=s4: e.matmul(
                                    pY[0][:, hf * 512:(hf + 1) * 512], actT[:, j, s4 * 128:(s4 + 1) * 128], wd[:, j, hf * 512:(hf + 1) * 512],
                                    start=(j == 0), stop=(j == 21)),
                                    reads=[R(tag, "actT", j), R(tag, "wd")], writes=[R(tag, "pY", hf)])
                        post_norm_residual(P, tag, b, pY[0][:], [R(tag, "pY", 0), R(tag, "pY", 1)], gpost[:], xt[b][:], R(tag, "xt", b),
                                           sqj, ssq2[b], prstd[b][:], tmp[:], xt[b][:], R(tag, "xt", b))
                        P.dma("pool", x_dst[t * 128:(t + 1) * 128, :], xt[b][:], reads=[R(tag, "xt", b)])
                P.emit()
            return P.nops

        def phase_B1():
            tag = "B1"
            P = Phase(ctx, tag)
            NCH = 32
            NSL = 3
            with ExitStack() as es:
                sb = lambda n, s, d: es.enter_context(nc.sbuf_tensor(tag + n, list(s), d))
                ps = lambda n, s, d: es.enter_context(nc.psum_tensor(tag + n, list(s), d))
                bk = [ps("bk%d" % i, [128, 512], F32) for i in range(8)]
                rbk = lambda i, j: R(tag, "bk", i)
                for i in range(8):
                    P.excl.add(rbk(i, 0))
                Gall = sb("Gall", [128, NCH, 12], F32)
                Ball = sb("Ball", [128, NCH, 12], F32)
                GC = sb("GC", [128, NCH, 12], F32)
                NGC = sb("NGC", [128, NCH, 12], F32)
                EG = sb("EG", [128, NCH, 12], F32)
                NBEG = sb("NBEG", [128, NCH, 12], F32)
                NBt = sb("NBt", [128, NCH, 12], F32)
                GLb = sb("GLb", [128, NCH, 12], F32)
                EGL = sb("EGL", [128, NCH, 12], F32)
                EKD = sb("EKD", [128, NCH, 12], F32)
                cw = sb("cw", [128, 18, 5], F32)
                for n in range(NCH):
                    P.dma("sp", Gall[:, n, :], GD[n * 128:(n + 1) * 128, :], writes=[R(tag, "Gall")])
                    P.dma("sp", Ball[:, n, :], BETA[n * 128:(n + 1) * 128, :], writes=[R(tag, "Ball")])
                for bb in range(18):
                    P.dma("sp", cw[:, bb, :], dn_convT[bb * 128:(bb + 1) * 128, :], writes=[R(tag, "cw")])
                v3 = lambda ap: ap.rearrange("p (n c) -> p n c", c=12)
                rg = [R(tag, "gates")]
                allbk0 = [rbk(0, j) for j in range(4)]
                allbk1 = [rbk(1, j) for j in range(4)]
                P.op("pe", lambda e: e.matmul(bk[0][:, 0:384], cst[:, C_TRIL:C_TRIL + 128], Gall[:].rearrange("p n c -> p (n c)"), start=True, stop=True),
                     reads=[R("cst"), R(tag, "Gall")], writes=allbk0)
                P.op("pe", lambda e: e.matmul(bk[1][:, 0:384], cst[:, C_TRIU:C_TRIU + 128], Gall[:].rearrange("p n c -> p (n c)"), start=True, stop=True),
                     reads=[R("cst"), R(tag, "Gall")], writes=allbk1)
                P.op("act", lambda e: e.copy(GC[:, :, 0:6], v3(bk[0][:, 0:384])[:, :, 0:6]), reads=allbk0, writes=rg)
                P.op("act", lambda e: e.copy(GC[:, :, 6:12], v3(bk[1][:, 0:384])[:, :, 6:12]), reads=allbk1, writes=rg)
                P.op("pe", lambda e: e.matmul(bk[0][:, 0:384], cst[:, C_SELL:C_SELL + 128], GC[:].rearrange("p n c -> p (n c)"), start=True, stop=True),
                     reads=[R("cst")] + rg, writes=allbk0)
                P.op("pe", lambda e: e.matmul(bk[1][:, 0:384], cst[:, C_SELF:C_SELF + 128], GC[:].rearrange("p n c -> p (n c)"), start=True, stop=True),
                     reads=[R("cst")] + rg, writes=allbk1)
                P.op("act", lambda e: e.copy(GLb[:, :, 0:6], v3(bk[0][:, 0:384])[:, :, 0:6]), reads=allbk0, writes=rg)
                P.op("act", lambda e: e.copy(GLb[:, :, 6:12], v3(bk[1][:, 0:384])[:, :, 6:12]), reads=allbk1, writes=rg)
                P.op("dve", lambda e: e.tensor_scalar(NGC[:], GC[:], -1.0, None, ALU.mult), reads=rg, writes=rg)
                P.op("act", lambda e: e.activation(EG[:], GC[:], AF.Exp), reads=rg, writes=rg)
                P.op("dve", lambda e: e.scalar_tensor_tensor(NBEG[:], Ball[:], -1.0, EG[:], ALU.mult, ALU.mult), reads=rg + [R(tag, "Ball")], writes=rg)
                P.op("dve", lambda e: e.tensor_scalar(NBt[:], Ball[:], -1.0, None, ALU.mult), reads=[R(tag, "Ball")], writes=rg)
                P.op("act", lambda e: e.activation(EGL[:], GLb[:], AF.Exp), reads=rg, writes=rg)
                P.op("dve", lambda e: e.tensor_tensor(EKD[:], GLb[:], GC[:], ALU.subtract), reads=rg, writes=rg)
                P.op("act", lambda e: e.activation(EKD[:], EKD[:], AF.Exp), reads=rg, writes=rg)

                raw = [sb("raw%d" % i, [128, S + 4], BF16) for i in range(2)]
                acc = [sb("acc%d" % i, [128, 1024], F32) for i in range(2)]
                slu = [sb("slu%d" % i, [128, 1024], F32) for i in range(2)]
                sq = [sb("sq%d" % i, [128, 1024], BF16) for i in range(2)]
                rinv = [sb("rinv%d" % i, [128, 512], F32) for i in range(2)]
                kqT = sb("kqT", [128, NCH, 2, 128], BF16)
                vT = sb("vT", [128, S], BF16)
                ktok = sb("ktok", [128, NCH, 128], BF16)
                vtok = sb("vtok", [128, NCH, 128], BF16)
                for i in range(2):
                    P.op("pool", lambda e, i=i: e.memset(raw[i][:, 0:2], 0.0), writes=[R(tag, "rawpad", i)])
                    P.op("pool", lambda e, i=i: e.memset(raw[i][:, S + 2:S + 4], 0.0), writes=[R(tag, "rawpad", i)])
                dg2 = [[sb("dg2_%d_%d" % (d, i), [128, 256], F32) for i in range(NSL)] for d in range(2)]
                dS_ = [[sb("dS_%d_%d" % (d, i), [128, 128], F32) for i in range(NSL)] for d in range(2)]
                dT_ = [[sb("dT_%d_%d" % (d, i), [128, 128], F32) for i in range(NSL)] for d in range(2)]
                Xb = [[[sb("X_%d_%d_%d" % (d, i, k), [128, 128], F32) for k in range(2)] for i in range(NSL)] for d in range(2)]
                Yb = [[[sb("Y_%d_%d_%d" % (d, i, k), [128, 128], F32) for k in range(2)] for i in range(NSL)] for d in range(2)]
                Qb = [[[sb("Q_%d_%d_%d" % (d, i, k), [128, 128], F32) for k in range(2)] for i in range(NSL)] for d in range(2)]
                inT = [[sb("inT_%d_%d" % (d, i), [128, 128], BF16) for i in range(NSL)] for d in range(2)]
                TTb = [[sb("TTb_%d_%d" % (d, i), [128, 128], BF16) for i in range(NSL)] for d in range(2)]
                BV = [[sb("BV_%d_%d" % (d, i), [128, 128], BF16) for i in range(NSL)] for d in range(2)]
                kdec = [[sb("kdec_%d_%d" % (d, i), [128, 128], BF16) for i in range(NSL)] for d in range(2)]
                S32 = [sb("S32_%d" % d, [128, 128], F32) for d in range(2)]
                Sbf = [[sb("Sbf_%d_%d" % (d, k), [128, 128], BF16) for k in range(2)] for d in range(2)]
                Rt = [[sb("Rt_%d_%d" % (d, k), [128, 128], BF16) for k in range(2)] for d in range(2)]
                vnb = [[sb("vnb_%d_%d" % (d, k), [128, 128], BF16) for k in range(2)] for d in range(2)]
                oq = [[sb("oq_%d_%d" % (d, k), [128, 128], F32) for k in range(2)] for d in range(2)]
                oo = [[sb("oo_%d_%d" % (d, k), [128, 128], F32) for k in range(2)] for d in range(2)]
                dblc = [0, 0]

                def dbl_region(d):
                    j = dblc[d] % 4
                    dblc[d] += 1
                    return bk[3 + d][:, j * 128:(j + 1) * 128], rbk(3 + d, j)

                def head_prep(h):
                    nraw = [0]
                    for kind in range(3):
                        blk = kind * 6 + h
                        rb = (h * 3 + kind) % 2
                        P.dma("sp", raw[rb][:, 2:S + 2], QKVT[blk * 128:(blk + 1) * 128, :], writes=[R(tag, "raw", rb)])
                        rr = [R(tag, "raw", rb), R(tag, "rawpad", rb), R(tag, "cw")]
                        for pc in range(4):
                            t0 = pc * 1024
                            ab = pc % 2
                            P.op("dve", lambda e, rb=rb, ab=ab, t0=t0, blk=blk: e.tensor_scalar(
                                acc[ab][:], raw[rb][:, t0:t0 + 1024], cw[:, blk, 0:1], None, ALU.mult),
                                reads=rr, writes=[R(tag, "acc", ab)])
                            for j in range(1, 5):
                                P.op("dve", lambda e, rb=rb, ab=ab, t0=t0, blk=blk, j=j: e.scalar_tensor_tensor(
                                    acc[ab][:], raw[rb][:, t0 + j:t0 + j + 1024], cw[:, blk, j:j + 1], acc[ab][:], ALU.mult, ALU.add),
                                    reads=rr + [R(tag, "acc", ab)], writes=[R(tag, "acc", ab)])
                            if kind == 2:
                                P.op("act", lambda e, ab=ab, t0=t0: e.activation(vT[:, t0:t0 + 1024], acc[ab][:], AF.Silu),
                                     reads=[R(tag, "acc", ab)], writes=[R(tag, "vT")])
                                continue
                            P.op("act", lambda e, ab=ab: e.activation(slu[ab][:], acc[ab][:], AF.Silu),
                                 reads=[R(tag, "acc", ab)], writes=[R(tag, "slu", ab)])
                            P.op("pool", lambda e, ab=ab: e.tensor_tensor(sq[ab][:], slu[ab][:], slu[ab][:], ALU.mult),
                                 reads=[R(tag, "slu", ab)], writes=[R(tag, "sq", ab)])
                            for hf in range(2):
                                wr = [rbk(3 + hf, j) for j in range(4)]
                                P.op("pe", lambda e, ab=ab, hf=hf: e.matmul(bk[3 + hf][:], onesb[:], sq[ab][:, hf * 512:(hf + 1) * 512], start=True, stop=True),
                                     reads=[R(tag, "sq", ab), R("onesb")], writes=wr)
                                P.op("act", lambda e, hf=hf: e.activation(rinv[hf][:], bk[3 + hf][:], AF.Sqrt, bias=EPS, scale=1.0),
                                     reads=wr, writes=[R(tag, "rinv", hf)])
                                P.op("dve", lambda e, hf=hf: e.reciprocal(rinv[hf][:], rinv[hf][:]), reads=[R(tag, "rinv", hf)], writes=[R(tag, "rinv", hf)])
                                n0 = pc * 8 + hf * 4
                                kidx = 1 if kind == 0 else 0
                                scl = (128.0 ** -0.5) if kind == 0 else 1.0
                                P.op("dve", lambda e, ab=ab, hf=hf, n0=n0, kidx=kidx, scl=scl: e.scalar_tensor_tensor(
                                    kqT[:, n0:n0 + 4, kidx, :], slu[ab][:, hf * 512:(hf + 1) * 512].rearrange("p (n t) -> p n t", n=4), scl,
                                    rinv[hf][:].rearrange("p (n t) -> p n t", n=4), ALU.mult, ALU.mult),
                                    reads=[R(tag, "slu", ab), R(tag, "rinv", hf)], writes=[R(tag, "kqT")])
                    for which in range(2):
                        for n0 in range(0, NCH, 4):
                            bi = 5 + (n0 // 4) % 2
                            pv = bk[bi][:].bitcast(BF16)[:, 0:512].rearrange("p (n t) -> p n t", n=4)
                            wr = [rbk(bi, j) for j in range(4)]
                            for k4 in range(4):
                                n = n0 + k4
                                src = kqT[:, n, 0, :] if which == 0 else vT[:, n * 128:(n + 1) * 128]
                                P.op("pe", lambda e, pv=pv, k4=k4, src=src: e.transpose(pv[:, k4, :], src, identb[:]),
                                     reads=[R(tag, "kqT"), R(tag, "vT"), R("identb")], writes=wr)
                            dst = (ktok if which == 0 else vtok)[:, n0:n0 + 4, :]
                            P.op("act" if (n0 // 4) % 2 else "dve",
                                 (lambda e, dst=dst, pv=pv: e.copy(dst, pv)) if (n0 // 4) % 2 else (lambda e, dst=dst, pv=pv: e.tensor_copy(dst, pv)),
                                 reads=wr, writes=[R(tag, "tok", which)])

                def prep(h, d, n, sl):
                    col = d * 6 + h
                    gc = GC[:, n, col:col + 1]
                    ngc = NGC[:, n, col:col + 1]
                    rs = lambda nm: R(tag, nm, d, sl)
                    P.op("pool", lambda e: e.tensor_scalar(dg2[d][sl][:], cst[:, C_ID2:C_ID2 + 256], gc, 1.0, ALU.mult, ALU.mult),
                         reads=rg + [R("cst")], writes=[rs("dg2")])
                    if DBG.get("cut", 99) < 2:
                        return
                    bmr = bk[1 + d][:, (sl % 2) * 256:(sl % 2) * 256 + 256]
                    rbm = [rbk(1 + d, (sl % 2) * 2), rbk(1 + d, (sl % 2) * 2 + 1)]
                    mc = C_MF if d == 0 else C_MB
                    P.op("pe", lambda e: e.matmul(bmr, ones32[:], dg2[d][sl][:], start=True, stop=False),
                         reads=[rs("dg2"), R("ones32")], writes=rbm)
                    P.op("pe", lambda e: e.matmul(bmr, cst[:, C_ID:C_ID + 128], cst[:, mc:mc + 256], start=False, stop=True),
                         reads=[R("cst")], writes=rbm)
                    if DBG.get("cut", 99) < 3:
                        return
                    P.op("act", lambda e: e.activation(dS_[d][sl][:], bmr[:, 0:128], AF.Exp, bias=gc, scale=-1.0),
                         reads=rbm + rg, writes=[rs("dS")])
                    P.op("act", lambda e: e.activation(dT_[d][sl][:], bmr[:, 128:256], AF.Exp, bias=ngc, scale=1.0),
                         reads=rbm + rg, writes=[rs("dT")])
                    if DBG.get("cut", 99) < 4:
                        return
                    kk = bk[0][:, (sl % 2) * 256:(sl % 2) * 256 + 256]
                    rkk = [rbk(0, (sl % 2) * 2), rbk(0, (sl % 2) * 2 + 1)]
                    P.op("pe", lambda e: e.matmul(kk, kqT[:, n, 0, :], kqT[:, n, :, :].rearrange("p a t -> p (a t)"), start=True, stop=True),
                         reads=[R(tag, "kqT")], writes=rkk)
                    X = Xb[d][sl]
                    Y = Yb[d][sl]
                    Q = Qb[d][sl]
                    P.op("dve", lambda e: e.scalar_tensor_tensor(X[0][:], kk[:, 0:128], NBt[:, n, col:col + 1], dS_[d][sl][:], ALU.mult, ALU.mult),
                         reads=rkk + rg + [rs("dS")], writes=[rs("X0")])
                    P.op("dve", lambda e: e.tensor_tensor(inT[d][sl][:], kk[:, 128:256], dT_[d][sl][:], ALU.mult),
                         reads=rkk + [rs("dT")], writes=[rs("inT")])
                    if DBG.get("cut", 99) < 5:
                        return
                    pr, rpr = dbl_region(d)
                    P.op("pe", lambda e: e.matmul(pr, X[0][:], cst[:, C_ID:C_ID + 128], start=True, stop=True), reads=[rs("X0"), R("cst")], writes=[rpr])
                    P.op("act", lambda e: e.copy(Y[0][:], pr), reads=[rpr], writes=[rs("Y0")])
                    P.op("dve", lambda e: e.tensor_tensor(Q[0][:], pr, cst[:, C_ID:C_ID + 128], ALU.add), reads=[rpr, R("cst")], writes=[rs("Q0")])
                    if DBG.get("cut", 99) < 6:
                        return
                    NL = 6
                    for l in range(NL):
                        a, b_ = l % 2, (l + 1) % 2
                        rX, rY, rQ = rs("X%d" % a), rs("Y%d" % a), rs("Q%d" % a)
                        rX2, rY2, rQ2 = rs("X%d" % b_), rs("Y%d" % b_), rs("Q%d" % b_)
                        px, rpx = dbl_region(d)
                        P.op("pe", lambda e: e.matmul(px, Y[a][:], X[a][:], start=True, stop=True), reads=[rX, rY], writes=[rpx])
                        if l < NL - 1:
                            py, rpy = dbl_region(d)
                            P.op("pe", lambda e: e.matmul(py, X[a][:], Y[a][:], start=True, stop=True), reads=[rX, rY], writes=[rpy])
                        P.op("act", lambda e: e.copy(X[b_][:], px), reads=[rpx], writes=[rX2])
                        if l < NL - 1:
                            P.op("dve", lambda e: e.tensor_copy(Y[b_][:], py), reads=[rpy], writes=[rY2])
                        pq, rpq = dbl_region(d)
                        P.op("pe", lambda e: e.matmul(pq, X[b_][:], Q[a][:], start=True, stop=True), reads=[rX2, rQ], writes=[rpq])
                        if l < NL - 1:
                            P.op("dve", lambda e: e.tensor_tensor(Q[b_][:], pq, Q[a][:], ALU.add), reads=[rpq, rQ], writes=[rQ2])
                        else:
                            P.op("dve", lambda e: e.tensor_tensor(TTb[d][sl][:], pq, Q[a][:], ALU.add), reads=[rpq, rQ], writes=[rs("TTb")])
                    if DBG.get("cut", 99) < 7:
                        return
                    P.op("pool", lambda e: e.tensor_scalar(BV[d][sl][:], vtok[:, n, :], Ball[:, n, col:col + 1], 1.0, ALU.mult, ALU.mult),
                         reads=[R(tag, "tok", 1), R(tag, "Ball")], writes=[rs("BV")])
                    P.op("pool", lambda e: e.tensor_scalar(kdec[d][sl][:], ktok[:, n, :], EKD[:, n, col:col + 1], 1.0, ALU.mult, ALU.mult),
                         reads=[R(tag, "tok", 0)] + rg, writes=[rs("kdec")])

                def scan(h, d, n, sl, s):
                    col = d * 6 + h
                    rs = lambda nm: R(tag, nm, d, sl)
                    cur, nxt = s % 2, (s + 1) % 2
                    k2 = s % 2
                    sbank = bk[5 + d]
                    r_kS, r_vn, r_dS, r_qS = [rbk(5 + d, j) for j in range(4)]
                    pkS, pvn, pdS, pqS = [sbank[:, j * 128:(j + 1) * 128] for j in range(4)]
                    oi = bk[7][:, (d * 2 + k2) * 128:(d * 2 + k2 + 1) * 128]
                    r_oi = rbk(7, d * 2 + k2)
                    rSb = R(tag, "Sbf", d, cur)
                    rSn = R(tag, "Sbf", d, nxt)
                    P.op("pe", lambda e: e.matmul(pkS, kqT[:, n, 0, :], Sbf[d][cur][:], start=True, stop=True),
                         reads=[R(tag, "kqT"), rSb], writes=[r_kS])
                    P.op("dve", lambda e: e.scalar_tensor_tensor(Rt[d][k2][:], pkS, NBEG[:, n, col:col + 1], BV[d][sl][:], ALU.mult, ALU.add),
                         reads=[r_kS, rs("BV")] + rg, writes=[R(tag, "Rt", d, k2)])
                    P.op("pe", lambda e: e.matmul(pvn, TTb[d][sl][:], Rt[d][k2][:], start=True, stop=True),
                         reads=[rs("TTb"), R(tag, "Rt", d, k2)], writes=[r_vn])
                    P.op("act", lambda e: e.copy(vnb[d][k2][:], pvn), reads=[r_vn], writes=[R(tag, "vnb", d, k2)])
                    P.op("pe", lambda e: e.matmul(pdS, kdec[d][sl][:], vnb[d][k2][:], start=True, stop=True),
                         reads=[rs("kdec"), R(tag, "vnb", d, k2)], writes=[r_dS])
                    P.op("dve", lambda e: e.scalar_tensor_tensor(Sbf[d][nxt][:], S32[d][:], EGL[:, n, col:col + 1], pdS, ALU.mult, ALU.add),
                         reads=[R(tag, "S32", d), r_dS] + rg, writes=[rSn])
                    P.op("dve", lambda e: e.scalar_tensor_tensor(S32[d][:], S32[d][:], EGL[:, n, col:col + 1], pdS, ALU.mult, ALU.add),
                         reads=[R(tag, "S32", d), r_dS] + rg, writes=[R(tag, "S32", d)])
                    P.op("pe", lambda e: e.matmul(pqS, kqT[:, n, 1, :], Sbf[d][cur][:], start=True, stop=True),
                         reads=[R(tag, "kqT"), rSb], writes=[r_qS])
                    P.op("pe", lambda e: e.matmul(oi, inT[d][sl][:], vnb[d][k2][:], start=True, stop=True),
                         reads=[rs("inT"), R(tag, "vnb", d, k2)], writes=[r_oi])
                    P.op("act", lambda e: e.activation(oq[d][k2][:], pqS, AF.Copy, scale=EG[:, n, col:col + 1]),
                         reads=[r_qS] + rg, writes=[R(tag, "oq", d, k2)])
                    P.op("dve", lambda e: e.tensor_tensor(oo[d][k2][:], oq[d][k2][:], oi, ALU.add),
                         reads=[R(tag, "oq", d, k2), r_oi], writes=[R(tag, "oo", d, k2)])
                    dst = (OF if d == 0 else OB)[n * 128:(n + 1) * 128, h * 128:(h + 1) * 128]
                    P.dma("pool", dst, oo[d][k2][:], reads=[R(tag, "oo", d, k2)])

                for h in range(DBG["b1_heads"]):
                    if DBG["b1_stage"] >= 1:
                        head_prep(h)
                    if DBG["b1_stage"] < 2:
                        continue
                    NCHR = DBG["b1_steps"]
                    for d in range(2):
                        P.op("pool", lambda e, d=d: e.memset(S32[d][:], 0.0), writes=[R(tag, "S32", d)])
                        P.op("pool", lambda e, d=d: e.memset(Sbf[d][0][:], 0.0), writes=[R(tag, "Sbf", d, 0)])
                    for d in range(2):
                        prep(h, d, 0 if d == 0 else NCH - 1, 0)
                    for s in range(NCHR):
                        if s + 1 < NCHR:
                            for d in range(2):
                                prep(h, d, (s + 1) if d == 0 else NCH - 2 - s, (s + 1) % NSL)
                        if DBG["b1_stage"] >= 3:
                            for d in range(2):
                                scan(h, d, s if d == 0 else NCH - 1 - s, s % NSL, s)
                P.emit()
            return P.nops

        nops = {}
        with nc.sbuf_tensor("QmT0", [128, 2, S], BF16) as qm0:
            qm_box[0] = qm0
            if want("A0"):
                nops["A0"] = phase_A(0)
            if want("B0"):
                nops["B0"] = phase_B0()
            if want("C0"):
                nops["C0"] = phase_C(0)
        if want("D0"):
            nops["D0"] = phase_D(0)
        with nc.sbuf_tensor("QmT1", [128, 2, S], BF16) as qm1:
            qm_box[0] = qm1
            if want("A1"):
                nops["A1"] = phase_A(1)
            if want("B1"):
                nops["B1"] = phase_B1()
            if want("C1"):
                nops["C1"] = phase_C(1)
        if want("D1"):
            nops["D1"] = phase_D(1)
    return nc, nops


def make_in_maps(inputs):
    f = lambda a: np.ascontiguousarray(np.asarray(a, dtype=np.float32))
    consts = make_consts()
    biasg = gather_bias(f(inputs["rel_bias"]))

    def pc(v):
        return f(v).reshape(8, 128).T
    gains = np.stack([pc(inputs["norm_mix_pre"][0]), pc(inputs["norm_mix_pre"][1]),
                      pc(inputs["norm_ffn_pre"][0]), pc(inputs["norm_ffn_pre"][1]),
                      pc(inputs["mem_norm"][0]), pc(inputs["mem_norm"][1])], axis=1)
    shared = {
        "biasg": biasg, "consts": consts, "gains_pc": f(gains),
        "att_w_in": f(inputs["att_w_in"][0]), "att_w_out": f(inputs["att_w_out"][0]),
        "dn_w_in": f(inputs["dn_w_in"][0]), "dn_convT": f(np.asarray(inputs["dn_conv"][0]).T),
        "dn_a_log": f(inputs["dn_a_log"][0]).reshape(12), "dn_dt_bias": f(inputs["dn_dt_bias"][0]).reshape(12),
        "dn_out_norm": f(inputs["dn_out_norm"][0]), "dn_w_out": f(inputs["dn_w_out"][0]),
        "mem_w_kv": f(inputs["mem_w_kv"]), "norm_mix_post": f(inputs["norm_mix_post"]),
        "norm_ffn_post": f(inputs["norm_ffn_post"]), "ffn_wgu": f(inputs["ffn_w_gate_up"]),
        "ffn_wd": f(inputs["ffn_w_down"]),
    }
    x = f(inputs["x"])
    mem = f(inputs["mem"])
    maps = []
    for b in range(8):
        m = dict(shared)
        m["x"] = x[b]
        m["mem"] = mem[b]
        maps.append(m)
    return maps


def kernel(**inputs):
    nc, _ = build()
    maps = make_in_maps(inputs)
    res = run_bass_kernel_spmd(nc, maps, core_ids=list(range(8)))
    return np.stack([np.asarray(r["y"], dtype=np.float32) for r in res.results], axis=0)
```

```python
import math
import types
from contextlib import ExitStack
import numpy as np
import concourse.bass as bass
import concourse.mybir as mybir
from concourse.bass_utils import run_bass_kernel_spmd

F32 = mybir.dt.float32
F32R = mybir.dt.float32r
BF16 = mybir.dt.bfloat16
AF = mybir.ActivationFunctionType
ALU = mybir.AluOpType
AX = mybir.AxisListType

ENGS = ("pe", "act", "dve", "pool", "sp")
N_DMA_SLOTS = 12

S = 4096
D = 1024
NT = S // 128
DFF = 2816
EPS = 1e-6
BIG = 1.0e30
GROUPS = ((128, 1), (512, 4), (2048, 16))
PSUM_NAMES = ("pT", "pF", "pV", "pL", "pO", "pY", "pG", "pU", "bk")
DBG = {"b1_heads": 6, "b1_steps": 32, "b1_stage": 9}


class Reg:
    __slots__ = ("name", "w", "r", "key")

    def __init__(self, name="", key=()):
        self.name = name
        self.key = key
        self.w = None
        self.r = []


class RegMap(dict):
    def __call__(self, *key):
        r = self.get(key)
        if r is None:
            r = Reg(str(key), key)
            self[key] = r
        return r


class Ctx:
    def __init__(self, nc):
        self.nc = nc
        self.sem = {}
        self.cnt = {}
        for e in ("pe", "act", "dve", "pool"):
            self.sem[e] = nc.alloc_semaphore("c_" + e)
            self.cnt[e] = 0
        self.dq = ("sp", "act", "pool")
        self.dslots = {}
        for q in self.dq:
            self.dslots[q] = []
            for i in range(N_DMA_SLOTS):
                k = "d_%s_%d" % (q, i)
                self.sem[k] = nc.alloc_semaphore(k)
                self.cnt[k] = 0
                self.dslots[q].append(k)
        self.dnext = {q: 0 for q in self.dq}
        self.known = {e: {} for e in ENGS}
        self.snap = {}


def _freeze(fn):
    if fn.__closure__ is None:
        return fn
    cells = []
    for c in fn.__closure__:
        try:
            cells.append(types.CellType(c.cell_contents))
        except ValueError:
            cells.append(c)
    return types.FunctionType(fn.__code__, fn.__globals__, fn.__name__, fn.__defaults__, tuple(cells))


class Phase:
    def __init__(self, ctx, name="ph"):
        self.ctx = ctx
        self.name = name
        self.q = {e: [] for e in ENGS}
        self.nops = 0
        self.excl = set()

    def _waits(self, e, reads, writes):
        ctx = self.ctx
        deps = {}

        def add(tok):
            if tok is None:
                return
            k, v = tok
            if deps.get(k, 0) < v:
                deps[k] = v
        for r in reads:
            add(r.w)
        for w in writes:
            if w.w is not None and w.w[0] != e:
                add(w.w)
            for t in w.r:
                if t[0] != e:
                    add(t)
        known = ctx.known[e]
        waits = []
        for k, v in deps.items():
            if k == e and e == "pe":
                continue
            if known.get(k, 0) >= v:
                continue
            waits.append((k, v))
            known[k] = v
            sn = ctx.snap.get((k, v))
            if sn is not None:
                for k2, v2 in sn.items():
                    if known.get(k2, 0) < v2:
                        known[k2] = v2
        return waits

    def op(self, e, fn, reads=(), writes=()):
        ctx = self.ctx
        ex = [r for r in reads if (len(r.key) > 1 and r.key[1] in PSUM_NAMES) or r in self.excl]
        if ex:
            writes = list(writes) + ex
        waits = self._waits(e, reads, writes)
        ctx.cnt[e] += 1
        tok = (e, ctx.cnt[e])
        ctx.snap[tok] = dict(ctx.known[e])
        for r in reads:
            r.r.append(tok)
        for w in writes:
            w.w = tok
            w.r = []
        self.q[e].append((waits, _freeze(fn), (e, 1)))
        self.nops += 1
        return tok

    def dma(self, q, out, in_, reads=(), writes=(), **kw):
        ctx = self.ctx
        slots = ctx.dslots[q]
        k = slots[ctx.dnext[q] % len(slots)]
        ctx.dnext[q] += 1
        waits = self._waits(q, reads, writes)
        known = ctx.known[q]
        if ctx.cnt[k] > 0 and known.get(k, 0) < ctx.cnt[k]:
            waits.append((k, ctx.cnt[k]))
            known[k] = ctx.cnt[k]
        ctx.cnt[k] += 16
        tok = (k, ctx.cnt[k])
        for r in reads:
            r.r.append(tok)
        for w in writes:
            w.w = tok
            w.r = []

        def fn(eng, out=out, in_=in_, kw=kw):
            return eng.dma_start(out=out, in_=in_, **kw)
        self.q[q].append((waits, fn, (k, 16)))
        self.nops += 1
        return tok

    def emit(self):
        ctx = self.ctx
        nc = ctx.nc
        fin = []
        for q in ctx.dq:
            for k in ctx.dslots[q]:
                if ctx.cnt[k] > 0 and ctx.known["sp"].get(k, 0) < ctx.cnt[k]:
                    fin.append((k, ctx.cnt[k]))
                    ctx.known["sp"][k] = ctx.cnt[k]
        qs = self.q
        sem = ctx.sem

        def body(e):
            def f(eng):
                for waits, fn, inc in qs[e]:
                    for k, v in waits:
                        eng.wait_ge(sem[k], v)
                    ins = fn(eng)
                    ins.then_inc(sem[inc[0]], inc[1])
                if e == "sp":
                    for k, v in fin:
                        eng.wait_ge(sem[k], v)
            return f

        with nc.Block() as block:
            block.tensor(body("pe"))
            block.scalar(body("act"))
            block.vector(body("dve"))
            block.gpsimd(body("pool"))
            block.sync(body("sp"))
        full = {}
        for e in ("pe", "act", "dve", "pool"):
            full[e] = ctx.cnt[e]
        for q in ctx.dq:
            for k in ctx.dslots[q]:
                full[k] = ctx.cnt[k]
        for e in ENGS:
            ctx.known[e] = dict(full)
        ctx.snap = {}


def clear_sems(ctx):
    nc = ctx.nc
    sems = list(ctx.sem.values())
    with nc.Block() as block:
        def f(eng):
            for s in sems:
                eng.sem_clear(s)
        block.gpsimd(f)


def _t5_bucket(rel):
    half = 16
    max_exact = 8
    n = np.abs(rel)
    large = max_exact + (np.log(np.maximum(n, 1) / max_exact) / math.log(1024 / max_exact)
                         * (half - max_exact)).astype(np.int64)
    large = np.minimum(large, half - 1)
    return ((rel > 0) * half + np.where(n < max_exact, n, large)).astype(np.int32)


C_ID = 0
C_MNEG = 128
C_MF = 384
C_MB = 640
C_TRIL = 896
C_TRIU = 1024
C_SELL = 1152
C_SELF = 1280
C_ID2 = 1408
NCONST = 1664


def make_consts():
    c = np.zeros((128, NCONST), np.float32)
    a = np.arange(128)[:, None]
    b = np.arange(128)[None, :]
    c[:, C_ID:C_ID + 128] = np.eye(128)
    col = np.arange(256)[None, :]
    rel = col - 64 - a
    c[:, C_MNEG:C_MNEG + 256] = np.where(np.abs(rel) <= 64, 0.0, -BIG)
    c[:, C_MF:C_MF + 128] = np.where(b >= a, BIG, 0.0)
    c[:, C_MF + 128:C_MF + 256] = np.where(b < a, -BIG, 0.0)
    c[:, C_MB:C_MB + 128] = np.where(b <= a, BIG, 0.0)
    c[:, C_MB + 128:C_MB + 256] = np.where(b > a, -BIG, 0.0)
    c[:, C_TRIL:C_TRIL + 128] = (a <= b)
    c[:, C_TRIU:C_TRIU + 128] = (a >= b)
    c[127, C_SELL:C_SELL + 128] = 1.0
    c[0, C_SELF:C_SELF + 128] = 1.0
    c[:, C_ID2:C_ID2 + 128] = np.eye(128)
    c[:, C_ID2 + 128:C_ID2 + 256] = np.eye(128)
    return c


def gather_bias(rel_bias):
    out = np.zeros((12, 128, 256), np.float32)
    a = np.arange(128)[:, None]
    col = np.arange(256)[None, :]
    rel = col - 64 - a
    inb = np.abs(rel) <= 64
    relc = np.clip(rel, -64, 64)
    for gi, (window, dil) in enumerate(GROUPS):
        bk = _t5_bucket(relc * dil)
        for hh in range(4):
            h = gi * 4 + hh
            out[h] = np.where(inb, rel_bias[bk, h], 0.0)
    return out


def build(debug=False, phases=None):
    nc = bass.Bass("TRN2", target_bir_lowering=False)

    def din(name, shape, dt=F32):
        return nc.dram_tensor(name, list(shape), dt, kind="ExternalInput").ap()

    def dscr(name, shape, dt):
        kind = "ExternalOutput" if (debug and (DBG.get("outs") is None or name in DBG["outs"])) else "Internal"
        return nc.dram_tensor(name, list(shape), dt, kind=kind).ap()

    def want(p):
        return phases is None or p in phases

    x_in = din("x", [S, D])
    mem_in = din("mem", [256, D])
    biasg = din("biasg", [12, 128, 256])
    consts_in = din("consts", [128, NCONST])
    gains_pc = din("gains_pc", [128, 6, 8])
    att_w_in = din("att_w_in", [D, 2560])
    att_w_out = din("att_w_out", [D, D])
    dn_w_in = din("dn_w_in", [D, 3352])
    dn_convT = din("dn_convT", [2304, 5])
    dn_a_log = din("dn_a_log", [12])
    dn_dt_bias = din("dn_dt_bias", [12])
    dn_out_norm = din("dn_out_norm", [128])
    dn_w_out = din("dn_w_out", [D, D])
    mem_w_kv = din("mem_w_kv", [2, D, 512])
    norm_mix_post = din("norm_mix_post", [2, D])
    norm_ffn_post = din("norm_ffn_post", [2, D])
    ffn_wgu = din("ffn_wgu", [2, D, 2 * DFF])
    ffn_wd = din("ffn_wd", [2, DFF, D])
    y_out = nc.dram_tensor("y", [S, D], F32, kind="ExternalOutput").ap()

    QTd = dscr("QTd", [6, 128, S], BF16)
    KTd = dscr("KTd", [6, 128, S], BF16)
    V0 = dscr("V0", [S, 768], BF16)
    OATT = dscr("OATT", [S, 768], F32)
    LSE = dscr("LSE", [S, 12], F32)
    X1 = dscr("X1", [S, D], F32)
    X2 = dscr("X2", [S, D], F32)
    X3 = dscr("X3", [S, D], F32)
    QKVT = dscr("QKVT", [2304, S], BF16)
    SZ = dscr("SZ", [S, 768], BF16)
    GD = dscr("GD", [S, 12], F32)
    BETA = dscr("BETA", [S, 12], F32)
    OF = dscr("OF", [S, 768], F32)
    OB = dscr("OB", [S, 768], F32)

    ctx = Ctx(nc)
    clear_sems(ctx)
    R = RegMap()

    with ExitStack() as gs:
        def gsb(name, shape, dt):
            return gs.enter_context(nc.sbuf_tensor(name, list(shape), dt))

        cst = gsb("cst", [128, NCONST], F32)
        identb = gsb("identb", [128, 128], BF16)
        onesb = gsb("onesb", [128, 128], BF16)
        ones32 = gsb("ones32", [128, 128], F32)
        gpc = gsb("gpc", [128, 6, 8], F32)
        qm_box = [None]
        kmT = gsb("kmT", [128, 2, 256], BF16)
        vm = gsb("vm", [128, 2, 256], BF16)
        ident32 = cst[:, C_ID:C_ID + 128]

        P = Phase(ctx, "setup")
        P.dma("sp", cst[:], consts_in, writes=[R("cst")])
        P.dma("sp", gpc[:], gains_pc, writes=[R("gpc")])
        P.op("dve", lambda e: e.tensor_copy(identb[:], cst[:, C_ID:C_ID + 128]), reads=[R("cst")], writes=[R("identb")])
        P.op("pool", lambda e: e.memset(onesb[:], 1.0), writes=[R("onesb")])
        P.op("pool", lambda e: e.memset(ones32[:], 1.0), writes=[R("ones32")])
        P.emit()

        def load_weight(P, tag, dst, src, KC, N, stg, gain=None, PW=512, engs=("dve", "pool")):
            srcv = src.rearrange("(c p) n -> p c n", p=128)
            i = 0
            for c0 in range(0, N, PW):
                w = min(PW, N - c0)
                b = i % len(stg)
                P.dma("sp", stg[b][:, :, 0:w], srcv[:, :, c0:c0 + w], writes=[R("stg", id(stg[b]))])
                eng = engs[i % len(engs)]
                if gain is not None:
                    P.op(eng, lambda e, b=b, c0=c0, w=w: e.tensor_tensor(
                        dst[:, :, c0:c0 + w], stg[b][:, :, 0:w],
                        gain.unsqueeze(2).broadcast_to([128, KC, w]), ALU.mult),
                        reads=[R("stg", id(stg[b])), R("gpc")], writes=[R(tag, "w")])
                else:
                    P.op(eng, lambda e, b=b, c0=c0, w=w: e.tensor_copy(dst[:, :, c0:c0 + w], stg[b][:, :, 0:w]),
                         reads=[R("stg", id(stg[b]))], writes=[R(tag, "w")])
                i += 1

        def rms_prep(P, tag, b, xt, sqj, ssq, rstd, xn, r_x):
            P.op("act", lambda e: e.activation(sqj, xt, AF.Square, accum_out=ssq),
                 reads=[r_x], writes=[R(tag, "sqj"), R(tag, "ssq", b)])
            P.op("act", lambda e: e.activation(rstd, ssq, AF.Sqrt, bias=EPS, scale=1.0 / D),
                 reads=[R(tag, "ssq", b)], writes=[R(tag, "rstd", b)])
            P.op("dve", lambda e: e.reciprocal(rstd, rstd), reads=[R(tag, "rstd", b)], writes=[R(tag, "rstd", b)])
            P.op("dve", lambda e: e.tensor_scalar(xn, xt, rstd, None, ALU.mult),
                 reads=[r_x, R(tag, "rstd", b)], writes=[R(tag, "xn", b)])

        def transpose8(P, tag, b, xn, pT, r_pT, dst, r_dst, evac="act"):
            for c in range(8):
                P.op("pe", lambda e, c=c: e.transpose(pT[:, c, :], xn[:, c * 128:(c + 1) * 128], identb[:]),
                     reads=[R(tag, "xn", b), R("identb")], writes=[r_pT])
            if evac == "act":
                P.op("act", lambda e: e.copy(dst, pT[:]), reads=[r_pT], writes=[r_dst])
            else:
                P.op("dve", lambda e: e.tensor_copy(dst, pT[:]), reads=[r_pT], writes=[r_dst])

        def post_norm_residual(P, tag, b, pY, r_pY, gpost, xres, r_xres, sqj, ssq2, rstd, tmp, xo, r_xo):
            rpy = r_pY if isinstance(r_pY, list) else [r_pY]
            for hf in range(2):
                P.op("act", lambda e, hf=hf: e.activation(sqj[:, hf * 512:(hf + 1) * 512], pY[:, hf * 512:(hf + 1) * 512],
                                                          AF.Square, accum_out=ssq2[:, hf:hf + 1]),
                     reads=rpy, writes=[R(tag, "psqj"), R(tag, "pssq", b, hf)])
            P.op("dve", lambda e: e.tensor_tensor(rstd, ssq2[:, 0:1], ssq2[:, 1:2], ALU.add),
                 reads=[R(tag, "pssq", b, 0), R(tag, "pssq", b, 1)], writes=[R(tag, "prstd", b)])
            P.op("act", lambda e: e.activation(rstd, rstd, AF.Sqrt, bias=EPS, scale=1.0 / D),
                 reads=[R(tag, "prstd", b)], writes=[R(tag, "prstd", b)])
            P.op("dve", lambda e: e.reciprocal(rstd, rstd), reads=[R(tag, "prstd", b)], writes=[R(tag, "prstd", b)])
            P.op("dve", lambda e: e.scalar_tensor_tensor(tmp, pY, rstd, gpost, ALU.mult, ALU.mult),
                 reads=rpy + [R(tag, "prstd", b), R(tag, "gpost")], writes=[R(tag, "ptmp")])
            P.op("pool", lambda e: e.tensor_tensor(xo, tmp, xres, ALU.add),
                 reads=[R(tag, "ptmp"), r_xres], writes=[r_xo])

        def mem_kv(P, es, li, pbank, stg, mt, sqj, ssq, rstd, xn):
            tag = "mkv%d" % li
            sb = lambda n, s, d: es.enter_context(nc.sbuf_tensor(tag + n, list(s), d))
            wkv = sb("w", [128, 8, 512], BF16)
            mT = sb("mT", [128, 8, 256], BF16)
            load_weight(P, tag, wkv, mem_w_kv[li], 8, 512, stg, gain=gpc[:, 4 + li, :], PW=256)
            for t in range(2):
                ptag = "A%d" % li
                P.dma("sp", mt[:], mem_in[t * 128:(t + 1) * 128, :], writes=[R(ptag, "xt", 0)])
                rms_prep(P, ptag, 0, mt[:], sqj[:], ssq[:], rstd[:], xn[:], R(ptag, "xt", 0))
                transpose8(P, ptag, 0, xn, pbank["T"], R(ptag, "pT", 0), mT[:, :, t * 128:(t + 1) * 128], R(tag, "mT"))
            pk = pbank["F"]
            for cb in range(2):
                for c in range(8):
                    P.op("pe", lambda e, cb=cb, c=c: e.matmul(pk[:, 0:256], wkv[:, c, cb * 128:(cb + 1) * 128], mT[:, c, :],
                                                              start=(c == 0), stop=(c == 7)),
                         reads=[R(tag, "w"), R(tag, "mT")], writes=[R("A%d" % li, "pF", 0)])
                P.op("act", lambda e, cb=cb: e.copy(kmT[:, cb, :], pk[:, 0:256]), reads=[R("A%d" % li, "pF", 0)], writes=[R("kmT")])
            for t in range(2):
                for c in range(8):
                    P.op("pe", lambda e, t=t, c=c: e.matmul(pk[:, 0:256], mT[:, c, t * 128:(t + 1) * 128], wkv[:, c, 256:512],
                                                            start=(c == 0), stop=(c == 7)),
                         reads=[R(tag, "w"), R(tag, "mT")], writes=[R("A%d" % li, "pF", 0)])
                P.op("act", lambda e, t=t: e.copy(vm[:, t, :], pk[:, 0:256]), reads=[R("A%d" % li, "pF", 0)], writes=[R("vm")])

        def phase_A(li):
            tag = "A%d" % li
            P = Phase(ctx, tag)
            with ExitStack() as es:
                sb = lambda n, s, d: es.enter_context(nc.sbuf_tensor(tag + n, list(s), d))
                ps = lambda n, s, d: es.enter_context(nc.psum_tensor(tag + n, list(s), d))
                NW = 2560 if li == 0 else 3352
                w_src = att_w_in if li == 0 else dn_w_in
                x_src = x_in if li == 0 else X2
                wb = sb("wb", [128, 8, NW], BF16)
                stg = [sb("stg%d" % i, [128, 8, 256], F32) for i in range(2)]
                xt = [sb("xt%d" % i, [128, D], F32) for i in range(2)]
                sqj = sb("sqj", [128, D], BF16)
                ssq = [sb("ssq%d" % i, [128, 1], F32) for i in range(2)]
                rstd = [sb("rstd%d" % i, [128, 1], F32) for i in range(2)]
                xn = [sb("xn%d" % i, [128, D], BF16) for i in range(2)]
                hT = [sb("hT%d" % i, [128, 8, 512], BF16) for i in range(2)]
                fo = [sb("fo%d" % i, [128, 512], BF16) for i in range(3)]
                pT = [ps("pT%d" % i, [128, 8, 128], BF16) for i in range(2)]
                pF = [ps("pF%d" % i, [128, 512], F32) for i in range(2)]
                pV = [ps("pV%d" % i, [128, 1024], F32) for i in range(1)]
                mem_kv(P, es, li, {"T": pT[0], "F": pF[0]}, stg, xt[0], sqj, ssq[0], rstd[0], xn[0])
                load_weight(P, tag, wb, w_src, 8, NW, stg, gain=gpc[:, li, :], PW=256)
                if li == 0:
                    vo = [sb("vo%d" % i, [128, 768], BF16) for i in range(2)]
                    feat_blocks = [("q", i) for i in range(6)] + [("k", i) for i in range(6)] + [("m", i) for i in range(2)]
                else:
                    vo = [sb("vo%d" % i, [128, 768], BF16) for i in range(2)]
                    feat_blocks = [("c", i) for i in range(18)] + [("m", i) for i in range(2)]
                    gi = [sb("gi%d" % i, [128, 24], F32) for i in range(2)]
                    gw = [sb("gw%d" % i, [128, 6, 12], F32) for i in range(2)]
                    go = [sb("go%d" % i, [128, 2, 12], F32) for i in range(2)]
                    alog_b = sb("alogb", [128, 12], F32)
                    dtb_b = sb("dtbb", [128, 12], F32)
                    P.dma("sp", alog_b[:], dn_a_log.partition_broadcast(128), writes=[R(tag, "alog")])
                    P.dma("sp", dtb_b[:], dn_dt_bias.partition_broadcast(128), writes=[R(tag, "dtb")])
                    P.op("act", lambda e: e.activation(alog_b[:], alog_b[:], AF.Exp), reads=[R(tag, "alog")], writes=[R(tag, "alog")])
                    P.op("dve", lambda e: e.tensor_scalar(alog_b[:], alog_b[:], -1.0, None, ALU.mult),
                         reads=[R(tag, "alog")], writes=[R(tag, "alog")])
                nfo = 0
                for st in range(S // 512):
                    hb = st % 2
                    for s4 in range(4):
                        t = st * 4 + s4
                        b = t % 2
                        P.dma("sp", xt[b][:], x_src[t * 128:(t + 1) * 128, :], writes=[R(tag, "xt", b)])
                        rms_prep(P, tag, b, xt[b][:], sqj[:], ssq[b][:], rstd[b][:], xn[b][:], R(tag, "xt", b))
                        transpose8(P, tag, b, xn[b], pT[b], R(tag, "pT", b),
                                   hT[hb][:, :, s4 * 128:(s4 + 1) * 128], R(tag, "hT", hb))
                    for kind, i in feat_blocks:
                        if li == 0:
                            col0 = {"q": 0, "k": 768, "m": 2304}[kind] + i * 128
                        else:
                            col0 = {"c": 0, "m": 3096}[kind] + i * 128
                        pb = nfo % 2
                        for c in range(8):
                            P.op("pe", lambda e, c=c, col0=col0, pb=pb: e.matmul(pF[pb][:], wb[:, c, col0:col0 + 128], hT[hb][:, c, :],
                                                                                  start=(c == 0), stop=(c == 7)),
                                 reads=[R(tag, "w"), R(tag, "hT", hb)], writes=[R(tag, "pF", pb)])
                        sc = 0.125 if kind in ("q", "m") else 1.0
                        if kind == "m":
                            P.op("act", lambda e, pb=pb, i=i, sc=sc: e.activation(qm_box[0][:, i, st * 512:(st + 1) * 512], pF[pb][:], AF.Copy, scale=sc),
                                 reads=[R(tag, "pF", pb)], writes=[R("QmT")])
                        elif kind in ("q", "k"):
                            dst_t = QTd if kind == "q" else KTd
                            dil = GROUPS[i // 2][1]
                            m0 = st * 512 // dil
                            nm = 512 // dil
                            fb = nfo % 3
                            dsts = fo[fb][:].rearrange("p (r m) -> p r m", r=dil)
                            src = pF[pb][:].rearrange("p (m r) -> p r m", r=dil)
                            P.op("act", lambda e, dsts=dsts, src=src, sc=sc: e.activation(dsts, src, AF.Copy, scale=sc),
                                 reads=[R(tag, "pF", pb)], writes=[R(tag, "fo", fb)])
                            P.dma("pool", dst_t[i].rearrange("p (r l) -> p r l", r=dil)[:, :, m0:m0 + nm], dsts,
                                  reads=[R(tag, "fo", fb)])
                        else:
                            fb = nfo % 3
                            P.op("act", lambda e, pb=pb, fb=fb: e.copy(fo[fb][:], pF[pb][:]),
                                 reads=[R(tag, "pF", pb)], writes=[R(tag, "fo", fb)])
                            P.dma("pool", QKVT[i * 128:(i + 1) * 128, st * 512:(st + 1) * 512], fo[fb][:], reads=[R(tag, "fo", fb)])
                        nfo += 1
                    vcol = 1536 if li == 0 else 2304
                    for s4 in range(4):
                        t = st * 4 + s4
                        b = t % 2
                        for (n0, nn) in ((0, 512), (512, 256)):
                            for c in range(8):
                                P.op("pe", lambda e, c=c, n0=n0, nn=nn, s4=s4: e.matmul(
                                    pV[0][:, n0:n0 + nn], hT[hb][:, c, s4 * 128:(s4 + 1) * 128], wb[:, c, vcol + n0:vcol + n0 + nn],
                                    start=(c == 0), stop=(c == 7)),
                                    reads=[R(tag, "w"), R(tag, "hT", hb)], writes=[R(tag, "pV", n0)])
                        if li == 0:
                            P.op("dve", lambda e, b=b: e.tensor_copy(vo[b][:], pV[0][:, 0:768]),
                                 reads=[R(tag, "pV", 0), R(tag, "pV", 512)], writes=[R(tag, "vo", b)])
                            P.dma("pool", V0[t * 128:(t + 1) * 128, :], vo[b][:], reads=[R(tag, "vo", b)])
                        else:
                            for c in range(8):
                                P.op("pe", lambda e, c=c, s4=s4: e.matmul(
                                    pV[0][:, 768:792], hT[hb][:, c, s4 * 128:(s4 + 1) * 128], wb[:, c, 3072:3096],
                                    start=(c == 0), stop=(c == 7)),
                                    reads=[R(tag, "w"), R(tag, "hT", hb)], writes=[R(tag, "pV", 512)])
                            P.op("act", lambda e, b=b: e.activation(vo[b][:], pV[0][:, 0:768], AF.Silu),
                                 reads=[R(tag, "pV", 0), R(tag, "pV", 512)], writes=[R(tag, "vo", b)])
                            P.dma("pool", SZ[t * 128:(t + 1) * 128, :], vo[b][:], reads=[R(tag, "vo", b)])
                            P.op("dve", lambda e, b=b: e.tensor_copy(gi[b][:], pV[0][:, 768:792]),
                                 reads=[R(tag, "pV", 512)], writes=[R(tag, "gi", b)])
                            giv = gi[b][:].rearrange("p (d a h) -> p d a h", d=2, a=2)
                            w0 = gw[b][:, 0, :].rearrange("p (d h) -> p d h", d=2)
                            w1 = gw[b][:, 1, :].rearrange("p (d h) -> p d h", d=2)
                            rg = [R(tag, "gw", b)]
                            P.op("dve", lambda e, w0=w0, giv=giv: e.tensor_tensor(
                                w0, giv[:, :, 0, :], dtb_b[:].rearrange("p (d h) -> p d h", d=2), ALU.add),
                                reads=[R(tag, "gi", b), R(tag, "dtb")], writes=rg)
                            P.op("dve", lambda e, w1=w1, giv=giv: e.tensor_copy(w1, giv[:, :, 1, :]),
                                 reads=[R(tag, "gi", b)], writes=rg)
                            P.op("act", lambda e, b=b: e.activation(gw[b][:, 2, :], gw[b][:, 0, :], AF.Abs),
                                 reads=rg, writes=rg)
                            P.op("act", lambda e, b=b: e.activation(gw[b][:, 2, :], gw[b][:, 2, :], AF.Exp, scale=-1.0),
                                 reads=rg, writes=rg)
                            P.op("act", lambda e, b=b: e.activation(gw[b][:, 2, :], gw[b][:, 2, :], AF.Ln, bias=1.0),
                                 reads=rg, writes=rg)
                            P.op("dve", lambda e, b=b: e.scalar_tensor_tensor(gw[b][:, 3, :], gw[b][:, 0, :], 0.0, gw[b][:, 2, :], ALU.max, ALU.add),
                                 reads=rg, writes=rg)
                            P.op("dve", lambda e, b=b: e.tensor_tensor(go[b][:, 0, :], gw[b][:, 3, :], alog_b[:], ALU.mult),
                                 reads=rg + [R(tag, "alog")], writes=[R(tag, "go", b)])
                            P.op("act", lambda e, b=b: e.activation(gw[b][:, 4, :], gw[b][:, 1, :], AF.Exp, scale=-1.0),
                                 reads=rg, writes=rg)
                            P.op("dve", lambda e, b=b: e.tensor_scalar(gw[b][:, 4, :], gw[b][:, 4, :], 1.0, None, ALU.add),
                                 reads=rg, writes=rg)
                            P.op("dve", lambda e, b=b: e.reciprocal(go[b][:, 1, :], gw[b][:, 4, :]),
                                 reads=rg, writes=[R(tag, "go", b)])
                            P.dma("pool", GD[t * 128:(t + 1) * 128, :], go[b][:, 0, :], reads=[R(tag, "go", b)])
                            P.dma("pool", BETA[t * 128:(t + 1) * 128, :], go[b][:, 1, :], reads=[R(tag, "go", b)])
                P.emit()
            return P.nops

        def phase_B0():
            tag = "B0"
            P = Phase(ctx, tag)
            with ExitStack() as es:
                sb = lambda n, s, d: es.enter_context(nc.sbuf_tensor(tag + n, list(s), d))
                ps = lambda n, s, d: es.enter_context(nc.psum_tensor(tag + n, list(s), d))
                bm = sb("bm", [128, 12, 256], F32)
                P.dma("sp", bm[:], biasg.rearrange("h q k -> q h k"), writes=[R(tag, "bm")])
                P.op("dve", lambda e: e.tensor_tensor(bm[:], bm[:], cst[:, C_MNEG:C_MNEG + 256].unsqueeze(1).broadcast_to([128, 12, 256]), ALU.add),
                     reads=[R(tag, "bm"), R("cst")], writes=[R(tag, "bm")])
                NB = 3
                vw = [sb("vw%d" % i, [128, 2, 256], BF16) for i in range(NB)]
                sl = [sb("sl%d" % i, [128, 256], F32) for i in range(NB)]
                pe_ = [sb("p%d" % i, [128, 256], BF16) for i in range(NB)]
                pTs = [sb("pTs%d" % i, [128, 2, 128], BF16) for i in range(NB)]
                st_ = [sb("st%d" % i, [128, 3, 4], F32) for i in range(NB)]
                on = [sb("on%d" % i, [128, 4, 64], F32) for i in range(NB)]
                pL = [ps("pL%d" % i, [128, 256], F32) for i in range(2)]
                pT = [ps("pT%d" % i, [128, 2, 128], BF16) for i in range(2)]
                pO = [ps("pO%d" % i, [128, 4, 64], F32) for i in range(2)]
                qs = [sb("qs%d" % i, [128, 2, S], BF16) for i in range(2)]
                ks = [sb("ks%d" % i, [128, 2, S], BF16) for i in range(2)]
                it = 0
                ih = 0
                for g, (window, dil) in enumerate(GROUPS):
                    L = S // dil
                    ntile = L // 128
                    gb = g % 2
                    QT = qs[gb]
                    KT = ks[gb]
                    for b2 in range(2):
                        P.dma("sp", QT[:, b2, :], QTd[2 * g + b2], writes=[R(tag, "QK", gb)])
                        P.dma("sp", KT[:, b2, :], KTd[2 * g + b2], writes=[R(tag, "QK", gb)])
                    for r in range(dil):
                        for qt in range(ntile):
                            b = it % NB
                            ob = it % 2
                            base = r * L + qt * 128
                            first = (qt == 0)
                            last = (qt == ntile - 1)
                            c_lo = 64 if first else 0
                            c_hi = 192 if last else 256
                            for c in range(2):
                                m0 = qt * 128 - 64 + 128 * c
                                p_lo = 64 if (first and c == 0) else 0
                                p_hi = 64 if (last and c == 1) else 128
                                row0 = (m0 + p_lo) * dil + r
                                nrow = p_hi - p_lo
                                srcv = V0[row0:row0 + (nrow - 1) * dil + 1:dil, g * 256:(g + 1) * 256] if dil > 1 else \
                                    V0[row0:row0 + nrow, g * 256:(g + 1) * 256]
                                P.dma("sp", vw[b][p_lo:p_hi, c, :], srcv, writes=[R(tag, "vw", b, c)])
                            for hh in range(4):
                                h = g * 4 + hh
                                blk = hh // 2
                                pl = (hh % 2) * 64
                                lb = ih % 2
                                P.op("pe", lambda e, lb=lb, blk=blk, pl=pl, base=base, c_lo=c_lo, c_hi=c_hi, QT=QT, KT=KT: e.matmul(
                                    pL[lb][:, c_lo:c_hi], QT[pl:pl + 64, blk, base:base + 128],
                                    KT[pl:pl + 64, blk, base - 64 + c_lo:base - 64 + c_hi], start=True, stop=True),
                                    reads=[R(tag, "QK", gb)], writes=[R(tag, "pL", lb)])
                                P.op("dve", lambda e, lb=lb, b=b, h=h, c_lo=c_lo, c_hi=c_hi: e.tensor_tensor(
                                    sl[b][:, c_lo:c_hi], pL[lb][:, c_lo:c_hi], bm[:, h, c_lo:c_hi], ALU.add),
                                    reads=[R(tag, "pL", lb), R(tag, "bm")], writes=[R(tag, "sl", b)])
                                P.op("dve", lambda e, b=b, hh=hh, c_lo=c_lo, c_hi=c_hi: e.tensor_reduce(
                                    st_[b][:, 0, hh:hh + 1], sl[b][:, c_lo:c_hi], AX.X, ALU.max, negate=True),
                                    reads=[R(tag, "sl", b)], writes=[R(tag, "nmx", b, hh)])
                                P.op("act", lambda e, b=b, hh=hh, c_lo=c_lo, c_hi=c_hi: e.activation(
                                    pe_[b][:, c_lo:c_hi], sl[b][:, c_lo:c_hi], AF.Exp, bias=st_[b][:, 0, hh:hh + 1],
                                    accum_out=st_[b][:, 1, hh:hh + 1]),
                                    reads=[R(tag, "sl", b), R(tag, "nmx", b, hh)], writes=[R(tag, "p", b), R(tag, "den", b, hh)])
                                for c in range(2):
                                    P.op("pe", lambda e, lb=lb, b=b, c=c: e.transpose(pT[lb][:, c, :], pe_[b][:, c * 128:(c + 1) * 128], identb[:]),
                                         reads=[R(tag, "p", b), R("identb")], writes=[R(tag, "pT", lb)])
                                P.op("dve" if hh % 2 else "act",
                                     (lambda e, lb=lb, b=b: e.tensor_copy(pTs[b][:], pT[lb][:])) if hh % 2 else
                                     (lambda e, lb=lb, b=b: e.copy(pTs[b][:], pT[lb][:])),
                                     reads=[R(tag, "pT", lb)], writes=[R(tag, "pTs", b)])
                                for c in range(2):
                                    p_lo = 64 if (first and c == 0) else 0
                                    p_hi = 64 if (last and c == 1) else 128
                                    P.op("pe", lambda e, ob=ob, b=b, c=c, hh=hh, p_lo=p_lo, p_hi=p_hi: e.matmul(
                                        pO[ob][:, hh, :], pTs[b][p_lo:p_hi, c, :], vw[b][p_lo:p_hi, c, hh * 64:(hh + 1) * 64],
                                        start=(c == 0), stop=(c == 1)),
                                        reads=[R(tag, "pTs", b), R(tag, "vw", b, c)], writes=[R(tag, "pO", ob, hh)])
                                ih += 1
                            rden = [R(tag, "den", b, hh) for hh in range(4)]
                            rnmx = [R(tag, "nmx", b, hh) for hh in range(4)]
                            P.op("act", lambda e, b=b: e.activation(st_[b][:, 2, :], st_[b][:, 1, :], AF.Ln),
                                 reads=rden, writes=[R(tag, "lse", b)])
                            P.op("dve", lambda e, b=b: e.tensor_tensor(st_[b][:, 2, :], st_[b][:, 2, :], st_[b][:, 0, :], ALU.subtract),
                                 reads=[R(tag, "lse", b)] + rnmx, writes=[R(tag, "lse", b)])
                            P.op("dve", lambda e, b=b: e.reciprocal(st_[b][:, 1, :], st_[b][:, 1, :]),
                                 reads=rden + [R(tag, "lse", b)], writes=rden)
                            P.op("dve", lambda e, b=b, ob=ob: e.tensor_tensor(
                                on[b][:], pO[ob][:], st_[b][:, 1, :].unsqueeze(2).broadcast_to([128, 4, 64]), ALU.mult),
                                reads=rden + [R(tag, "pO", ob, hh) for hh in range(4)], writes=[R(tag, "on", b)])
                            row0 = (qt * 128) * dil + r
                            if dil > 1:
                                dsto = OATT[row0:row0 + 127 * dil + 1:dil, g * 256:(g + 1) * 256]
                                dstl = LSE[row0:row0 + 127 * dil + 1:dil, g * 4:(g + 1) * 4]
                            else:
                                dsto = OATT[row0:row0 + 128, g * 256:(g + 1) * 256]
                                dstl = LSE[row0:row0 + 128, g * 4:(g + 1) * 4]
                            P.dma("pool", dsto, on[b][:].rearrange("p h d -> p (h d)"), reads=[R(tag, "on", b)])
                            P.dma("pool", dstl, st_[b][:, 2, :], reads=[R(tag, "lse", b)])
                            it += 1
                P.emit()
            return P.nops

        def phase_C(li):
            tag = "C%d" % li
            P = Phase(ctx, tag)
            x_src = x_in if li == 0 else X2
            x_dst = X1 if li == 0 else X3
            w_src = att_w_out if li == 0 else dn_w_out
            with ExitStack() as es:
                sb = lambda n, s, d: es.enter_context(nc.sbuf_tensor(tag + n, list(s), d))
                ps = lambda n, s, d: es.enter_context(nc.psum_tensor(tag + n, list(s), d))
                wo = sb("wo", [128, 8, D], BF16)
                stg = [sb("stg%d" % i, [128, 8, 256], F32) for i in range(2)]
                gpost = sb("gpost", [128, D], F32)
                P.dma("sp", gpost[:], norm_mix_post[li].partition_broadcast(128), writes=[R(tag, "gpost")])
                load_weight(P, tag, wo, w_src, 8, D, stg, PW=256)
                NB = 2
                ot = [sb("ot%d" % i, [128, 768], F32) for i in range(NB)]
                cat = [sb("cat%d" % i, [128, D], BF16) for i in range(NB)]
                catT = [sb("catT%d" % i, [128, 8, 128], BF16) for i in range(NB)]
                xr = [sb("xr%d" % i, [128, D], F32) for i in range(NB)]
                xo = [sb("xo%d" % i, [128, D], F32) for i in range(NB)]
                tmp = [sb("tmp%d" % i, [128, D], F32) for i in range(NB)]
                sqj = sb("sqj", [128, D], BF16)
                ssq2 = [sb("ssq2%d" % i, [128, 2], F32) for i in range(NB)]
                rstd = [sb("rstd%d" % i, [128, 1], F32) for i in range(NB)]
                pm = [sb("pm%d" % i, [128, 256], BF16) for i in range(NB)]
                pmT = [sb("pmT%d" % i, [128, 2, 128], BF16) for i in range(NB)]
                ms = [sb("ms%d" % i, [128, 2, 4], F32) for i in range(NB)]
                if li == 0:
                    ls = [sb("ls%d" % i, [128, 4, 12], F32) for i in range(NB)]
                else:
                    ot2 = [sb("ot2%d" % i, [128, 768], F32) for i in range(NB)]
                    szt = [sb("szt%d" % i, [128, 768], BF16) for i in range(NB)]
                    os_ = [sb("os%d" % i, [128, 2, 6], F32) for i in range(NB)]
                    onb = sb("onb", [128, 128], F32)
                    P.dma("sp", onb[:], dn_out_norm.partition_broadcast(128), writes=[R(tag, "onb")])
                pL = [ps("pL%d" % i, [128, 256], F32) for i in range(2)]
                pT = [ps("pT%d" % i, [128, 8, 128], BF16) for i in range(1)]
                pO = [ps("pO%d" % i, [128, 4, 64], F32) for i in range(1)]
                pY = [ps("pY%d" % i, [128, D], F32) for i in range(2)]
                ih = 0
                for t in range(NT):
                    b = t % NB
                    rows = slice(t * 128, (t + 1) * 128)
                    P.dma("sp", xr[b][:], x_src[rows, :], writes=[R(tag, "xr", b)])
                    if li == 0:
                        P.dma("sp", ot[b][:], OATT[rows, :], writes=[R(tag, "ot", b)])
                        P.dma("sp", ls[b][:, 0, :], LSE[rows, :], writes=[R(tag, "ls", b)])
                        rl = [R(tag, "ls", b)]
                        l3 = ls[b][:, 0, :].rearrange("p (g h) -> p g h", g=3)
                        e3 = ls[b][:, 1, :].rearrange("p (g h) -> p g h", g=3)
                        P.op("dve", lambda e, b=b, l3=l3: e.tensor_tensor(ls[b][:, 2, 0:4], l3[:, 0, :], l3[:, 1, :], ALU.max), reads=rl, writes=rl)
                        P.op("dve", lambda e, b=b, l3=l3: e.tensor_tensor(ls[b][:, 2, 0:4], ls[b][:, 2, 0:4], l3[:, 2, :], ALU.max), reads=rl, writes=rl)
                        P.op("dve", lambda e, b=b, l3=l3, e3=e3: e.tensor_tensor(
                            e3, l3, ls[b][:, 2, 0:4].unsqueeze(1).broadcast_to([128, 3, 4]), ALU.subtract), reads=rl, writes=rl)
                        P.op("act", lambda e, b=b: e.activation(ls[b][:, 1, :], ls[b][:, 1, :], AF.Exp), reads=rl, writes=rl)
                        P.op("dve", lambda e, b=b, e3=e3: e.tensor_tensor(ls[b][:, 2, 4:8], e3[:, 0, :], e3[:, 1, :], ALU.add), reads=rl, writes=rl)
                        P.op("dve", lambda e, b=b, e3=e3: e.tensor_tensor(ls[b][:, 2, 4:8], ls[b][:, 2, 4:8], e3[:, 2, :], ALU.add), reads=rl, writes=rl)
                        P.op("dve", lambda e, b=b: e.reciprocal(ls[b][:, 2, 4:8], ls[b][:, 2, 4:8]), reads=rl, writes=rl)
                        P.op("dve", lambda e, b=b, e3=e3: e.tensor_tensor(
                            e3, e3, ls[b][:, 2, 4:8].unsqueeze(1).broadcast_to([128, 3, 4]), ALU.mult), reads=rl, writes=rl)
                        P.op("pool", lambda e, b=b: e.tensor_tensor(
                            cat[b][:, 0:768].rearrange("p (h d) -> p h d", h=12), ot[b][:].rearrange("p (h d) -> p h d", h=12),
                            ls[b][:, 1, :].unsqueeze(2).broadcast_to([128, 12, 64]), ALU.mult),
                            reads=rl + [R(tag, "ot", b)], writes=[R(tag, "cat", b)])
                    else:
                        P.dma("sp", ot[b][:], OF[rows, :], writes=[R(tag, "ot", b)])
                        P.dma("sp", ot2[b][:], OB[rows, :], writes=[R(tag, "ot2", b)])
                        P.dma("sp", szt[b][:], SZ[rows, :], writes=[R(tag, "szt", b)])
                        P.op("pool", lambda e, b=b: e.tensor_tensor(ot[b][:], ot[b][:], ot2[b][:], ALU.add),
                             reads=[R(tag, "ot", b), R(tag, "ot2", b)], writes=[R(tag, "ot", b)])
                        for h in range(6):
                            P.op("act", lambda e, b=b, h=h: e.activation(ot2[b][:, h * 128:(h + 1) * 128], ot[b][:, h * 128:(h + 1) * 128],
                                                                         AF.Square, accum_out=os_[b][:, 0, h:h + 1]),
                                 reads=[R(tag, "ot", b)], writes=[R(tag, "ot2", b), R(tag, "os", b)])
                        P.op("act", lambda e, b=b: e.activation(os_[b][:, 1, :], os_[b][:, 0, :], AF.Sqrt, bias=EPS, scale=1.0 / 128),
                             reads=[R(tag, "os", b)], writes=[R(tag, "os", b)])
                        P.op("dve", lambda e, b=b: e.reciprocal(os_[b][:, 1, :], os_[b][:, 1, :]), reads=[R(tag, "os", b)], writes=[R(tag, "os", b)])
                        o3 = ot[b][:].rearrange("p (h d) -> p h d", h=6)
                        P.op("dve", lambda e, b=b, o3=o3: e.tensor_tensor(o3, o3, os_[b][:, 1, :].unsqueeze(2).broadcast_to([128, 6, 128]), ALU.mult),
                             reads=[R(tag, "ot", b), R(tag, "os", b)], writes=[R(tag, "ot", b)])
                        P.op("pool", lambda e, b=b, o3=o3: e.tensor_tensor(o3, o3, onb[:].unsqueeze(1).broadcast_to([128, 6, 128]), ALU.mult),
                             reads=[R(tag, "ot", b), R(tag, "onb")], writes=[R(tag, "ot", b)])
                        P.op("pool", lambda e, b=b: e.tensor_tensor(cat[b][:, 0:768], ot[b][:], szt[b][:], ALU.mult),
                             reads=[R(tag, "ot", b), R(tag, "szt", b)], writes=[R(tag, "cat", b)])
                    for hd in range(4):
                        blk = hd // 2
                        pl = (hd % 2) * 64
                        lb = ih % 2
                        P.op("pe", lambda e, lb=lb, blk=blk, pl=pl, t=t: e.matmul(
                            pL[lb][:], qm_box[0][pl:pl + 64, blk, t * 128:(t + 1) * 128], kmT[pl:pl + 64, blk, :], start=True, stop=True),
                            reads=[R("QmT"), R("kmT")], writes=[R(tag, "pL", lb)])
                        P.op("dve", lambda e, lb=lb, b=b, hd=hd: e.tensor_reduce(ms[b][:, 0, hd:hd + 1], pL[lb][:], AX.X, ALU.max, negate=True),
                             reads=[R(tag, "pL", lb)], writes=[R(tag, "mnmx", b, hd)])
                        P.op("act", lambda e, lb=lb, b=b, hd=hd: e.activation(pm[b][:], pL[lb][:], AF.Exp, bias=ms[b][:, 0, hd:hd + 1],
                                                                             accum_out=ms[b][:, 1, hd:hd + 1]),
                             reads=[R(tag, "pL", lb), R(tag, "mnmx", b, hd)], writes=[R(tag, "pm", b), R(tag, "mden", b, hd)])
                        for c in range(2):
                            P.op("pe", lambda e, b=b, c=c: e.transpose(pT[0][:, c, :], pm[b][:, c * 128:(c + 1) * 128], identb[:]),
                                 reads=[R(tag, "pm", b), R("identb")], writes=[R(tag, "pT")])
                        P.op("dve", lambda e, b=b: e.tensor_copy(pmT[b][:], pT[0][:, 0:2, :]), reads=[R(tag, "pT")], writes=[R(tag, "pmT", b)])
                        for c in range(2):
                            P.op("pe", lambda e, b=b, c=c, hd=hd: e.matmul(pO[0][:, hd, :], pmT[b][:, c, :], vm[:, c, hd * 64:(hd + 1) * 64],
                                                                           start=(c == 0), stop=(c == 1)),
                                 reads=[R(tag, "pmT", b), R("vm")], writes=[R(tag, "pO", hd)])
                        ih += 1
                    rden = [R(tag, "mden", b, hd) for hd in range(4)]
                    P.op("dve", lambda e, b=b: e.reciprocal(ms[b][:, 1, :], ms[b][:, 1, :]), reads=rden, writes=rden)
                    P.op("dve", lambda e, b=b: e.tensor_tensor(
                        cat[b][:, 768:1024].rearrange("p (h d) -> p h d", h=4), pO[0][:],
                        ms[b][:, 1, :].unsqueeze(2).broadcast_to([128, 4, 64]), ALU.mult),
                        reads=rden + [R(tag, "pO", hd) for hd in range(4)], writes=[R(tag, "cat", b)])
                    for c in range(8):
                        P.op("pe", lambda e, b=b, c=c: e.transpose(pT[0][:, c, :], cat[b][:, c * 128:(c + 1) * 128], identb[:]),
                             reads=[R(tag, "cat", b), R("identb")], writes=[R(tag, "pT")])
                    P.op("act", lambda e, b=b: e.copy(catT[b][:], pT[0][:]), reads=[R(tag, "pT")], writes=[R(tag, "catT", b)])
                    yb = t % 2
                    for hf in range(2):
                        for c in range(8):
                            P.op("pe", lambda e, b=b, c=c, hf=hf, yb=yb: e.matmul(
                                pY[yb][:, hf * 512:(hf + 1) * 512], catT[b][:, c, :], wo[:, c, hf * 512:(hf + 1) * 512],
                                start=(c == 0), stop=(c == 7)),
                                reads=[R(tag, "catT", b), R(tag, "w")], writes=[R(tag, "pY", yb)])
                    post_norm_residual(P, tag, b, pY[yb][:], R(tag, "pY", yb), gpost[:], xr[b][:], R(tag, "xr", b),
                                       sqj, ssq2[b], rstd[b][:], tmp[b][:], xo[b][:], R(tag, "xo", b))
                    P.dma("pool", x_dst[rows, :], xo[b][:], reads=[R(tag, "xo", b)])
                P.emit()
            return P.nops

        def phase_D(li):
            tag = "D%d" % li
            P = Phase(ctx, tag)
            x_src = X1 if li == 0 else X3
            x_dst = X2 if li == 0 else y_out
            TS = 256
            with ExitStack() as es:
                sb = lambda n, s, d: es.enter_context(nc.sbuf_tensor(tag + n, list(s), d))
                ps = lambda n, s, d: es.enter_context(nc.psum_tensor(tag + n, list(s), d))
                wgu = sb("wgu", [128, 8, 2 * DFF], BF16)
                wd = sb("wd", [128, 22, D], BF16)
                stg = [sb("stg%d" % i, [128, 8, 256], F32) for i in range(1)]
                gpost = sb("gpost", [128, D], F32)
                P.dma("sp", gpost[:], norm_ffn_post[li].partition_broadcast(128), writes=[R(tag, "gpost")])
                load_weight(P, tag + "gu", wgu, ffn_wgu[li], 8, 2 * DFF, stg, gain=gpc[:, 2 + li, :], PW=256)
                wdv = ffn_wd[li].rearrange("(c p) n -> p c n", p=128)
                stv = stg[0][:].rearrange("p c n -> p (c n)").rearrange("p (c n) -> p c n", c=2)
                for i in range(11):
                    P.dma("sp", stv, wdv[:, 2 * i:2 * i + 2, :], writes=[R("stg", id(stg[0]))])
                    P.op("pool" if i % 2 else "dve", lambda e, i=i: e.tensor_copy(wd[:, 2 * i:2 * i + 2, :], stv),
                         reads=[R("stg", id(stg[0]))], writes=[R(tag, "wd")])
                xt = [sb("xt%d" % i, [128, D], F32) for i in range(2)]
                tmp = sb("tmp", [128, D], F32)
                sqj = sb("sqj", [128, D], BF16)
                ssq = [sb("ssq%d" % i, [128, 1], F32) for i in range(2)]
                rstd = [sb("rstd%d" % i, [128, 1], F32) for i in range(2)]
                ssq2 = [sb("ssq2%d" % i, [128, 2], F32) for i in range(2)]
                prstd = [sb("prstd%d" % i, [128, 1], F32) for i in range(2)]
                xn = [sb("xn%d" % i, [128, D], BF16) for i in range(2)]
                hT = [sb("hT%d" % i, [128, 8, TS], BF16) for i in range(2)]
                actT = sb("actT", [128, 22, TS], BF16)
                sg = [sb("sg%d" % i, [128, TS], BF16) for i in range(2)]
                pT = [ps("pT%d" % i, [128, 8, 128], BF16) for i in range(1)]
                pG = [ps("pG%d" % i, [128, 512], F32) for i in range(2)]
                pU = [ps("pU%d" % i, [128, 512], F32) for i in range(2)]
                pY = [ps("pY%d" % i, [128, D], F32) for i in range(1)]
                nsub = TS // 128
                for st in range(S // TS):
                    hb = st % 2
                    for s4 in range(nsub):
                        t = st * nsub + s4
                        b = t % 2
                        P.dma("sp", xt[b][:], x_src[t * 128:(t + 1) * 128, :], writes=[R(tag, "xt", b)])
                        rms_prep(P, tag, b, xt[b][:], sqj[:], ssq[b][:], rstd[b][:], xn[b][:], R(tag, "xt", b))
                        transpose8(P, tag, b, xn[b], pT[0], R(tag, "pT"), hT[hb][:, :, s4 * 128:(s4 + 1) * 128], R(tag, "hT", hb))
                    for j in range(22):
                        pb = j % 2
                        for c in range(8):
                            P.op("pe", lambda e, c=c, j=j, pb=pb, hb=hb: e.matmul(pG[pb][:, 0:TS], wgu[:, c, j * 128:(j + 1) * 128], hT[hb][:, c, :],
                                                                                  start=(c == 0), stop=(c == 7)),
                                 reads=[R(tag + "gu", "w"), R(tag, "hT", hb)], writes=[R(tag, "pG", pb)])
                        for c in range(8):
                            P.op("pe", lambda e, c=c, j=j, pb=pb, hb=hb: e.matmul(pU[pb][:, 0:TS], wgu[:, c, DFF + j * 128:DFF + (j + 1) * 128], hT[hb][:, c, :],
                                                                                  start=(c == 0), stop=(c == 7)),
                                 reads=[R(tag + "gu", "w"), R(tag, "hT", hb)], writes=[R(tag, "pU", pb)])
                        P.op("act", lambda e, pb=pb: e.activation(sg[pb][:], pG[pb][:, 0:TS], AF.Silu),
                             reads=[R(tag, "pG", pb)], writes=[R(tag, "sg", pb)])
                        P.op("dve", lambda e, pb=pb, j=j: e.tensor_tensor(actT[:, j, :], sg[pb][:], pU[pb][:, 0:TS], ALU.mult),
                             reads=[R(tag, "sg", pb), R(tag, "pU", pb)], writes=[R(tag, "actT", j)])
                    for s4 in range(nsub):
                        t = st * nsub + s4
                        b = t % 2
                        P.dma("sp", xt[b][:], x_src[t * 128:(t + 1) * 128, :], writes=[R(tag, "xt", b)])
                        for hf in range(2):
                            for j in range(22):
                                P.op("pe", lambda e, j=j, hf=hf, s4=s4: e.matmul(
                                    pY[0][:, hf * 512:(hf + 1) * 512], actT[:, j, s4 * 128:(s4 + 1) * 128], wd[:, j, hf * 512:(hf + 1) * 512],
                                    start=(j == 0), stop=(j == 21)),
                                    reads=[R(tag, "actT", j), R(tag, "wd")], writes=[R(tag, "pY", hf)])
                        post_norm_residual(P, tag, b, pY[0][:], [R(tag, "pY", 0), R(tag, "pY", 1)], gpost[:], xt[b][:], R(tag, "xt", b),
                                           sqj, ssq2[b], prstd[b][:], tmp[:], xt[b][:], R(tag, "xt", b))
                        P.dma("pool", x_dst[t * 128:(t + 1) * 128, :], xt[b][:], reads=[R(tag, "xt", b)])
                P.emit()
            return P.nops

        def phase_B1():
            tag = "B1"
            P = Phase(ctx, tag)
            NCH = 32
            NSL = 4
            with ExitStack() as es:
                sb = lambda n, s, d: es.enter_context(nc.sbuf_tensor(tag + n, list(s), d))
                ps = lambda n, s, d: es.enter_context(nc.psum_tensor(tag + n, list(s), d))
                bk = [ps("bk%d" % i, [128, 512], F32) for i in range(8)]
                rbk = lambda i, j: R(tag, "bk", i)
                for i in range(8):
                    P.excl.add(rbk(i, 0))
                Gall = sb("Gall", [128, NCH, 12], F32)
                Ball = sb("Ball", [128, NCH, 12], F32)
                GC = sb("GC", [128, NCH, 12], F32)
                NGC = sb("NGC", [128, NCH, 12], F32)
                EG = sb("EG", [128, NCH, 12], F32)
                NBEG = sb("NBEG", [128, NCH, 12], F32)
                NBt = sb("NBt", [128, NCH, 12], F32)
                GLb = sb("GLb", [128, NCH, 12], F32)
                EGL = sb("EGL", [128, NCH, 12], F32)
                EKD = sb("EKD", [128, NCH, 12], F32)
                cw = sb("cw", [128, 18, 5], F32)
                for n in range(NCH):
                    P.dma("sp", Gall[:, n, :], GD[n * 128:(n + 1) * 128, :], writes=[R(tag, "Gall")])
                    P.dma("sp", Ball[:, n, :], BETA[n * 128:(n + 1) * 128, :], writes=[R(tag, "Ball")])
                for bb in range(18):
                    P.dma("sp", cw[:, bb, :], dn_convT[bb * 128:(bb + 1) * 128, :], writes=[R(tag, "cw")])
                v3 = lambda ap: ap.rearrange("p (n c) -> p n c", c=12)
                rg = [R(tag, "gates")]
                allbk0 = [rbk(0, j) for j in range(4)]
                allbk1 = [rbk(1, j) for j in range(4)]
                P.op("pe", lambda e: e.matmul(bk[0][:, 0:384], cst[:, C_TRIL:C_TRIL + 128], Gall[:].rearrange("p n c -> p (n c)"), start=True, stop=True),
                     reads=[R("cst"), R(tag, "Gall")], writes=allbk0)
                P.op("pe", lambda e: e.matmul(bk[1][:, 0:384], cst[:, C_TRIU:C_TRIU + 128], Gall[:].rearrange("p n c -> p (n c)"), start=True, stop=True),
                     reads=[R("cst"), R(tag, "Gall")], writes=allbk1)
                P.op("act", lambda e: e.copy(GC[:, :, 0:6], v3(bk[0][:, 0:384])[:, :, 0:6]), reads=allbk0, writes=rg)
                P.op("act", lambda e: e.copy(GC[:, :, 6:12], v3(bk[1][:, 0:384])[:, :, 6:12]), reads=allbk1, writes=rg)
                P.op("pe", lambda e: e.matmul(bk[0][:, 0:384], cst[:, C_SELL:C_SELL + 128], GC[:].rearrange("p n c -> p (n c)"), start=True, stop=True),
                     reads=[R("cst")] + rg, writes=allbk0)
                P.op("pe", lambda e: e.matmul(bk[1][:, 0:384], cst[:, C_SELF:C_SELF + 128], GC[:].rearrange("p n c -> p (n c)"), start=True, stop=True),
                     reads=[R("cst")] + rg, writes=allbk1)
                P.op("act", lambda e: e.copy(GLb[:, :, 0:6], v3(bk[0][:, 0:384])[:, :, 0:6]), reads=allbk0, writes=rg)
                P.op("act", lambda e: e.copy(GLb[:, :, 6:12], v3(bk[1][:, 0:384])[:, :, 6:12]), reads=allbk1, writes=rg)
                P.op("dve", lambda e: e.tensor_scalar(NGC[:], GC[:], -1.0, None, ALU.mult), reads=rg, writes=rg)
                P.op("act", lambda e: e.activation(EG[:], GC[:], AF.Exp), reads=rg, writes=rg)
                P.op("dve", lambda e: e.scalar_tensor_tensor(NBEG[:], Ball[:], -1.0, EG[:], ALU.mult, ALU.mult), reads=rg + [R(tag, "Ball")], writes=rg)
                P.op("dve", lambda e: e.tensor_scalar(NBt[:], Ball[:], -1.0, None, ALU.mult), reads=[R(tag, "Ball")], writes=rg)
                P.op("act", lambda e: e.activation(EGL[:], GLb[:], AF.Exp), reads=rg, writes=rg)
                P.op("dve", lambda e: e.tensor_tensor(EKD[:], GLb[:], GC[:], ALU.subtract), reads=rg, writes=rg)
                P.op("act", lambda e: e.activation(EKD[:], EKD[:], AF.Exp), reads=rg, writes=rg)

                raw = [sb("raw%d" % i, [128, S + 4], BF16) for i in range(2)]
                acc = [sb("acc%d" % i, [128, 1024], F32) for i in range(2)]
                slu = [sb("slu%d" % i, [128, 1024], F32) for i in range(2)]
                sq = [sb("sq%d" % i, [128, 1024], BF16) for i in range(2)]
                rinv = [sb("rinv%d" % i, [128, 512], F32) for i in range(2)]
                kqT = sb("kqT", [128, NCH, 2, 128], BF16)
                vT = sb("vT", [128, S], BF16)
                ktok = sb("ktok", [128, NCH, 128], BF16)
                vtok = sb("vtok", [128, NCH, 128], BF16)
                for i in range(2):
                    P.op("pool", lambda e, i=i: e.memset(raw[i][:, 0:2], 0.0), writes=[R(tag, "rawpad", i)])
                    P.op("pool", lambda e, i=i: e.memset(raw[i][:, S + 2:S + 4], 0.0), writes=[R(tag, "rawpad", i)])
                dg2 = [[sb("dg2_%d_%d" % (d, i), [128, 256], F32) for i in range(NSL)] for d in range(2)]
                dS_ = [[sb("dS_%d_%d" % (d, i), [128, 128], F32) for i in range(NSL)] for d in range(2)]
                dT_ = [[sb("dT_%d_%d" % (d, i), [128, 128], F32) for i in range(NSL)] for d in range(2)]
                Xb = [[[sb("X_%d_%d_%d" % (d, i, k), [128, 128], F32) for k in range(2)] for i in range(NSL)] for d in range(2)]
                Yb = [[[sb("Y_%d_%d_%d" % (d, i, k), [128, 128], F32) for k in range(2)] for i in range(NSL)] for d in range(2)]
                Qb = [[[sb("Q_%d_%d_%d" % (d, i, k), [128, 128], F32) for k in range(2)] for i in range(NSL)] for d in range(2)]
                inT = [[sb("inT_%d_%d" % (d, i), [128, 128], BF16) for i in range(NSL)] for d in range(2)]
                TTb = [[sb("TTb_%d_%d" % (d, i), [128, 128], BF16) for i in range(NSL)] for d in range(2)]
                BV = [[sb("BV_%d_%d" % (d, i), [128, 128], BF16) for i in range(NSL)] for d in range(2)]
                kdec = [[sb("kdec_%d_%d" % (d, i), [128, 128], BF16) for i in range(NSL)] for d in range(2)]
                S32 = [sb("S32_%d" % d, [128, 128], F32) for d in range(2)]
                Sbf = [[sb("Sbf_%d_%d" % (d, k), [128, 128], BF16) for k in range(2)] for d in range(2)]
                Rt = [[sb("Rt_%d_%d" % (d, k), [128, 128], BF16) for k in range(2)] for d in range(2)]
                vnb = [[sb("vnb_%d_%d" % (d, k), [128, 128], BF16) for k in range(2)] for d in range(2)]
                oq = [[sb("oq_%d_%d" % (d, k), [128, 128], F32) for k in range(2)] for d in range(2)]
                oo = [[sb("oo_%d_%d" % (d, k), [128, 128], F32) for k in range(2)] for d in range(2)]
                dblc = [0, 0]

                def dbl_region(d, j):
                    return bk[3 + d][:, j * 128:(j + 1) * 128], rbk(3 + d, j)

                def head_prep(h):
                    nraw = [0]
                    for kind in range(3):
                        blk = kind * 6 + h
                        rb = (h * 3 + kind) % 2
                        P.dma("sp", raw[rb][:, 2:S + 2], QKVT[blk * 128:(blk + 1) * 128, :], writes=[R(tag, "raw", rb)])
                        rr = [R(tag, "raw", rb), R(tag, "rawpad", rb), R(tag, "cw")]
                        for pc in range(4):
                            t0 = pc * 1024
                            ab = pc % 2
                            P.op("dve", lambda e, rb=rb, ab=ab, t0=t0, blk=blk: e.tensor_scalar(
                                acc[ab][:], raw[rb][:, t0:t0 + 1024], cw[:, blk, 0:1], None, ALU.mult),
                                reads=rr, writes=[R(tag, "acc", ab)])
                            for j in range(1, 5):
                                P.op("dve", lambda e, rb=rb, ab=ab, t0=t0, blk=blk, j=j: e.scalar_tensor_tensor(
                                    acc[ab][:], raw[rb][:, t0 + j:t0 + j + 1024], cw[:, blk, j:j + 1], acc[ab][:], ALU.mult, ALU.add),
                                    reads=rr + [R(tag, "acc", ab)], writes=[R(tag, "acc", ab)])
                            if kind == 2:
                                P.op("act", lambda e, ab=ab, t0=t0: e.activation(vT[:, t0:t0 + 1024], acc[ab][:], AF.Silu),
                                     reads=[R(tag, "acc", ab)], writes=[R(tag, "vT")])
                                continue
                            P.op("act", lambda e, ab=ab: e.activation(slu[ab][:], acc[ab][:], AF.Silu),
                                 reads=[R(tag, "acc", ab)], writes=[R(tag, "slu", ab)])
                            P.op("pool", lambda e, ab=ab: e.tensor_tensor(sq[ab][:], slu[ab][:], slu[ab][:], ALU.mult),
                                 reads=[R(tag, "slu", ab)], writes=[R(tag, "sq", ab)])
                            for hf in range(2):
                                wr = [rbk(3 + hf, j) for j in range(4)]
                                P.op("pe", lambda e, ab=ab, hf=hf: e.matmul(bk[3 + hf][:], onesb[:], sq[ab][:, hf * 512:(hf + 1) * 512], start=True, stop=True),
                                     reads=[R(tag, "sq", ab), R("onesb")], writes=wr)
                                P.op("act", lambda e, hf=hf: e.activation(rinv[hf][:], bk[3 + hf][:], AF.Sqrt, bias=EPS, scale=1.0),
                                     reads=wr, writes=[R(tag, "rinv", hf)])
                                P.op("dve", lambda e, hf=hf: e.reciprocal(rinv[hf][:], rinv[hf][:]), reads=[R(tag, "rinv", hf)], writes=[R(tag, "rinv", hf)])
                                n0 = pc * 8 + hf * 4
                                kidx = 1 if kind == 0 else 0
                                scl = (128.0 ** -0.5) if kind == 0 else 1.0
                                P.op("dve", lambda e, ab=ab, hf=hf, n0=n0, kidx=kidx, scl=scl: e.scalar_tensor_tensor(
                                    kqT[:, n0:n0 + 4, kidx, :], slu[ab][:, hf * 512:(hf + 1) * 512].rearrange("p (n t) -> p n t", n=4), scl,
                                    rinv[hf][:].rearrange("p (n t) -> p n t", n=4), ALU.mult, ALU.mult),
                                    reads=[R(tag, "slu", ab), R(tag, "rinv", hf)], writes=[R(tag, "kqT")])
                    for which in range(2):
                        for n0 in range(0, NCH, 4):
                            bi = 5 + (n0 // 4) % 2
                            pv = bk[bi][:].bitcast(BF16)[:, 0:512].rearrange("p (n t) -> p n t", n=4)
                            wr = [rbk(bi, j) for j in range(4)]
                            for k4 in range(4):
                                n = n0 + k4
                                src = kqT[:, n, 0, :] if which == 0 else vT[:, n * 128:(n + 1) * 128]
                                P.op("pe", lambda e, pv=pv, k4=k4, src=src: e.transpose(pv[:, k4, :], src, identb[:]),
                                     reads=[R(tag, "kqT"), R(tag, "vT"), R("identb")], writes=wr)
                            dst = (ktok if which == 0 else vtok)[:, n0:n0 + 4, :]
                            P.op("act" if (n0 // 4) % 2 else "dve",
                                 (lambda e, dst=dst, pv=pv: e.copy(dst, pv)) if (n0 // 4) % 2 else (lambda e, dst=dst, pv=pv: e.tensor_copy(dst, pv)),
                                 reads=wr, writes=[R(tag, "tok", which)])

                def prep(h, d, n, sl):
                    col = d * 6 + h
                    gc = GC[:, n, col:col + 1]
                    ngc = NGC[:, n, col:col + 1]
                    rs = lambda nm: R(tag, nm, d, sl)
                    P.op("pool", lambda e: e.tensor_scalar(dg2[d][sl][:], cst[:, C_ID2:C_ID2 + 256], gc, 1.0, ALU.mult, ALU.mult),
                         reads=rg + [R("cst")], writes=[rs("dg2")])
                    yield
                    if DBG.get("cut", 99) < 2:
                        return
                    bmr = bk[1 + d][:, (sl % 2) * 256:(sl % 2) * 256 + 256]
                    rbm = [rbk(1 + d, (sl % 2) * 2), rbk(1 + d, (sl % 2) * 2 + 1)]
                    mc = C_MF if d == 0 else C_MB
                    P.op("pe", lambda e: e.matmul(bmr, ones32[:], dg2[d][sl][:], start=True, stop=False),
                         reads=[rs("dg2"), R("ones32")], writes=rbm)
                    P.op("pe", lambda e: e.matmul(bmr, cst[:, C_ID:C_ID + 128], cst[:, mc:mc + 256], start=False, stop=True),
                         reads=[R("cst")], writes=rbm)
                    yield
                    if DBG.get("cut", 99) < 3:
                        return
                    P.op("act", lambda e: e.activation(dS_[d][sl][:], bmr[:, 0:128], AF.Exp, bias=gc, scale=-1.0),
                         reads=rbm + rg, writes=[rs("dS")])
                    yield
                    P.op("act", lambda e: e.activation(dT_[d][sl][:], bmr[:, 128:256], AF.Exp, bias=ngc, scale=1.0),
                         reads=rbm + rg, writes=[rs("dT")])
                    yield
                    if DBG.get("cut", 99) < 4:
                        return
                    kk = bk[0][:, (sl % 2) * 256:(sl % 2) * 256 + 256]
                    rkk = [rbk(0, (sl % 2) * 2), rbk(0, (sl % 2) * 2 + 1)]
                    P.op("pe", lambda e: e.matmul(kk, kqT[:, n, 0, :], kqT[:, n, :, :].rearrange("p a t -> p (a t)"), start=True, stop=True),
                         reads=[R(tag, "kqT")], writes=rkk)
                    X = Xb[d][sl]
                    Y = Yb[d][sl]
                    Q = Qb[d][sl]
                    P.op("dve", lambda e: e.scalar_tensor_tensor(X[0][:], kk[:, 0:128], NBt[:, n, col:col + 1], dS_[d][sl][:], ALU.mult, ALU.mult),
                         reads=rkk + rg + [rs("dS")], writes=[rs("X0")])
                    P.op("dve", lambda e: e.tensor_tensor(inT[d][sl][:], kk[:, 128:256], dT_[d][sl][:], ALU.mult),
                         reads=rkk + [rs("dT")], writes=[rs("inT")])
                    yield
                    if DBG.get("cut", 99) < 5:
                        return
                    pr, rpr = dbl_region(d, 2 * (sl % 2))
                    P.op("pe", lambda e: e.matmul(pr, X[0][:], cst[:, C_ID:C_ID + 128], start=True, stop=True), reads=[rs("X0"), R("cst")], writes=[rpr])
                    yield
                    P.op("act", lambda e: e.copy(Y[0][:], pr), reads=[rpr], writes=[rs("Y0")])
                    yield
                    P.op("dve", lambda e: e.tensor_tensor(Q[0][:], pr, cst[:, C_ID:C_ID + 128], ALU.add), reads=[rpr, R("cst")], writes=[rs("Q0")])
                    yield
                    if DBG.get("cut", 99) < 6:
                        return
                    NL = 6
                    for l in range(NL):
                        a, b_ = l % 2, (l + 1) % 2
                        rX, rY, rQ = rs("X%d" % a), rs("Y%d" % a), rs("Q%d" % a)
                        rX2, rY2, rQ2 = rs("X%d" % b_), rs("Y%d" % b_), rs("Q%d" % b_)
                        px, rpx = dbl_region(d, 2 * (sl % 2))
                        P.op("pe", lambda e: e.matmul(px, Y[a][:], X[a][:], start=True, stop=True), reads=[rX, rY], writes=[rpx])
                        yield
                        if l < NL - 1:
                            py, rpy = dbl_region(d, 2 * (sl % 2) + 1)
                            P.op("pe", lambda e: e.matmul(py, X[a][:], Y[a][:], start=True, stop=True), reads=[rX, rY], writes=[rpy])
                            yield
                        P.op("act", lambda e: e.copy(X[b_][:], px), reads=[rpx], writes=[rX2])
                        yield
                        if l < NL - 1:
                            P.op("dve", lambda e: e.tensor_copy(Y[b_][:], py), reads=[rpy], writes=[rY2])
                            yield
                        pq, rpq = dbl_region(d, 2 * (sl % 2))
                        P.op("pe", lambda e: e.matmul(pq, X[b_][:], Q[a][:], start=True, stop=True), reads=[rX2, rQ], writes=[rpq])
                        yield
                        if l < NL - 1:
                            P.op("dve", lambda e: e.tensor_tensor(Q[b_][:], pq, Q[a][:], ALU.add), reads=[rpq, rQ], writes=[rQ2])
                            yield
                        else:
                            P.op("dve", lambda e: e.tensor_tensor(TTb[d][sl][:], pq, Q[a][:], ALU.add), reads=[rpq, rQ], writes=[rs("TTb")])
                            yield
                    if DBG.get("cut", 99) < 7:
                        return
                    P.op("pool", lambda e: e.tensor_scalar(BV[d][sl][:], vtok[:, n, :], Ball[:, n, col:col + 1], 1.0, ALU.mult, ALU.mult),
                         reads=[R(tag, "tok", 1), R(tag, "Ball")], writes=[rs("BV")])
                    yield
                    P.op("pool", lambda e: e.tensor_scalar(kdec[d][sl][:], ktok[:, n, :], EKD[:, n, col:col + 1], 1.0, ALU.mult, ALU.mult),
                         reads=[R(tag, "tok", 0)] + rg, writes=[rs("kdec")])
                    yield

                def scan(h, d, n, sl, s):
                    col = d * 6 + h
                    rs = lambda nm: R(tag, nm, d, sl)
                    cur, nxt = s % 2, (s + 1) % 2
                    k2 = s % 2
                    sbank = bk[5 + d]
                    r_kS, r_vn, r_dS, r_qS = [rbk(5 + d, j) for j in range(4)]
                    pkS, pvn, pdS, pqS = [sbank[:, j * 128:(j + 1) * 128] for j in range(4)]
                    oi = bk[7][:, (d * 2 + k2) * 128:(d * 2 + k2 + 1) * 128]
                    r_oi = rbk(7, d * 2 + k2)
                    rSb = R(tag, "Sbf", d, cur)
                    rSn = R(tag, "Sbf", d, nxt)
                    P.op("pe", lambda e: e.matmul(pkS, kqT[:, n, 0, :], Sbf[d][cur][:], start=True, stop=True),
                         reads=[R(tag, "kqT"), rSb], writes=[r_kS])
                    yield
                    P.op("dve", lambda e: e.scalar_tensor_tensor(Rt[d][k2][:], pkS, NBEG[:, n, col:col + 1], BV[d][sl][:], ALU.mult, ALU.add),
                         reads=[r_kS, rs("BV")] + rg, writes=[R(tag, "Rt", d, k2)])
                    yield
                    P.op("pe", lambda e: e.matmul(pvn, TTb[d][sl][:], Rt[d][k2][:], start=True, stop=True),
                         reads=[rs("TTb"), R(tag, "Rt", d, k2)], writes=[r_vn])
                    yield
                    P.op("act", lambda e: e.copy(vnb[d][k2][:], pvn), reads=[r_vn], writes=[R(tag, "vnb", d, k2)])
                    yield
                    P.op("pe", lambda e: e.matmul(pdS, kdec[d][sl][:], vnb[d][k2][:], start=True, stop=True),
                         reads=[rs("kdec"), R(tag, "vnb", d, k2)], writes=[r_dS])
                    yield
                    P.op("dve", lambda e: e.scalar_tensor_tensor(Sbf[d][nxt][:], S32[d][:], EGL[:, n, col:col + 1], pdS, ALU.mult, ALU.add),
                         reads=[R(tag, "S32", d), r_dS] + rg, writes=[rSn])
                    yield
                    P.op("dve", lambda e: e.scalar_tensor_tensor(S32[d][:], S32[d][:], EGL[:, n, col:col + 1], pdS, ALU.mult, ALU.add),
                         reads=[R(tag, "S32", d), r_dS] + rg, writes=[R(tag, "S32", d)])
                    yield
                    P.op("pe", lambda e: e.matmul(pqS, kqT[:, n, 1, :], Sbf[d][cur][:], start=True, stop=True),
                         reads=[R(tag, "kqT"), rSb], writes=[r_qS])
                    yield
                    P.op("pe", lambda e: e.matmul(oi, inT[d][sl][:], vnb[d][k2][:], start=True, stop=True),
                         reads=[rs("inT"), R(tag, "vnb", d, k2)], writes=[r_oi])
                    yield
                    P.op("act", lambda e: e.activation(oq[d][k2][:], pqS, AF.Copy, scale=EG[:, n, col:col + 1]),
                         reads=[r_qS] + rg, writes=[R(tag, "oq", d, k2)])
                    yield
                    P.op("dve", lambda e: e.tensor_tensor(oo[d][k2][:], oq[d][k2][:], oi, ALU.add),
                         reads=[R(tag, "oq", d, k2), r_oi], writes=[R(tag, "oo", d, k2)])
                    yield
                    dst = (OF if d == 0 else OB)[n * 128:(n + 1) * 128, h * 128:(h + 1) * 128]
                    P.dma("pool", dst, oo[d][k2][:], reads=[R(tag, "oo", d, k2)])
                    yield

                def run_rr(gens):
                    gens = list(gens)
                    while gens:
                        for g in list(gens):
                            try:
                                next(g)
                            except StopIteration:
                                gens.remove(g)

                def scan2(h, d, steps):
                    for s in steps:
                        n = s if d == 0 else NCH - 1 - s
                        for _ in scan(h, d, n, s % NSL, s):
                            yield

                def nof(d, s):
                    return s if d == 0 else NCH - 1 - s

                for h in range(DBG["b1_heads"]):
                    if DBG["b1_stage"] >= 1:
                        head_prep(h)
                    if DBG["b1_stage"] < 2:
                        continue
                    NCHR = DBG["b1_steps"]
                    for d in range(2):
                        P.op("pool", lambda e, d=d: e.memset(S32[d][:], 0.0), writes=[R(tag, "S32", d)])
                        P.op("pool", lambda e, d=d: e.memset(Sbf[d][0][:], 0.0), writes=[R(tag, "Sbf", d, 0)])
                    GRP = 2
                    for r0 in range(0, NCHR + GRP, GRP):
                        gens = []
                        for s in range(r0, min(r0 + GRP, NCHR)):
                            for d in range(2):
                                gens.append(prep(h, d, nof(d, s), s % NSL))
                        if r0 >= GRP and DBG["b1_stage"] >= 3:
                            for d in range(2):
                                gens.append(scan2(h, d, range(r0 - GRP, min(r0, NCHR))))
                        run_rr(gens)
                P.emit()
            return P.nops

        nops = {}
        with nc.sbuf_tensor("QmT0", [128, 2, S], BF16) as qm0:
            qm_box[0] = qm0
            if want("A0"):
                nops["A0"] = phase_A(0)
            if want("B0"):
                nops["B0"] = phase_B0()
            if want("C0"):
                nops["C0"] = phase_C(0)
        if want("D0"):
            nops["D0"] = phase_D(0)
        with nc.sbuf_tensor("QmT1", [128, 2, S], BF16) as qm1:
            qm_box[0] = qm1
            if want("A1"):
                nops["A1"] = phase_A(1)
            if want("B1"):
                nops["B1"] = phase_B1()
            if want("C1"):
                nops["C1"] = phase_C(1)
        if want("D1"):
            nops["D1"] = phase_D(1)
    return nc, nops


def make_in_maps(inputs):
    f = lambda a: np.ascontiguousarray(np.asarray(a, dtype=np.float32))
    consts = make_consts()
    biasg = gather_bias(f(inputs["rel_bias"]))

    def pc(v):
        return f(v).reshape(8, 128).T
    gains = np.stack([pc(inputs["norm_mix_pre"][0]), pc(inputs["norm_mix_pre"][1]),
                      pc(inputs["norm_ffn_pre"][0]), pc(inputs["norm_ffn_pre"][1]),
                      pc(inputs["mem_norm"][0]), pc(inputs["mem_norm"][1])], axis=1)
    shared = {
        "biasg": biasg, "consts": consts, "gains_pc": f(gains),
        "att_w_in": f(inputs["att_w_in"][0]), "att_w_out": f(inputs["att_w_out"][0]),
        "dn_w_in": f(inputs["dn_w_in"][0]), "dn_convT": f(np.asarray(inputs["dn_conv"][0]).T),
        "dn_a_log": f(inputs["dn_a_log"][0]).reshape(12), "dn_dt_bias": f(inputs["dn_dt_bias"][0]).reshape(12),
        "dn_out_norm": f(inputs["dn_out_norm"][0]), "dn_w_out": f(inputs["dn_w_out"][0]),
        "mem_w_kv": f(inputs["mem_w_kv"]), "norm_mix_post": f(inputs["norm_mix_post"]),
        "norm_ffn_post": f(inputs["norm_ffn_post"]), "ffn_wgu": f(inputs["ffn_w_gate_up"]),
        "ffn_wd": f(inputs["ffn_w_down"]),
    }
    x = f(inputs["x"])
    mem = f(inputs["mem"])
    maps = []
    for b in range(8):
        m = dict(shared)
        m["x"] = x[b]
        m["mem"] = mem[b]
        maps.append(m)
    return maps


def kernel(**inputs):
    nc, _ = build()
    maps = make_in_maps(inputs)
    res = run_bass_kernel_spmd(nc, maps, core_ids=list(range(8)))
    return np.stack([np.asarray(r["y"], dtype=np.float32) for r in res.results], axis=0)
```

```python
import math
import types
from contextlib import ExitStack
import numpy as np
import concourse.bass as bass
import concourse.mybir as mybir
from concourse.bass_utils import run_bass_kernel_spmd

F32 = mybir.dt.float32
F32R = mybir.dt.float32r
BF16 = mybir.dt.bfloat16
AF = mybir.ActivationFunctionType
ALU = mybir.AluOpType
AX = mybir.AxisListType

ENGS = ("pe", "act", "dve", "pool", "sp")
N_DMA_SLOTS = 12

S = 4096
D = 1024
NT = S // 128
DFF = 2816
EPS = 1e-6
BIG = 1.0e30
GROUPS = ((128, 1), (512, 4), (2048, 16))
PSUM_NAMES = ("pT", "pF", "pV", "pL", "pO", "pY", "pG", "pU", "bk")
DBG = {"b1_heads": 6, "b1_steps": 32, "b1_stage": 9}


class Reg:
    __slots__ = ("name", "w", "r", "key")

    def __init__(self, name="", key=()):
        self.name = name
        self.key = key
        self.w = None
        self.r = []


class RegMap(dict):
    def __call__(self, *key):
        r = self.get(key)
        if r is None:
            r = Reg(str(key), key)
            self[key] = r
        return r


class Ctx:
    def __init__(self, nc):
        self.nc = nc
        self.sem = {}
        self.cnt = {}
        for e in ("pe", "act", "dve", "pool"):
            self.sem[e] = nc.alloc_semaphore("c_" + e)
            self.cnt[e] = 0
        self.dq = ("sp", "act", "pool")
        self.dslots = {}
        for q in self.dq:
            self.dslots[q] = []
            for i in range(N_DMA_SLOTS):
                k = "d_%s_%d" % (q, i)
                self.sem[k] = nc.alloc_semaphore(k)
                self.cnt[k] = 0
                self.dslots[q].append(k)
        self.dnext = {q: 0 for q in self.dq}
        self.known = {e: {} for e in ENGS}
        self.snap = {}


def _freeze(fn):
    if fn.__closure__ is None:
        return fn
    cells = []
    for c in fn.__closure__:
        try:
            cells.append(types.CellType(c.cell_contents))
        except ValueError:
            cells.append(c)
    return types.FunctionType(fn.__code__, fn.__globals__, fn.__name__, fn.__defaults__, tuple(cells))


class Phase:
    def __init__(self, ctx, name="ph"):
        self.ctx = ctx
        self.name = name
        self.q = {e: [] for e in ENGS}
        self.nops = 0
        self.excl = set()

    def _waits(self, e, reads, writes):
        ctx = self.ctx
        deps = {}

        def add(tok):
            if tok is None:
                return
            k, v = tok
            if deps.get(k, 0) < v:
                deps[k] = v
        for r in reads:
            add(r.w)
        for w in writes:
            if w.w is not None and w.w[0] != e:
                add(w.w)
            for t in w.r:
                if t[0] != e:
                    add(t)
        known = ctx.known[e]
        waits = []
        for k, v in deps.items():
            if k == e and e == "pe":
                continue
            if known.get(k, 0) >= v:
                continue
            waits.append((k, v))
            known[k] = v
            sn = ctx.snap.get((k, v))
            if sn is not None:
                for k2, v2 in sn.items():
                    if known.get(k2, 0) < v2:
                        known[k2] = v2
        return waits

    def op(self, e, fn, reads=(), writes=()):
        ctx = self.ctx
        ex = [r for r in reads if (len(r.key) > 1 and r.key[1] in PSUM_NAMES) or r in self.excl]
        if ex:
            writes = list(writes) + ex
        waits = self._waits(e, reads, writes)
        ctx.cnt[e] += 1
        tok = (e, ctx.cnt[e])
        ctx.snap[tok] = dict(ctx.known[e])
        for r in reads:
            r.r.append(tok)
        for w in writes:
            w.w = tok
            w.r = []
        self.q[e].append((waits, _freeze(fn), (e, 1)))
        self.nops += 1
        return tok

    def dma(self, q, out, in_, reads=(), writes=(), **kw):
        ctx = self.ctx
        slots = ctx.dslots[q]
        k = slots[ctx.dnext[q] % len(slots)]
        ctx.dnext[q] += 1
        waits = self._waits(q, reads, writes)
        known = ctx.known[q]
        if ctx.cnt[k] > 0 and known.get(k, 0) < ctx.cnt[k]:
            waits.append((k, ctx.cnt[k]))
            known[k] = ctx.cnt[k]
        ctx.cnt[k] += 16
        tok = (k, ctx.cnt[k])
        for r in reads:
            r.r.append(tok)
        for w in writes:
            w.w = tok
            w.r = []

        def fn(eng, out=out, in_=in_, kw=kw):
            return eng.dma_start(out=out, in_=in_, **kw)
        self.q[q].append((waits, fn, (k, 16)))
        self.nops += 1
        return tok

    def emit(self):
        ctx = self.ctx
        nc = ctx.nc
        fin = []
        for q in ctx.dq:
            for k in ctx.dslots[q]:
                if ctx.cnt[k] > 0 and ctx.known["sp"].get(k, 0) < ctx.cnt[k]:
                    fin.append((k, ctx.cnt[k]))
                    ctx.known["sp"][k] = ctx.cnt[k]
        qs = self.q
        sem = ctx.sem

        def body(e):
            def f(eng):
                for waits, fn, inc in qs[e]:
                    for k, v in waits:
                        eng.wait_ge(sem[k], v)
                    ins = fn(eng)
                    ins.then_inc(sem[inc[0]], inc[1])
                if e == "sp":
                    for k, v in fin:
                        eng.wait_ge(sem[k], v)
            return f

        with nc.Block() as block:
            block.tensor(body("pe"))
            block.scalar(body("act"))
            block.vector(body("dve"))
            block.gpsimd(body("pool"))
            block.sync(body("sp"))
        full = {}
        for e in ("pe", "act", "dve", "pool"):
            full[e] = ctx.cnt[e]
        for q in ctx.dq:
            for k in ctx.dslots[q]:
                full[k] = ctx.cnt[k]
        for e in ENGS:
            ctx.known[e] = dict(full)
        ctx.snap = {}


def clear_sems(ctx):
    nc = ctx.nc
    sems = list(ctx.sem.values())
    with nc.Block() as block:
        def f(eng):
            for s in sems:
                eng.sem_clear(s)
        block.gpsimd(f)


def _t5_bucket(rel):
    half = 16
    max_exact = 8
    n = np.abs(rel)
    large = max_exact + (np.log(np.maximum(n, 1) / max_exact) / math.log(1024 / max_exact)
                         * (half - max_exact)).astype(np.int64)
    large = np.minimum(large, half - 1)
    return ((rel > 0) * half + np.where(n < max_exact, n, large)).astype(np.int32)


C_ID = 0
C_MNEG = 128
C_MF = 384
C_MB = 640
C_TRIL = 896
C_TRIU = 1024
C_SELL = 1152
C_SELF = 1280
C_ID2 = 1408
NCONST = 1664


def make_consts():
    c = np.zeros((128, NCONST), np.float32)
    a = np.arange(128)[:, None]
    b = np.arange(128)[None, :]
    c[:, C_ID:C_ID + 128] = np.eye(128)
    col = np.arange(256)[None, :]
    rel = col - 64 - a
    c[:, C_MNEG:C_MNEG + 256] = np.where(np.abs(rel) <= 64, 0.0, -BIG)
    c[:, C_MF:C_MF + 128] = np.where(b >= a, BIG, 0.0)
    c[:, C_MF + 128:C_MF + 256] = np.where(b < a, -BIG, 0.0)
    c[:, C_MB:C_MB + 128] = np.where(b <= a, BIG, 0.0)
    c[:, C_MB + 128:C_MB + 256] = np.where(b > a, -BIG, 0.0)
    c[:, C_TRIL:C_TRIL + 128] = (a <= b)
    c[:, C_TRIU:C_TRIU + 128] = (a >= b)
    c[127, C_SELL:C_SELL + 128] = 1.0
    c[0, C_SELF:C_SELF + 128] = 1.0
    c[:, C_ID2:C_ID2 + 128] = np.eye(128)
    c[:, C_ID2 + 128:C_ID2 + 256] = np.eye(128)
    return c


def gather_bias(rel_bias):
    out = np.zeros((12, 128, 256), np.float32)
    a = np.arange(128)[:, None]
    col = np.arange(256)[None, :]
    rel = col - 64 - a
    inb = np.abs(rel) <= 64
    relc = np.clip(rel, -64, 64)
    for gi, (window, dil) in enumerate(GROUPS):
        bk = _t5_bucket(relc * dil)
        for hh in range(4):
            h = gi * 4 + hh
            out[h] = np.where(inb, rel_bias[bk, h], 0.0)
    return out


def build(debug=False, phases=None):
    nc = bass.Bass("TRN2", target_bir_lowering=False)

    def din(name, shape, dt=F32):
        return nc.dram_tensor(name, list(shape), dt, kind="ExternalInput").ap()

    def dscr(name, shape, dt):
        kind = "ExternalOutput" if (debug and (DBG.get("outs") is None or name in DBG["outs"])) else "Internal"
        return nc.dram_tensor(name, list(shape), dt, kind=kind).ap()

    def want(p):
        return phases is None or p in phases

    x_in = din("x", [S, D])
    mem_in = din("mem", [256, D])
    biasg = din("biasg", [12, 128, 256])
    consts_in = din("consts", [128, NCONST])
    gains_pc = din("gains_pc", [128, 6, 8])
    att_w_in = din("att_w_in", [D, 2560])
    att_w_out = din("att_w_out", [D, D])
    dn_w_in = din("dn_w_in", [D, 3352])
    dn_convT = din("dn_convT", [2304, 5])
    dn_a_log = din("dn_a_log", [12])
    dn_dt_bias = din("dn_dt_bias", [12])
    dn_out_norm = din("dn_out_norm", [128])
    dn_w_out = din("dn_w_out", [D, D])
    mem_w_kv = din("mem_w_kv", [2, D, 512])
    norm_mix_post = din("norm_mix_post", [2, D])
    norm_ffn_post = din("norm_ffn_post", [2, D])
    ffn_wgu = din("ffn_wgu", [2, D, 2 * DFF])
    ffn_wd = din("ffn_wd", [2, DFF, D])
    y_out = nc.dram_tensor("y", [S, D], F32, kind="ExternalOutput").ap()

    QTd = dscr("QTd", [6, 128, S], BF16)
    KTd = dscr("KTd", [6, 128, S], BF16)
    V0 = dscr("V0", [S, 768], BF16)
    OATT = dscr("OATT", [S, 768], F32)
    LSE = dscr("LSE", [S, 12], F32)
    X1 = dscr("X1", [S, D], F32)
    X2 = dscr("X2", [S, D], F32)
    X3 = dscr("X3", [S, D], F32)
    QKVT = dscr("QKVT", [2304, S], BF16)
    SZ = dscr("SZ", [S, 768], BF16)
    GD = dscr("GD", [S, 12], F32)
    BETA = dscr("BETA", [S, 12], F32)
    OF = dscr("OF", [S, 768], F32)
    OB = dscr("OB", [S, 768], F32)

    ctx = Ctx(nc)
    clear_sems(ctx)
    R = RegMap()

    with ExitStack() as gs:
        def gsb(name, shape, dt):
            return gs.enter_context(nc.sbuf_tensor(name, list(shape), dt))

        cst = gsb("cst", [128, NCONST], F32)
        identb = gsb("identb", [128, 128], BF16)
        onesb = gsb("onesb", [128, 128], BF16)
        ones32 = gsb("ones32", [128, 128], F32)
        gpc = gsb("gpc", [128, 6, 8], F32)
        qm_box = [None]
        kmT = gsb("kmT", [128, 2, 256], BF16)
        vm = gsb("vm", [128, 2, 256], BF16)
        ident32 = cst[:, C_ID:C_ID + 128]

        P = Phase(ctx, "setup")
        P.dma("sp", cst[:], consts_in, writes=[R("cst")])
        P.dma("sp", gpc[:], gains_pc, writes=[R("gpc")])
        P.op("dve", lambda e: e.tensor_copy(identb[:], cst[:, C_ID:C_ID + 128]), reads=[R("cst")], writes=[R("identb")])
        P.op("pool", lambda e: e.memset(onesb[:], 1.0), writes=[R("onesb")])
        P.op("pool", lambda e: e.memset(ones32[:], 1.0), writes=[R("ones32")])
        P.emit()

        def load_weight(P, tag, dst, src, KC, N, stg, gain=None, PW=512, engs=("dve", "pool")):
            srcv = src.rearrange("(c p) n -> p c n", p=128)
            i = 0
            for c0 in range(0, N, PW):
                w = min(PW, N - c0)
                b = i % len(stg)
                P.dma("sp", stg[b][:, :, 0:w], srcv[:, :, c0:c0 + w], writes=[R("stg", id(stg[b]))])
                eng = engs[i % len(engs)]
                if gain is not None:
                    P.op(eng, lambda e, b=b, c0=c0, w=w: e.tensor_tensor(
                        dst[:, :, c0:c0 + w], stg[b][:, :, 0:w],
                        gain.unsqueeze(2).broadcast_to([128, KC, w]), ALU.mult),
                        reads=[R("stg", id(stg[b])), R("gpc")], writes=[R(tag, "w")])
                else:
                    P.op(eng, lambda e, b=b, c0=c0, w=w: e.tensor_copy(dst[:, :, c0:c0 + w], stg[b][:, :, 0:w]),
                         reads=[R("stg", id(stg[b]))], writes=[R(tag, "w")])
                i += 1

        def rms_prep(P, tag, b, xt, sqj, ssq, rstd, xn, r_x):
            P.op("act", lambda e: e.activation(sqj, xt, AF.Square, accum_out=ssq),
                 reads=[r_x], writes=[R(tag, "sqj"), R(tag, "ssq", b)])
            P.op("act", lambda e: e.activation(rstd, ssq, AF.Sqrt, bias=EPS, scale=1.0 / D),
                 reads=[R(tag, "ssq", b)], writes=[R(tag, "rstd", b)])
            P.op("dve", lambda e: e.reciprocal(rstd, rstd), reads=[R(tag, "rstd", b)], writes=[R(tag, "rstd", b)])
            P.op("dve", lambda e: e.tensor_scalar(xn, xt, rstd, None, ALU.mult),
                 reads=[r_x, R(tag, "rstd", b)], writes=[R(tag, "xn", b)])

        def transpose8(P, tag, b, xn, pT, r_pT, dst, r_dst, evac="act"):
            for c in range(8):
                P.op("pe", lambda e, c=c: e.transpose(pT[:, c, :], xn[:, c * 128:(c + 1) * 128], identb[:]),
                     reads=[R(tag, "xn", b), R("identb")], writes=[r_pT])
            if evac == "act":
                P.op("act", lambda e: e.copy(dst, pT[:]), reads=[r_pT], writes=[r_dst])
            else:
                P.op("dve", lambda e: e.tensor_copy(dst, pT[:]), reads=[r_pT], writes=[r_dst])

        def post_norm_residual(P, tag, b, pY, r_pY, gpost, xres, r_xres, sqj, ssq2, rstd, tmp, xo, r_xo):
            rpy = r_pY if isinstance(r_pY, list) else [r_pY]
            for hf in range(2):
                P.op("act", lambda e, hf=hf: e.activation(sqj[:, hf * 512:(hf + 1) * 512], pY[:, hf * 512:(hf + 1) * 512],
                                                          AF.Square, accum_out=ssq2[:, hf:hf + 1]),
                     reads=rpy, writes=[R(tag, "psqj"), R(tag, "pssq", b, hf)])
            P.op("dve", lambda e: e.tensor_tensor(rstd, ssq2[:, 0:1], ssq2[:, 1:2], ALU.add),
                 reads=[R(tag, "pssq", b, 0), R(tag, "pssq", b, 1)], writes=[R(tag, "prstd", b)])
            P.op("act", lambda e: e.activation(rstd, rstd, AF.Sqrt, bias=EPS, scale=1.0 / D),
                 reads=[R(tag, "prstd", b)], writes=[R(tag, "prstd", b)])
            P.op("dve", lambda e: e.reciprocal(rstd, rstd), reads=[R(tag, "prstd", b)], writes=[R(tag, "prstd", b)])
            P.op("dve", lambda e: e.scalar_tensor_tensor(tmp, pY, rstd, gpost, ALU.mult, ALU.mult),
                 reads=rpy + [R(tag, "prstd", b), R(tag, "gpost")], writes=[R(tag, "ptmp")])
            P.op("pool", lambda e: e.tensor_tensor(xo, tmp, xres, ALU.add),
                 reads=[R(tag, "ptmp"), r_xres], writes=[r_xo])

        def mem_kv(P, es, li, pbank, stg, mt, sqj, ssq, rstd, xn):
            tag = "mkv%d" % li
            sb = lambda n, s, d: es.enter_context(nc.sbuf_tensor(tag + n, list(s), d))
            wkv = sb("w", [128, 8, 512], BF16)
            mT = sb("mT", [128, 8, 256], BF16)
            load_weight(P, tag, wkv, mem_w_kv[li], 8, 512, stg, gain=gpc[:, 4 + li, :], PW=256)
            for t in range(2):
                ptag = "A%d" % li
                P.dma("sp", mt[:], mem_in[t * 128:(t + 1) * 128, :], writes=[R(ptag, "xt", 0)])
                rms_prep(P, ptag, 0, mt[:], sqj[:], ssq[:], rstd[:], xn[:], R(ptag, "xt", 0))
                transpose8(P, ptag, 0, xn, pbank["T"], R(ptag, "pT", 0), mT[:, :, t * 128:(t + 1) * 128], R(tag, "mT"))
            pk = pbank["F"]
            for cb in range(2):
                for c in range(8):
                    P.op("pe", lambda e, cb=cb, c=c: e.matmul(pk[:, 0:256], wkv[:, c, cb * 128:(cb + 1) * 128], mT[:, c, :],
                                                              start=(c == 0), stop=(c == 7)),
                         reads=[R(tag, "w"), R(tag, "mT")], writes=[R("A%d" % li, "pF", 0)])
                P.op("act", lambda e, cb=cb: e.copy(kmT[:, cb, :], pk[:, 0:256]), reads=[R("A%d" % li, "pF", 0)], writes=[R("kmT")])
            for t in range(2):
                for c in range(8):
                    P.op("pe", lambda e, t=t, c=c: e.matmul(pk[:, 0:256], mT[:, c, t * 128:(t + 1) * 128], wkv[:, c, 256:512],
                                                            start=(c == 0), stop=(c == 7)),
                         reads=[R(tag, "w"), R(tag, "mT")], writes=[R("A%d" % li, "pF", 0)])
                P.op("act", lambda e, t=t: e.copy(vm[:, t, :], pk[:, 0:256]), reads=[R("A%d" % li, "pF", 0)], writes=[R("vm")])

        def phase_A(li):
            tag = "A%d" % li
            P = Phase(ctx, tag)
            with ExitStack() as es:
                sb = lambda n, s, d: es.enter_context(nc.sbuf_tensor(tag + n, list(s), d))
                ps = lambda n, s, d: es.enter_context(nc.psum_tensor(tag + n, list(s), d))
                NW = 2560 if li == 0 else 3352
                w_src = att_w_in if li == 0 else dn_w_in
                x_src = x_in if li == 0 else X2
                wb = sb("wb", [128, 8, NW], BF16)
                stg = [sb("stg%d" % i, [128, 8, 256], F32) for i in range(2)]
                xt = [sb("xt%d" % i, [128, D], F32) for i in range(2)]
                sqj = sb("sqj", [128, D], BF16)
                ssq = [sb("ssq%d" % i, [128, 1], F32) for i in range(2)]
                rstd = [sb("rstd%d" % i, [128, 1], F32) for i in range(2)]
                xn = [sb("xn%d" % i, [128, D], BF16) for i in range(2)]
                hT = [sb("hT%d" % i, [128, 8, 512], BF16) for i in range(2)]
                fo = [sb("fo%d" % i, [128, 512], BF16) for i in range(3)]
                pT = [ps("pT%d" % i, [128, 8, 128], BF16) for i in range(2)]
                pF = [ps("pF%d" % i, [128, 512], F32) for i in range(2)]
                pV = [ps("pV%d" % i, [128, 1024], F32) for i in range(1)]
                mem_kv(P, es, li, {"T": pT[0], "F": pF[0]}, stg, xt[0], sqj, ssq[0], rstd[0], xn[0])
                load_weight(P, tag, wb, w_src, 8, NW, stg, gain=gpc[:, li, :], PW=256)
                if li == 0:
                    vo = [sb("vo%d" % i, [128, 768], BF16) for i in range(2)]
                    feat_blocks = [("q", i) for i in range(6)] + [("k", i) for i in range(6)] + [("m", i) for i in range(2)]
                else:
                    vo = [sb("vo%d" % i, [128, 768], BF16) for i in range(2)]
                    feat_blocks = [("c", i) for i in range(18)] + [("m", i) for i in range(2)]
                    gi = [sb("gi%d" % i, [128, 24], F32) for i in range(2)]
                    gw = [sb("gw%d" % i, [128, 6, 12], F32) for i in range(2)]
                    go = [sb("go%d" % i, [128, 2, 12], F32) for i in range(2)]
                    alog_b = sb("alogb", [128, 12], F32)
                    dtb_b = sb("dtbb", [128, 12], F32)
                    P.dma("sp", alog_b[:], dn_a_log.partition_broadcast(128), writes=[R(tag, "alog")])
                    P.dma("sp", dtb_b[:], dn_dt_bias.partition_broadcast(128), writes=[R(tag, "dtb")])
                    P.op("act", lambda e: e.activation(alog_b[:], alog_b[:], AF.Exp), reads=[R(tag, "alog")], writes=[R(tag, "alog")])
                    P.op("dve", lambda e: e.tensor_scalar(alog_b[:], alog_b[:], -1.0, None, ALU.mult),
                         reads=[R(tag, "alog")], writes=[R(tag, "alog")])
                nfo = 0
                for st in range(S // 512):
                    hb = st % 2
                    for s4 in range(4):
                        t = st * 4 + s4
                        b = t % 2
                        P.dma("sp", xt[b][:], x_src[t * 128:(t + 1) * 128, :], writes=[R(tag, "xt", b)])
                        rms_prep(P, tag, b, xt[b][:], sqj[:], ssq[b][:], rstd[b][:], xn[b][:], R(tag, "xt", b))
                        transpose8(P, tag, b, xn[b], pT[b], R(tag, "pT", b),
                                   hT[hb][:, :, s4 * 128:(s4 + 1) * 128], R(tag, "hT", hb))
                    for kind, i in feat_blocks:
                        if li == 0:
                            col0 = {"q": 0, "k": 768, "m": 2304}[kind] + i * 128
                        else:
                            col0 = {"c": 0, "m": 3096}[kind] + i * 128
                        pb = nfo % 2
                        for c in range(8):
                            P.op("pe", lambda e, c=c, col0=col0, pb=pb: e.matmul(pF[pb][:], wb[:, c, col0:col0 + 128], hT[hb][:, c, :],
                                                                                  start=(c == 0), stop=(c == 7)),
                                 reads=[R(tag, "w"), R(tag, "hT", hb)], writes=[R(tag, "pF", pb)])
                        sc = 0.125 if kind in ("q", "m") else 1.0
                        if kind == "m":
                            P.op("act", lambda e, pb=pb, i=i, sc=sc: e.activation(qm_box[0][:, i, st * 512:(st + 1) * 512], pF[pb][:], AF.Copy, scale=sc),
                                 reads=[R(tag, "pF", pb)], writes=[R("QmT")])
                        elif kind in ("q", "k"):
                            dst_t = QTd if kind == "q" else KTd
                            dil = GROUPS[i // 2][1]
                            m0 = st * 512 // dil
                            nm = 512 // dil
                            fb = nfo % 3
                            dsts = fo[fb][:].rearrange("p (r m) -> p r m", r=dil)
                            src = pF[pb][:].rearrange("p (m r) -> p r m", r=dil)
                            P.op("act", lambda e, dsts=dsts, src=src, sc=sc: e.activation(dsts, src, AF.Copy, scale=sc),
                                 reads=[R(tag, "pF", pb)], writes=[R(tag, "fo", fb)])
                            P.dma("pool", dst_t[i].rearrange("p (r l) -> p r l", r=dil)[:, :, m0:m0 + nm], dsts,
                                  reads=[R(tag, "fo", fb)])
                        else:
                            fb = nfo % 3
                            P.op("act", lambda e, pb=pb, fb=fb: e.copy(fo[fb][:], pF[pb][:]),
                                 reads=[R(tag, "pF", pb)], writes=[R(tag, "fo", fb)])
                            P.dma("pool", QKVT[i * 128:(i + 1) * 128, st * 512:(st + 1) * 512], fo[fb][:], reads=[R(tag, "fo", fb)])
                        nfo += 1
                    vcol = 1536 if li == 0 else 2304
                    for s4 in range(4):
                        t = st * 4 + s4
                        b = t % 2
                        for (n0, nn) in ((0, 512), (512, 256)):
                            for c in range(8):
                                P.op("pe", lambda e, c=c, n0=n0, nn=nn, s4=s4: e.matmul(
                                    pV[0][:, n0:n0 + nn], hT[hb][:, c, s4 * 128:(s4 + 1) * 128], wb[:, c, vcol + n0:vcol + n0 + nn],
                                    start=(c == 0), stop=(c == 7)),
                                    reads=[R(tag, "w"), R(tag, "hT", hb)], writes=[R(tag, "pV", n0)])
                        if li == 0:
                            P.op("dve", lambda e, b=b: e.tensor_copy(vo[b][:], pV[0][:, 0:768]),
                                 reads=[R(tag, "pV", 0), R(tag, "pV", 512)], writes=[R(tag, "vo", b)])
                            P.dma("pool", V0[t * 128:(t + 1) * 128, :], vo[b][:], reads=[R(tag, "vo", b)])
                        else:
                            for c in range(8):
                                P.op("pe", lambda e, c=c, s4=s4: e.matmul(
                                    pV[0][:, 768:792], hT[hb][:, c, s4 * 128:(s4 + 1) * 128], wb[:, c, 3072:3096],
                                    start=(c == 0), stop=(c == 7)),
                                    reads=[R(tag, "w"), R(tag, "hT", hb)], writes=[R(tag, "pV", 512)])
                            P.op("act", lambda e, b=b: e.activation(vo[b][:], pV[0][:, 0:768], AF.Silu),
                                 reads=[R(tag, "pV", 0), R(tag, "pV", 512)], writes=[R(tag, "vo", b)])
                            P.dma("pool", SZ[t * 128:(t + 1) * 128, :], vo[b][:], reads=[R(tag, "vo", b)])
                            P.op("dve", lambda e, b=b: e.tensor_copy(gi[b][:], pV[0][:, 768:792]),
                                 reads=[R(tag, "pV", 512)], writes=[R(tag, "gi", b)])
                            giv = gi[b][:].rearrange("p (d a h) -> p d a h", d=2, a=2)
                            w0 = gw[b][:, 0, :].rearrange("p (d h) -> p d h", d=2)
                            w1 = gw[b][:, 1, :].rearrange("p (d h) -> p d h", d=2)
                            rg = [R(tag, "gw", b)]
                            P.op("dve", lambda e, w0=w0, giv=giv: e.tensor_tensor(
                                w0, giv[:, :, 0, :], dtb_b[:].rearrange("p (d h) -> p d h", d=2), ALU.add),
                                reads=[R(tag, "gi", b), R(tag, "dtb")], writes=rg)
                            P.op("dve", lambda e, w1=w1, giv=giv: e.tensor_copy(w1, giv[:, :, 1, :]),
                                 reads=[R(tag, "gi", b)], writes=rg)
                            P.op("act", lambda e, b=b: e.activation(gw[b][:, 2, :], gw[b][:, 0, :], AF.Abs),
                                 reads=rg, writes=rg)
                            P.op("act", lambda e, b=b: e.activation(gw[b][:, 2, :], gw[b][:, 2, :], AF.Exp, scale=-1.0),
                                 reads=rg, writes=rg)
                            P.op("act", lambda e, b=b: e.activation(gw[b][:, 2, :], gw[b][:, 2, :], AF.Ln, bias=1.0),
                                 reads=rg, writes=rg)
                            P.op("dve", lambda e, b=b: e.scalar_tensor_tensor(gw[b][:, 3, :], gw[b][:, 0, :], 0.0, gw[b][:, 2, :], ALU.max, ALU.add),
                                 reads=rg, writes=rg)
                            P.op("dve", lambda e, b=b: e.tensor_tensor(go[b][:, 0, :], gw[b][:, 3, :], alog_b[:], ALU.mult),
                                 reads=rg + [R(tag, "alog")], writes=[R(tag, "go", b)])
                            P.op("act", lambda e, b=b: e.activation(gw[b][:, 4, :], gw[b][:, 1, :], AF.Exp, scale=-1.0),
                                 reads=rg, writes=rg)
                            P.op("dve", lambda e, b=b: e.tensor_scalar(gw[b][:, 4, :], gw[b][:, 4, :], 1.0, None, ALU.add),
                                 reads=rg, writes=rg)
                            P.op("dve", lambda e, b=b: e.reciprocal(go[b][:, 1, :], gw[b][:, 4, :]),
                                 reads=rg, writes=[R(tag, "go", b)])
                            P.dma("pool", GD[t * 128:(t + 1) * 128, :], go[b][:, 0, :], reads=[R(tag, "go", b)])
                            P.dma("pool", BETA[t * 128:(t + 1) * 128, :], go[b][:, 1, :], reads=[R(tag, "go", b)])
                P.emit()
            return P.nops

        def phase_B0():
            tag = "B0"
            P = Phase(ctx, tag)
            with ExitStack() as es:
                sb = lambda n, s, d: es.enter_context(nc.sbuf_tensor(tag + n, list(s), d))
                ps = lambda n, s, d: es.enter_context(nc.psum_tensor(tag + n, list(s), d))
                bm = sb("bm", [128, 12, 256], F32)
                P.dma("sp", bm[:], biasg.rearrange("h q k -> q h k"), writes=[R(tag, "bm")])
                P.op("dve", lambda e: e.tensor_tensor(bm[:], bm[:], cst[:, C_MNEG:C_MNEG + 256].unsqueeze(1).broadcast_to([128, 12, 256]), ALU.add),
                     reads=[R(tag, "bm"), R("cst")], writes=[R(tag, "bm")])
                NB = 3
                vw = [sb("vw%d" % i, [128, 2, 256], BF16) for i in range(NB)]
                sl = [sb("sl%d" % i, [128, 256], F32) for i in range(4)]
                pe_ = [sb("p%d" % i, [128, 256], BF16) for i in range(4)]
                pTs = [sb("pTs%d" % i, [128, 2, 128], BF16) for i in range(4)]
                st_ = [sb("st%d" % i, [128, 3, 4], F32) for i in range(NB)]
                on = [sb("on%d" % i, [128, 4, 64], F32) for i in range(NB)]
                pLT = [ps("pLT%d" % i, [128, 512], F32) for i in range(4)]
                pL = [t_[:, 0:256] for t_ in pLT]
                pT = [t_[:, 256:512].bitcast(BF16)[:, 0:256].rearrange("p (c t) -> p c t", c=2) for t_ in pLT]
                pO = [ps("pO%d" % i, [128, 4, 64], F32) for i in range(2)]
                qs = [sb("qs%d" % i, [128, 2, S], BF16) for i in range(2)]
                ks = [sb("ks%d" % i, [128, 2, S], BF16) for i in range(2)]
                it = 0
                ih = 0
                for g, (window, dil) in enumerate(GROUPS):
                    L = S // dil
                    ntile = L // 128
                    gb = g % 2
                    QT = qs[gb]
                    KT = ks[gb]
                    for b2 in range(2):
                        P.dma("sp", QT[:, b2, :], QTd[2 * g + b2], writes=[R(tag, "QK", gb)])
                        P.dma("sp", KT[:, b2, :], KTd[2 * g + b2], writes=[R(tag, "QK", gb)])
                    for r in range(dil):
                        for qt in range(ntile):
                            b = it % NB
                            ob = it % 2
                            base = r * L + qt * 128
                            first = (qt == 0)
                            last = (qt == ntile - 1)
                            c_lo = 64 if first else 0
                            c_hi = 192 if last else 256
                            for c in range(2):
                                m0 = qt * 128 - 64 + 128 * c
                                p_lo = 64 if (first and c == 0) else 0
                                p_hi = 64 if (last and c == 1) else 128
                                row0 = (m0 + p_lo) * dil + r
                                nrow = p_hi - p_lo
                                srcv = V0[row0:row0 + (nrow - 1) * dil + 1:dil, g * 256:(g + 1) * 256] if dil > 1 else \
                                    V0[row0:row0 + nrow, g * 256:(g + 1) * 256]
                                P.dma("sp", vw[b][p_lo:p_hi, c, :], srcv, writes=[R(tag, "vw", b, c)])
                            def head_chain(hh, ih):
                                h = g * 4 + hh
                                blk = hh // 2
                                pl = (hh % 2) * 64
                                lb = ih % 4
                                hbuf = ih % 4
                                P.op("pe", lambda e, lb=lb, blk=blk, pl=pl, base=base, c_lo=c_lo, c_hi=c_hi, QT=QT, KT=KT: e.matmul(
                                    pL[lb][:, c_lo:c_hi], QT[pl:pl + 64, blk, base:base + 128],
                                    KT[pl:pl + 64, blk, base - 64 + c_lo:base - 64 + c_hi], start=True, stop=True),
                                    reads=[R(tag, "QK", gb)], writes=[R(tag, "pL", lb)])
                                yield
                                P.op("dve", lambda e: e.tensor_tensor(
                                    sl[hbuf][:, c_lo:c_hi], pL[lb][:, c_lo:c_hi], bm[:, h, c_lo:c_hi], ALU.add),
                                    reads=[R(tag, "pL", lb), R(tag, "bm")], writes=[R(tag, "sl", hbuf)])
                                yield
                                P.op("dve", lambda e: e.tensor_reduce(
                                    st_[b][:, 0, hh:hh + 1], sl[hbuf][:, c_lo:c_hi], AX.X, ALU.max, negate=True),
                                    reads=[R(tag, "sl", hbuf)], writes=[R(tag, "nmx", b, hh)])
                                yield
                                P.op("act", lambda e: e.activation(
                                    pe_[hbuf][:, c_lo:c_hi], sl[hbuf][:, c_lo:c_hi], AF.Exp, bias=st_[b][:, 0, hh:hh + 1],
                                    accum_out=st_[b][:, 1, hh:hh + 1]),
                                    reads=[R(tag, "sl", hbuf), R(tag, "nmx", b, hh)], writes=[R(tag, "p", hbuf), R(tag, "den", b, hh)])
                                yield
                                for c in range(2):
                                    P.op("pe", lambda e, c=c: e.transpose(pT[lb][:, c, :], pe_[hbuf][:, c * 128:(c + 1) * 128], identb[:]),
                                         reads=[R(tag, "p", hbuf), R("identb")], writes=[R(tag, "pL", lb)])
                                yield
                                P.op("dve" if hh % 2 else "act",
                                     (lambda e: e.tensor_copy(pTs[hbuf][:], pT[lb])) if hh % 2 else
                                     (lambda e: e.copy(pTs[hbuf][:], pT[lb])),
                                     reads=[R(tag, "pL", lb)], writes=[R(tag, "pTs", hbuf)])
                                yield
                                for c in range(2):
                                    p_lo = 64 if (first and c == 0) else 0
                                    p_hi = 64 if (last and c == 1) else 128
                                    P.op("pe", lambda e, c=c, p_lo=p_lo, p_hi=p_hi: e.matmul(
                                        pO[ob][:, hh, :], pTs[hbuf][p_lo:p_hi, c, :], vw[b][p_lo:p_hi, c, hh * 64:(hh + 1) * 64],
                                        start=(c == 0), stop=(c == 1)),
                                        reads=[R(tag, "pTs", hbuf), R(tag, "vw", b, c)], writes=[R(tag, "pO", ob, hh)])
                                yield
                            gens = [head_chain(hh, ih + hh) for hh in range(4)]
                            ih += 4
                            while gens:
                                for gq in list(gens):
                                    try:
                                        next(gq)
                                    except StopIteration:
                                        gens.remove(gq)
                            rden = [R(tag, "den", b, hh) for hh in range(4)]
                            rnmx = [R(tag, "nmx", b, hh) for hh in range(4)]
                            P.op("act", lambda e, b=b: e.activation(st_[b][:, 2, :], st_[b][:, 1, :], AF.Ln),
                                 reads=rden, writes=[R(tag, "lse", b)])
                            P.op("dve", lambda e, b=b: e.tensor_tensor(st_[b][:, 2, :], st_[b][:, 2, :], st_[b][:, 0, :], ALU.subtract),
                                 reads=[R(tag, "lse", b)] + rnmx, writes=[R(tag, "lse", b)])
                            P.op("dve", lambda e, b=b: e.reciprocal(st_[b][:, 1, :], st_[b][:, 1, :]),
                                 reads=rden + [R(tag, "lse", b)], writes=rden)
                            P.op("dve", lambda e, b=b, ob=ob: e.tensor_tensor(
                                on[b][:], pO[ob][:], st_[b][:, 1, :].unsqueeze(2).broadcast_to([128, 4, 64]), ALU.mult),
                                reads=rden + [R(tag, "pO", ob, hh) for hh in range(4)], writes=[R(tag, "on", b)])
                            row0 = (qt * 128) * dil + r
                            if dil > 1:
                                dsto = OATT[row0:row0 + 127 * dil + 1:dil, g * 256:(g + 1) * 256]
                                dstl = LSE[row0:row0 + 127 * dil + 1:dil, g * 4:(g + 1) * 4]
                            else:
                                dsto = OATT[row0:row0 + 128, g * 256:(g + 1) * 256]
                                dstl = LSE[row0:row0 + 128, g * 4:(g + 1) * 4]
                            P.dma("pool", dsto, on[b][:].rearrange("p h d -> p (h d)"), reads=[R(tag, "on", b)])
                            P.dma("pool", dstl, st_[b][:, 2, :], reads=[R(tag, "lse", b)])
                            it += 1
                P.emit()
            return P.nops

        def phase_C(li):
            tag = "C%d" % li
            P = Phase(ctx, tag)
            x_src = x_in if li == 0 else X2
            x_dst = X1 if li == 0 else X3
            w_src = att_w_out if li == 0 else dn_w_out
            with ExitStack() as es:
                sb = lambda n, s, d: es.enter_context(nc.sbuf_tensor(tag + n, list(s), d))
                ps = lambda n, s, d: es.enter_context(nc.psum_tensor(tag + n, list(s), d))
                wo = sb("wo", [128, 8, D], BF16)
                stg = [sb("stg%d" % i, [128, 8, 256], F32) for i in range(2)]
                gpost = sb("gpost", [128, D], F32)
                P.dma("sp", gpost[:], norm_mix_post[li].partition_broadcast(128), writes=[R(tag, "gpost")])
                load_weight(P, tag, wo, w_src, 8, D, stg, PW=256)
                NB = 2
                ot = [sb("ot%d" % i, [128, 768], F32) for i in range(NB)]
                cat = [sb("cat%d" % i, [128, D], BF16) for i in range(NB)]
                catT = [sb("catT%d" % i, [128, 8, 128], BF16) for i in range(NB)]
                xr = [sb("xr%d" % i, [128, D], F32) for i in range(NB)]
                xo = [sb("xo%d" % i, [128, D], F32) for i in range(NB)]
                tmp = [sb("tmp%d" % i, [128, D], F32) for i in range(NB)]
                sqj = sb("sqj", [128, D], BF16)
                ssq2 = [sb("ssq2%d" % i, [128, 2], F32) for i in range(NB)]
                rstd = [sb("rstd%d" % i, [128, 1], F32) for i in range(NB)]
                pm = [sb("pm%d" % i, [128, 256], BF16) for i in range(4)]
                pmT = [sb("pmT%d" % i, [128, 2, 128], BF16) for i in range(4)]
                ms = [sb("ms%d" % i, [128, 2, 4], F32) for i in range(NB)]
                if li == 0:
                    ls = [sb("ls%d" % i, [128, 4, 12], F32) for i in range(NB)]
                else:
                    ot2 = [sb("ot2%d" % i, [128, 768], F32) for i in range(NB)]
                    szt = [sb("szt%d" % i, [128, 768], BF16) for i in range(NB)]
                    os_ = [sb("os%d" % i, [128, 2, 6], F32) for i in range(NB)]
                    onb = sb("onb", [128, 128], F32)
                    P.dma("sp", onb[:], dn_out_norm.partition_broadcast(128), writes=[R(tag, "onb")])
                pLT = [ps("pLT%d" % i, [128, 512], F32) for i in range(4)]
                pL = [t_[:, 0:256] for t_ in pLT]
                pTm = [t_[:, 256:512].bitcast(BF16)[:, 0:256].rearrange("p (c t) -> p c t", c=2) for t_ in pLT]
                pT = [ps("pT%d" % i, [128, 8, 128], BF16) for i in range(1)]
                pO = [ps("pO%d" % i, [128, 4, 64], F32) for i in range(1)]
                pY = [ps("pY%d" % i, [128, D], F32) for i in range(1)]
                ih = 0
                for t in range(NT):
                    b = t % NB
                    rows = slice(t * 128, (t + 1) * 128)
                    P.dma("sp", xr[b][:], x_src[rows, :], writes=[R(tag, "xr", b)])
                    if li == 0:
                        P.dma("sp", ot[b][:], OATT[rows, :], writes=[R(tag, "ot", b)])
                        P.dma("sp", ls[b][:, 0, :], LSE[rows, :], writes=[R(tag, "ls", b)])
                        rl = [R(tag, "ls", b)]
                        l3 = ls[b][:, 0, :].rearrange("p (g h) -> p g h", g=3)
                        e3 = ls[b][:, 1, :].rearrange("p (g h) -> p g h", g=3)
                        P.op("dve", lambda e, b=b, l3=l3: e.tensor_tensor(ls[b][:, 2, 0:4], l3[:, 0, :], l3[:, 1, :], ALU.max), reads=rl, writes=rl)
                        P.op("dve", lambda e, b=b, l3=l3: e.tensor_tensor(ls[b][:, 2, 0:4], ls[b][:, 2, 0:4], l3[:, 2, :], ALU.max), reads=rl, writes=rl)
                        P.op("dve", lambda e, b=b, l3=l3, e3=e3: e.tensor_tensor(
                            e3, l3, ls[b][:, 2, 0:4].unsqueeze(1).broadcast_to([128, 3, 4]), ALU.subtract), reads=rl, writes=rl)
                        P.op("act", lambda e, b=b: e.activation(ls[b][:, 1, :], ls[b][:, 1, :], AF.Exp), reads=rl, writes=rl)
                        P.op("dve", lambda e, b=b, e3=e3: e.tensor_tensor(ls[b][:, 2, 4:8], e3[:, 0, :], e3[:, 1, :], ALU.add), reads=rl, writes=rl)
                        P.op("dve", lambda e, b=b, e3=e3: e.tensor_tensor(ls[b][:, 2, 4:8], ls[b][:, 2, 4:8], e3[:, 2, :], ALU.add), reads=rl, writes=rl)
                        P.op("dve", lambda e, b=b: e.reciprocal(ls[b][:, 2, 4:8], ls[b][:, 2, 4:8]), reads=rl, writes=rl)
                        P.op("dve", lambda e, b=b, e3=e3: e.tensor_tensor(
                            e3, e3, ls[b][:, 2, 4:8].unsqueeze(1).broadcast_to([128, 3, 4]), ALU.mult), reads=rl, writes=rl)
                        P.op("pool", lambda e, b=b: e.tensor_tensor(
                            cat[b][:, 0:768].rearrange("p (h d) -> p h d", h=12), ot[b][:].rearrange("p (h d) -> p h d", h=12),
                            ls[b][:, 1, :].unsqueeze(2).broadcast_to([128, 12, 64]), ALU.mult),
                            reads=rl + [R(tag, "ot", b)], writes=[R(tag, "cat", b)])
                    else:
                        P.dma("sp", ot[b][:], OF[rows, :], writes=[R(tag, "ot", b)])
                        P.dma("sp", ot2[b][:], OB[rows, :], writes=[R(tag, "ot2", b)])
                        P.dma("sp", szt[b][:], SZ[rows, :], writes=[R(tag, "szt", b)])
                        P.op("pool", lambda e, b=b: e.tensor_tensor(ot[b][:], ot[b][:], ot2[b][:], ALU.add),
                             reads=[R(tag, "ot", b), R(tag, "ot2", b)], writes=[R(tag, "ot", b)])
                        for h in range(6):
                            P.op("act", lambda e, b=b, h=h: e.activation(ot2[b][:, h * 128:(h + 1) * 128], ot[b][:, h * 128:(h + 1) * 128],
                                                                         AF.Square, accum_out=os_[b][:, 0, h:h + 1]),
                                 reads=[R(tag, "ot", b)], writes=[R(tag, "ot2", b), R(tag, "os", b)])
                        P.op("act", lambda e, b=b: e.activation(os_[b][:, 1, :], os_[b][:, 0, :], AF.Sqrt, bias=EPS, scale=1.0 / 128),
                             reads=[R(tag, "os", b)], writes=[R(tag, "os", b)])
                        P.op("dve", lambda e, b=b: e.reciprocal(os_[b][:, 1, :], os_[b][:, 1, :]), reads=[R(tag, "os", b)], writes=[R(tag, "os", b)])
                        o3 = ot[b][:].rearrange("p (h d) -> p h d", h=6)
                        P.op("dve", lambda e, b=b, o3=o3: e.tensor_tensor(o3, o3, os_[b][:, 1, :].unsqueeze(2).broadcast_to([128, 6, 128]), ALU.mult),
                             reads=[R(tag, "ot", b), R(tag, "os", b)], writes=[R(tag, "ot", b)])
                        P.op("pool", lambda e, b=b, o3=o3: e.tensor_tensor(o3, o3, onb[:].unsqueeze(1).broadcast_to([128, 6, 128]), ALU.mult),
                             reads=[R(tag, "ot", b), R(tag, "onb")], writes=[R(tag, "ot", b)])
                        P.op("pool", lambda e, b=b: e.tensor_tensor(cat[b][:, 0:768], ot[b][:], szt[b][:], ALU.mult),
                             reads=[R(tag, "ot", b), R(tag, "szt", b)], writes=[R(tag, "cat", b)])
                    def mem_chain(hd, ihh):
                        blk = hd // 2
                        pl = (hd % 2) * 64
                        lb = ihh % 4
                        hb4 = ihh % 4
                        P.op("pe", lambda e: e.matmul(
                            pL[lb], qm_box[0][pl:pl + 64, blk, t * 128:(t + 1) * 128], kmT[pl:pl + 64, blk, :], start=True, stop=True),
                            reads=[R("QmT"), R("kmT")], writes=[R(tag, "pL", lb)])
                        yield
                        P.op("dve", lambda e: e.tensor_reduce(ms[b][:, 0, hd:hd + 1], pL[lb], AX.X, ALU.max, negate=True),
                             reads=[R(tag, "pL", lb)], writes=[R(tag, "mnmx", b, hd)])
                        yield
                        P.op("act", lambda e: e.activation(pm[hb4][:], pL[lb], AF.Exp, bias=ms[b][:, 0, hd:hd + 1],
                                                           accum_out=ms[b][:, 1, hd:hd + 1]),
                             reads=[R(tag, "pL", lb), R(tag, "mnmx", b, hd)], writes=[R(tag, "pm", hb4), R(tag, "mden", b, hd)])
                        yield
                        for c in range(2):
                            P.op("pe", lambda e, c=c: e.transpose(pTm[lb][:, c, :], pm[hb4][:, c * 128:(c + 1) * 128], identb[:]),
                                 reads=[R(tag, "pm", hb4), R("identb")], writes=[R(tag, "pL", lb)])
                        P.op("dve", lambda e: e.tensor_copy(pmT[hb4][:], pTm[lb]), reads=[R(tag, "pL", lb)], writes=[R(tag, "pmT", hb4)])
                        yield
                        for c in range(2):
                            P.op("pe", lambda e, c=c: e.matmul(pO[0][:, hd, :], pmT[hb4][:, c, :], vm[:, c, hd * 64:(hd + 1) * 64],
                                                               start=(c == 0), stop=(c == 1)),
                                 reads=[R(tag, "pmT", hb4), R("vm")], writes=[R(tag, "pO", hd)])
                        yield
                    gens = [mem_chain(hd, ih + hd) for hd in range(4)]
                    ih += 4
                    while gens:
                        for gq in list(gens):
                            try:
                                next(gq)
                            except StopIteration:
                                gens.remove(gq)
                    rden = [R(tag, "mden", b, hd) for hd in range(4)]
                    P.op("dve", lambda e, b=b: e.reciprocal(ms[b][:, 1, :], ms[b][:, 1, :]), reads=rden, writes=rden)
                    P.op("dve", lambda e, b=b: e.tensor_tensor(
                        cat[b][:, 768:1024].rearrange("p (h d) -> p h d", h=4), pO[0][:],
                        ms[b][:, 1, :].unsqueeze(2).broadcast_to([128, 4, 64]), ALU.mult),
                        reads=rden + [R(tag, "pO", hd) for hd in range(4)], writes=[R(tag, "cat", b)])
                    for c in range(8):
                        P.op("pe", lambda e, b=b, c=c: e.transpose(pT[0][:, c, :], cat[b][:, c * 128:(c + 1) * 128], identb[:]),
                             reads=[R(tag, "cat", b), R("identb")], writes=[R(tag, "pT", 0)])
                    P.op("act", lambda e, b=b: e.copy(catT[b][:], pT[0][:]), reads=[R(tag, "pT", 0)], writes=[R(tag, "catT", b)])
                    yb = 0
                    for hf in range(2):
                        for c in range(8):
                            P.op("pe", lambda e, b=b, c=c, hf=hf, yb=yb: e.matmul(
                                pY[yb][:, hf * 512:(hf + 1) * 512], catT[b][:, c, :], wo[:, c, hf * 512:(hf + 1) * 512],
                                start=(c == 0), stop=(c == 7)),
                                reads=[R(tag, "catT", b), R(tag, "w")], writes=[R(tag, "pY", yb)])
                    post_norm_residual(P, tag, b, pY[yb][:], R(tag, "pY", yb), gpost[:], xr[b][:], R(tag, "xr", b),
                                       sqj, ssq2[b], rstd[b][:], tmp[b][:], xo[b][:], R(tag, "xo", b))
                    P.dma("pool", x_dst[rows, :], xo[b][:], reads=[R(tag, "xo", b)])
                P.emit()
            return P.nops

        def phase_D(li):
            tag = "D%d" % li
            P = Phase(ctx, tag)
            x_src = X1 if li == 0 else X3
            x_dst = X2 if li == 0 else y_out
            TS = 256
            with ExitStack() as es:
                sb = lambda n, s, d: es.enter_context(nc.sbuf_tensor(tag + n, list(s), d))
                ps = lambda n, s, d: es.enter_context(nc.psum_tensor(tag + n, list(s), d))
                wgu = sb("wgu", [128, 8, 2 * DFF], BF16)
                wd = sb("wd", [128, 22, D], BF16)
                stg = [sb("stg%d" % i, [128, 8, 256], F32) for i in range(2)]
                gpost = sb("gpost", [128, D], F32)
                P.dma("sp", gpost[:], norm_ffn_post[li].partition_broadcast(128), writes=[R(tag, "gpost")])
                load_weight(P, tag + "gu", wgu, ffn_wgu[li], 8, 2 * DFF, stg, gain=gpc[:, 2 + li, :], PW=256)
                wdv = ffn_wd[li].rearrange("(c p) n -> p c n", p=128)
                for i in range(11):
                    sb_ = stg[i % 2]
                    stv = sb_[:].rearrange("p c n -> p (c n)").rearrange("p (c n) -> p c n", c=2)
                    P.dma("sp", stv, wdv[:, 2 * i:2 * i + 2, :], writes=[R("stg", id(sb_))])
                    P.op("pool" if i % 2 else "dve", lambda e, i=i, stv=stv: e.tensor_copy(wd[:, 2 * i:2 * i + 2, :], stv),
                         reads=[R("stg", id(sb_))], writes=[R(tag, "wd")])
                xt = [sb("xt%d" % i, [128, D], F32) for i in range(2)]
                tmp = sb("tmp", [128, D], F32)
                sqj = sb("sqj", [128, D], BF16)
                ssq = [sb("ssq%d" % i, [128, 1], F32) for i in range(2)]
                rstd = [sb("rstd%d" % i, [128, 1], F32) for i in range(2)]
                ssq2 = [sb("ssq2%d" % i, [128, 2], F32) for i in range(2)]
                prstd = [sb("prstd%d" % i, [128, 1], F32) for i in range(2)]
                xn = [sb("xn%d" % i, [128, D], BF16) for i in range(2)]
                hT = [sb("hT%d" % i, [128, 8, TS], BF16) for i in range(2)]
                actT = sb("actT", [128, 22, TS], BF16)
                sg = [sb("sg%d" % i, [128, TS], BF16) for i in range(2)]
                pT = [ps("pT%d" % i, [128, 8, 128], BF16) for i in range(1)]
                pG = [ps("pG%d" % i, [128, 512], F32) for i in range(2)]
                pU = [ps("pU%d" % i, [128, 512], F32) for i in range(2)]
                pY = [ps("pY%d" % i, [128, D], F32) for i in range(1)]
                nsub = TS // 128
                for st in range(S // TS):
                    hb = st % 2
                    for s4 in range(nsub):
                        t = st * nsub + s4
                        b = t % 2
                        P.dma("sp", xt[b][:], x_src[t * 128:(t + 1) * 128, :], writes=[R(tag, "xt", b)])
                        rms_prep(P, tag, b, xt[b][:], sqj[:], ssq[b][:], rstd[b][:], xn[b][:], R(tag, "xt", b))
                        transpose8(P, tag, b, xn[b], pT[0], R(tag, "pT"), hT[hb][:, :, s4 * 128:(s4 + 1) * 128], R(tag, "hT", hb))
                    for j in range(22):
                        pb = j % 2
                        for c in range(8):
                            P.op("pe", lambda e, c=c, j=j, pb=pb, hb=hb: e.matmul(pG[pb][:, 0:TS], wgu[:, c, j * 128:(j + 1) * 128], hT[hb][:, c, :],
                                                                                  start=(c == 0), stop=(c == 7)),
                                 reads=[R(tag + "gu", "w"), R(tag, "hT", hb)], writes=[R(tag, "pG", pb)])
                        for c in range(8):
                            P.op("pe", lambda e, c=c, j=j, pb=pb, hb=hb: e.matmul(pU[pb][:, 0:TS], wgu[:, c, DFF + j * 128:DFF + (j + 1) * 128], hT[hb][:, c, :],
                                                                                  start=(c == 0), stop=(c == 7)),
                                 reads=[R(tag + "gu", "w"), R(tag, "hT", hb)], writes=[R(tag, "pU", pb)])
                        P.op("act", lambda e, pb=pb: e.activation(sg[pb][:], pG[pb][:, 0:TS], AF.Silu),
                             reads=[R(tag, "pG", pb)], writes=[R(tag, "sg", pb)])
                        P.op("dve", lambda e, pb=pb, j=j: e.tensor_tensor(actT[:, j, :], sg[pb][:], pU[pb][:, 0:TS], ALU.mult),
                             reads=[R(tag, "sg", pb), R(tag, "pU", pb)], writes=[R(tag, "actT", j)])
                    for s4 in range(nsub):
                        t = st * nsub + s4
                        b = t % 2
                        P.dma("sp", xt[b][:], x_src[t * 128:(t + 1) * 128, :], writes=[R(tag, "xt", b)])
                        for hf in range(2):
                            for j in range(22):
                                P.op("pe", lambda e, j=j, hf=hf, s4=s4: e.matmul(
                                    pY[0][:, hf * 512:(hf + 1) * 512], actT[:, j, s4 * 128:(s4 + 1) * 128], wd[:, j, hf * 512:(hf + 1) * 512],
                                    start=(j == 0), stop=(j == 21)),
                                    reads=[R(tag, "actT", j), R(tag, "wd")], writes=[R(tag, "pY", hf)])
                        post_norm_residual(P, tag, b, pY[0][:], [R(tag, "pY", 0), R(tag, "pY", 1)], gpost[:], xt[b][:], R(tag, "xt", b),
                                           sqj, ssq2[b], prstd[b][:], tmp[:], xt[b][:], R(tag, "xt", b))
                        P.dma("pool", x_dst[t * 128:(t + 1) * 128, :], xt[b][:], reads=[R(tag, "xt", b)])
                P.emit()
            return P.nops

        def phase_B1():
            tag = "B1"
            P = Phase(ctx, tag)
            NCH = 32
            NSL = 4
            with ExitStack() as es:
                sb = lambda n, s, d: es.enter_context(nc.sbuf_tensor(tag + n, list(s), d))
                ps = lambda n, s, d: es.enter_context(nc.psum_tensor(tag + n, list(s), d))
                bk = [ps("bk%d" % i, [128, 512], F32) for i in range(8)]
                rbk = lambda i, j: R(tag, "bk", i)
                for i in range(8):
                    P.excl.add(rbk(i, 0))
                Gall = sb("Gall", [128, NCH, 12], F32)
                Ball = sb("Ball", [128, NCH, 12], F32)
                GC = sb("GC", [128, NCH, 12], F32)
                NGC = sb("NGC", [128, NCH, 12], F32)
                EG = sb("EG", [128, NCH, 12], F32)
                NBEG = sb("NBEG", [128, NCH, 12], F32)
                NBt = sb("NBt", [128, NCH, 12], F32)
                GLb = sb("GLb", [128, NCH, 12], F32)
                EGL = sb("EGL", [128, NCH, 12], F32)
                EKD = sb("EKD", [128, NCH, 12], F32)
                cw = sb("cw", [128, 18, 5], F32)
                for n in range(NCH):
                    P.dma("sp", Gall[:, n, :], GD[n * 128:(n + 1) * 128, :], writes=[R(tag, "Gall")])
                    P.dma("sp", Ball[:, n, :], BETA[n * 128:(n + 1) * 128, :], writes=[R(tag, "Ball")])
                for bb in range(18):
                    P.dma("sp", cw[:, bb, :], dn_convT[bb * 128:(bb + 1) * 128, :], writes=[R(tag, "cw")])
                v3 = lambda ap: ap.rearrange("p (n c) -> p n c", c=12)
                rg = [R(tag, "gates")]
                allbk0 = [rbk(0, j) for j in range(4)]
                allbk1 = [rbk(1, j) for j in range(4)]
                P.op("pe", lambda e: e.matmul(bk[0][:, 0:384], cst[:, C_TRIL:C_TRIL + 128], Gall[:].rearrange("p n c -> p (n c)"), start=True, stop=True),
                     reads=[R("cst"), R(tag, "Gall")], writes=allbk0)
                P.op("pe", lambda e: e.matmul(bk[1][:, 0:384], cst[:, C_TRIU:C_TRIU + 128], Gall[:].rearrange("p n c -> p (n c)"), start=True, stop=True),
                     reads=[R("cst"), R(tag, "Gall")], writes=allbk1)
                P.op("act", lambda e: e.copy(GC[:, :, 0:6], v3(bk[0][:, 0:384])[:, :, 0:6]), reads=allbk0, writes=rg)
                P.op("act", lambda e: e.copy(GC[:, :, 6:12], v3(bk[1][:, 0:384])[:, :, 6:12]), reads=allbk1, writes=rg)
                P.op("pe", lambda e: e.matmul(bk[0][:, 0:384], cst[:, C_SELL:C_SELL + 128], GC[:].rearrange("p n c -> p (n c)"), start=True, stop=True),
                     reads=[R("cst")] + rg, writes=allbk0)
                P.op("pe", lambda e: e.matmul(bk[1][:, 0:384], cst[:, C_SELF:C_SELF + 128], GC[:].rearrange("p n c -> p (n c)"), start=True, stop=True),
                     reads=[R("cst")] + rg, writes=allbk1)
                P.op("act", lambda e: e.copy(GLb[:, :, 0:6], v3(bk[0][:, 0:384])[:, :, 0:6]), reads=allbk0, writes=rg)
                P.op("act", lambda e: e.copy(GLb[:, :, 6:12], v3(bk[1][:, 0:384])[:, :, 6:12]), reads=allbk1, writes=rg)
                P.op("dve", lambda e: e.tensor_scalar(NGC[:], GC[:], -1.0, None, ALU.mult), reads=rg, writes=rg)
                P.op("act", lambda e: e.activation(EG[:], GC[:], AF.Exp), reads=rg, writes=rg)
                P.op("dve", lambda e: e.scalar_tensor_tensor(NBEG[:], Ball[:], -1.0, EG[:], ALU.mult, ALU.mult), reads=rg + [R(tag, "Ball")], writes=rg)
                P.op("dve", lambda e: e.tensor_scalar(NBt[:], Ball[:], -1.0, None, ALU.mult), reads=[R(tag, "Ball")], writes=rg)
                P.op("act", lambda e: e.activation(EGL[:], GLb[:], AF.Exp), reads=rg, writes=rg)
                P.op("dve", lambda e: e.tensor_tensor(EKD[:], GLb[:], GC[:], ALU.subtract), reads=rg, writes=rg)
                P.op("act", lambda e: e.activation(EKD[:], EKD[:], AF.Exp), reads=rg, writes=rg)

                raw = [sb("raw%d" % i, [128, S + 4], BF16) for i in range(2)]
                acc = [sb("acc%d" % i, [128, 1024], F32) for i in range(2)]
                slu = [sb("slu%d" % i, [128, 1024], F32) for i in range(2)]
                sq = [sb("sq%d" % i, [128, 1024], BF16) for i in range(2)]
                rinv = [sb("rinv%d" % i, [128, 512], F32) for i in range(2)]
                kqT = sb("kqT", [128, NCH, 2, 128], BF16)
                vT = sb("vT", [128, S], BF16)
                ktok = sb("ktok", [128, NCH, 128], BF16)
                vtok = sb("vtok", [128, NCH, 128], BF16)
                for i in range(2):
                    P.op("pool", lambda e, i=i: e.memset(raw[i][:, 0:2], 0.0), writes=[R(tag, "rawpad", i)])
                    P.op("pool", lambda e, i=i: e.memset(raw[i][:, S + 2:S + 4], 0.0), writes=[R(tag, "rawpad", i)])
                dg2 = [[sb("dg2_%d_%d" % (d, i), [128, 256], F32) for i in range(NSL)] for d in range(2)]
                dS_ = [[sb("dS_%d_%d" % (d, i), [128, 128], F32) for i in range(NSL)] for d in range(2)]
                dT_ = [[sb("dT_%d_%d" % (d, i), [128, 128], F32) for i in range(NSL)] for d in range(2)]
                Xb = [[[sb("X_%d_%d_%d" % (d, i, k), [128, 128], F32) for k in range(2)] for i in range(NSL)] for d in range(2)]
                Yb = [[[sb("Y_%d_%d_%d" % (d, i, k), [128, 128], F32) for k in range(2)] for i in range(NSL)] for d in range(2)]
                Qb = [[[sb("Q_%d_%d_%d" % (d, i, k), [128, 128], F32) for k in range(2)] for i in range(NSL)] for d in range(2)]
                inT = [[sb("inT_%d_%d" % (d, i), [128, 128], BF16) for i in range(NSL)] for d in range(2)]
                TTb = [[sb("TTb_%d_%d" % (d, i), [128, 128], BF16) for i in range(NSL)] for d in range(2)]
                BV = [[sb("BV_%d_%d" % (d, i), [128, 128], BF16) for i in range(NSL)] for d in range(2)]
                kdec = [[sb("kdec_%d_%d" % (d, i), [128, 128], BF16) for i in range(NSL)] for d in range(2)]
                S32 = [sb("S32_%d" % d, [128, 128], F32) for d in range(2)]
                Sbf = [[sb("Sbf_%d_%d" % (d, k), [128, 128], BF16) for k in range(2)] for d in range(2)]
                Rt = [[sb("Rt_%d_%d" % (d, k), [128, 128], BF16) for k in range(2)] for d in range(2)]
                vnb = [[sb("vnb_%d_%d" % (d, k), [128, 128], BF16) for k in range(2)] for d in range(2)]
                oq = [[sb("oq_%d_%d" % (d, k), [128, 128], F32) for k in range(2)] for d in range(2)]
                oo = [[sb("oo_%d_%d" % (d, k), [128, 128], F32) for k in range(2)] for d in range(2)]
                dblc = [0, 0]

                def dbl_region(d, j):
                    return bk[3 + d][:, j * 128:(j + 1) * 128], rbk(3 + d, j)

                def head_prep(h):
                    nraw = [0]
                    for kind in range(3):
                        blk = kind * 6 + h
                        rb = (h * 3 + kind) % 2
                        P.dma("sp", raw[rb][:, 2:S + 2], QKVT[blk * 128:(blk + 1) * 128, :], writes=[R(tag, "raw", rb)])
                        rr = [R(tag, "raw", rb), R(tag, "rawpad", rb), R(tag, "cw")]
                        for pc in range(4):
                            t0 = pc * 1024
                            ab = pc % 2
                            P.op("dve", lambda e, rb=rb, ab=ab, t0=t0, blk=blk: e.tensor_scalar(
                                acc[ab][:], raw[rb][:, t0:t0 + 1024], cw[:, blk, 0:1], None, ALU.mult),
                                reads=rr, writes=[R(tag, "acc", ab)])
                            for j in range(1, 5):
                                P.op("dve", lambda e, rb=rb, ab=ab, t0=t0, blk=blk, j=j: e.scalar_tensor_tensor(
                                    acc[ab][:], raw[rb][:, t0 + j:t0 + j + 1024], cw[:, blk, j:j + 1], acc[ab][:], ALU.mult, ALU.add),
                                    reads=rr + [R(tag, "acc", ab)], writes=[R(tag, "acc", ab)])
                            if kind == 2:
                                P.op("act", lambda e, ab=ab, t0=t0: e.activation(vT[:, t0:t0 + 1024], acc[ab][:], AF.Silu),
                                     reads=[R(tag, "acc", ab)], writes=[R(tag, "vT")])
                                continue
                            P.op("act", lambda e, ab=ab: e.activation(slu[ab][:], acc[ab][:], AF.Silu),
                                 reads=[R(tag, "acc", ab)], writes=[R(tag, "slu", ab)])
                            P.op("pool", lambda e, ab=ab: e.tensor_tensor(sq[ab][:], slu[ab][:], slu[ab][:], ALU.mult),
                                 reads=[R(tag, "slu", ab)], writes=[R(tag, "sq", ab)])
                            for hf in range(2):
                                wr = [rbk(3 + hf, j) for j in range(4)]
                                P.op("pe", lambda e, ab=ab, hf=hf: e.matmul(bk[3 + hf][:], onesb[:], sq[ab][:, hf * 512:(hf + 1) * 512], start=True, stop=True),
                                     reads=[R(tag, "sq", ab), R("onesb")], writes=wr)
                                P.op("act", lambda e, hf=hf: e.activation(rinv[hf][:], bk[3 + hf][:], AF.Sqrt, bias=EPS, scale=1.0),
                                     reads=wr, writes=[R(tag, "rinv", hf)])
                                P.op("dve", lambda e, hf=hf: e.reciprocal(rinv[hf][:], rinv[hf][:]), reads=[R(tag, "rinv", hf)], writes=[R(tag, "rinv", hf)])
                                n0 = pc * 8 + hf * 4
                                kidx = 1 if kind == 0 else 0
                                scl = (128.0 ** -0.5) if kind == 0 else 1.0
                                P.op("dve", lambda e, ab=ab, hf=hf, n0=n0, kidx=kidx, scl=scl: e.scalar_tensor_tensor(
                                    kqT[:, n0:n0 + 4, kidx, :], slu[ab][:, hf * 512:(hf + 1) * 512].rearrange("p (n t) -> p n t", n=4), scl,
                                    rinv[hf][:].rearrange("p (n t) -> p n t", n=4), ALU.mult, ALU.mult),
                                    reads=[R(tag, "slu", ab), R(tag, "rinv", hf)], writes=[R(tag, "kqT")])
                    for which in range(2):
                        for n0 in range(0, NCH, 4):
                            bi = 5 + (n0 // 4) % 2
                            pv = bk[bi][:].bitcast(BF16)[:, 0:512].rearrange("p (n t) -> p n t", n=4)
                            wr = [rbk(bi, j) for j in range(4)]
                            for k4 in range(4):
                                n = n0 + k4
                                src = kqT[:, n, 0, :] if which == 0 else vT[:, n * 128:(n + 1) * 128]
                                P.op("pe", lambda e, pv=pv, k4=k4, src=src: e.transpose(pv[:, k4, :], src, identb[:]),
                                     reads=[R(tag, "kqT"), R(tag, "vT"), R("identb")], writes=wr)
                            dst = (ktok if which == 0 else vtok)[:, n0:n0 + 4, :]
                            P.op("act" if (n0 // 4) % 2 else "dve",
                                 (lambda e, dst=dst, pv=pv: e.copy(dst, pv)) if (n0 // 4) % 2 else (lambda e, dst=dst, pv=pv: e.tensor_copy(dst, pv)),
                                 reads=wr, writes=[R(tag, "tok", which)])

                def prep(h, d, n, sl):
                    col = d * 6 + h
                    gc = GC[:, n, col:col + 1]
                    ngc = NGC[:, n, col:col + 1]
                    rs = lambda nm: R(tag, nm, d, sl)
                    P.op("pool", lambda e: e.tensor_scalar(dg2[d][sl][:], cst[:, C_ID2:C_ID2 + 256], gc, 1.0, ALU.mult, ALU.mult),
                         reads=rg + [R("cst")], writes=[rs("dg2")])
                    yield
                    if DBG.get("cut", 99) < 2:
                        return
                    bmr = bk[1 + d][:, (sl % 2) * 256:(sl % 2) * 256 + 256]
                    rbm = [rbk(1 + d, (sl % 2) * 2), rbk(1 + d, (sl % 2) * 2 + 1)]
                    mc = C_MF if d == 0 else C_MB
                    P.op("pe", lambda e: e.matmul(bmr, ones32[:], dg2[d][sl][:], start=True, stop=False),
                         reads=[rs("dg2"), R("ones32")], writes=rbm)
                    P.op("pe", lambda e: e.matmul(bmr, cst[:, C_ID:C_ID + 128], cst[:, mc:mc + 256], start=False, stop=True),
                         reads=[R("cst")], writes=rbm)
                    yield
                    if DBG.get("cut", 99) < 3:
                        return
                    P.op("act", lambda e: e.activation(dS_[d][sl][:], bmr[:, 0:128], AF.Exp, bias=gc, scale=-1.0),
                         reads=rbm + rg, writes=[rs("dS")])
                    yield
                    P.op("act", lambda e: e.activation(dT_[d][sl][:], bmr[:, 128:256], AF.Exp, bias=ngc, scale=1.0),
                         reads=rbm + rg, writes=[rs("dT")])
                    yield
                    if DBG.get("cut", 99) < 4:
                        return
                    kk = bk[0][:, (sl % 2) * 256:(sl % 2) * 256 + 256]
                    rkk = [rbk(0, (sl % 2) * 2), rbk(0, (sl % 2) * 2 + 1)]
                    P.op("pe", lambda e: e.matmul(kk, kqT[:, n, 0, :], kqT[:, n, :, :].rearrange("p a t -> p (a t)"), start=True, stop=True),
                         reads=[R(tag, "kqT")], writes=rkk)
                    X = Xb[d][sl]
                    Y = Yb[d][sl]
                    Q = Qb[d][sl]
                    P.op("dve", lambda e: e.scalar_tensor_tensor(X[0][:], kk[:, 0:128], NBt[:, n, col:col + 1], dS_[d][sl][:], ALU.mult, ALU.mult),
                         reads=rkk + rg + [rs("dS")], writes=[rs("X0")])
                    P.op("dve", lambda e: e.tensor_tensor(inT[d][sl][:], kk[:, 128:256], dT_[d][sl][:], ALU.mult),
                         reads=rkk + [rs("dT")], writes=[rs("inT")])
                    yield
                    if DBG.get("cut", 99) < 5:
                        return
                    pr, rpr = dbl_region(d, 2 * (sl % 2))
                    P.op("pe", lambda e: e.matmul(pr, X[0][:], cst[:, C_ID:C_ID + 128], start=True, stop=True), reads=[rs("X0"), R("cst")], writes=[rpr])
                    yield
                    P.op("act", lambda e: e.copy(Y[0][:], pr), reads=[rpr], writes=[rs("Y0")])
                    yield
                    P.op("dve", lambda e: e.tensor_tensor(Q[0][:], pr, cst[:, C_ID:C_ID + 128], ALU.add), reads=[rpr, R("cst")], writes=[rs("Q0")])
                    yield
                    if DBG.get("cut", 99) < 6:
                        return
                    NL = 6
                    for l in range(NL):
                        a, b_ = l % 2, (l + 1) % 2
                        rX, rY, rQ = rs("X%d" % a), rs("Y%d" % a), rs("Q%d" % a)
                        rX2, rY2, rQ2 = rs("X%d" % b_), rs("Y%d" % b_), rs("Q%d" % b_)
                        px, rpx = dbl_region(d, 2 * (sl % 2))
                        P.op("pe", lambda e: e.matmul(px, Y[a][:], X[a][:], start=True, stop=True), reads=[rX, rY], writes=[rpx])
                        yield
                        if l < NL - 1 and not DBG.get("b1_tr", False):
                            py, rpy = dbl_region(d, 2 * (sl % 2) + 1)
                            P.op("pe", lambda e: e.matmul(py, X[a][:], Y[a][:], start=True, stop=True), reads=[rX, rY], writes=[rpy])
                            yield
                        P.op("act", lambda e: e.copy(X[b_][:], px), reads=[rpx], writes=[rX2])
                        yield
                        if l < NL - 1 and DBG.get("b1_tr", False):
                            py, rpy = dbl_region(d, 2 * (sl % 2) + 1)
                            P.op("pe", lambda e: e.transpose(py, X[b_][:], cst[:, C_ID:C_ID + 128]), reads=[rX2, R("cst")], writes=[rpy])
                            yield
                        if l < NL - 1:
                            P.op("dve", lambda e: e.tensor_copy(Y[b_][:], py), reads=[rpy], writes=[rY2])
                            yield
                        pq, rpq = dbl_region(d, 2 * (sl % 2))
                        P.op("pe", lambda e: e.matmul(pq, X[b_][:], Q[a][:], start=True, stop=True), reads=[rX2, rQ], writes=[rpq])
                        yield
                        if l < NL - 1:
                            P.op("dve", lambda e: e.tensor_tensor(Q[b_][:], pq, Q[a][:], ALU.add), reads=[rpq, rQ], writes=[rQ2])
                            yield
                        else:
                            P.op("dve", lambda e: e.tensor_tensor(TTb[d][sl][:], pq, Q[a][:], ALU.add), reads=[rpq, rQ], writes=[rs("TTb")])
                            yield
                    if DBG.get("cut", 99) < 7:
                        return
                    P.op("pool", lambda e: e.tensor_scalar(BV[d][sl][:], vtok[:, n, :], Ball[:, n, col:col + 1], 1.0, ALU.mult, ALU.mult),
                         reads=[R(tag, "tok", 1), R(tag, "Ball")], writes=[rs("BV")])
                    yield
                    P.op("pool", lambda e: e.tensor_scalar(kdec[d][sl][:], ktok[:, n, :], EKD[:, n, col:col + 1], 1.0, ALU.mult, ALU.mult),
                         reads=[R(tag, "tok", 0)] + rg, writes=[rs("kdec")])
                    yield

                def scan(h, d, n, sl, s):
                    col = d * 6 + h
                    rs = lambda nm: R(tag, nm, d, sl)
                    cur, nxt = s % 2, (s + 1) % 2
                    k2 = s % 2
                    sbank = bk[5 + d]
                    r_kS, r_vn, r_dS, r_qS = [rbk(5 + d, j) for j in range(4)]
                    pkS, pvn, pdS, pqS = [sbank[:, j * 128:(j + 1) * 128] for j in range(4)]
                    oi = bk[7][:, (d * 2 + k2) * 128:(d * 2 + k2 + 1) * 128]
                    r_oi = rbk(7, d * 2 + k2)
                    rSb = R(tag, "Sbf", d, cur)
                    rSn = R(tag, "Sbf", d, nxt)
                    P.op("pe", lambda e: e.matmul(pkS, kqT[:, n, 0, :], Sbf[d][cur][:], start=True, stop=True),
                         reads=[R(tag, "kqT"), rSb], writes=[r_kS])
                    yield
                    P.op("dve", lambda e: e.scalar_tensor_tensor(Rt[d][k2][:], pkS, NBEG[:, n, col:col + 1], BV[d][sl][:], ALU.mult, ALU.add),
                         reads=[r_kS, rs("BV")] + rg, writes=[R(tag, "Rt", d, k2)])
                    yield
                    P.op("pe", lambda e: e.matmul(pvn, TTb[d][sl][:], Rt[d][k2][:], start=True, stop=True),
                         reads=[rs("TTb"), R(tag, "Rt", d, k2)], writes=[r_vn])
                    yield
                    P.op("act", lambda e: e.copy(vnb[d][k2][:], pvn), reads=[r_vn], writes=[R(tag, "vnb", d, k2)])
                    yield
                    P.op("pe", lambda e: e.matmul(pdS, kdec[d][sl][:], vnb[d][k2][:], start=True, stop=True),
                         reads=[rs("kdec"), R(tag, "vnb", d, k2)], writes=[r_dS])
                    yield
                    P.op("dve", lambda e: e.scalar_tensor_tensor(Sbf[d][nxt][:], S32[d][:], EGL[:, n, col:col + 1], pdS, ALU.mult, ALU.add),
                         reads=[R(tag, "S32", d), r_dS] + rg, writes=[rSn])
                    yield
                    P.op("dve", lambda e: e.scalar_tensor_tensor(S32[d][:], S32[d][:], EGL[:, n, col:col + 1], pdS, ALU.mult, ALU.add),
                         reads=[R(tag, "S32", d), r_dS] + rg, writes=[R(tag, "S32", d)])
                    yield
                    P.op("pe", lambda e: e.matmul(pqS, kqT[:, n, 1, :], Sbf[d][cur][:], start=True, stop=True),
                         reads=[R(tag, "kqT"), rSb], writes=[r_qS])
                    yield
                    P.op("pe", lambda e: e.matmul(oi, inT[d][sl][:], vnb[d][k2][:], start=True, stop=True),
                         reads=[rs("inT"), R(tag, "vnb", d, k2)], writes=[r_oi])
                    yield
                    P.op("act", lambda e: e.activation(oq[d][k2][:], pqS, AF.Copy, scale=EG[:, n, col:col + 1]),
                         reads=[r_qS] + rg, writes=[R(tag, "oq", d, k2)])
                    yield
                    P.op("dve", lambda e: e.tensor_tensor(oo[d][k2][:], oq[d][k2][:], oi, ALU.add),
                         reads=[R(tag, "oq", d, k2), r_oi], writes=[R(tag, "oo", d, k2)])
                    yield
                    dst = (OF if d == 0 else OB)[n * 128:(n + 1) * 128, h * 128:(h + 1) * 128]
                    P.dma("pool", dst, oo[d][k2][:], reads=[R(tag, "oo", d, k2)])
                    yield

                def run_rr(gens):
                    gens = list(gens)
                    while gens:
                        for g in list(gens):
                            try:
                                next(g)
                            except StopIteration:
                                gens.remove(g)

                def scan2(h, d, steps):
                    for s in steps:
                        n = s if d == 0 else NCH - 1 - s
                        for _ in scan(h, d, n, s % NSL, s):
                            yield

                def nof(d, s):
                    return s if d == 0 else NCH - 1 - s

                for h in range(DBG["b1_heads"]):
                    if DBG["b1_stage"] >= 1:
                        head_prep(h)
                    if DBG["b1_stage"] < 2:
                        continue
                    NCHR = DBG["b1_steps"]
                    for d in range(2):
                        P.op("pool", lambda e, d=d: e.memset(S32[d][:], 0.0), writes=[R(tag, "S32", d)])
                        P.op("pool", lambda e, d=d: e.memset(Sbf[d][0][:], 0.0), writes=[R(tag, "Sbf", d, 0)])
                    GRP = 2
                    for r0 in range(0, NCHR + GRP, GRP):
                        gens = []
                        for s in range(r0, min(r0 + GRP, NCHR)):
                            for d in range(2):
                                gens.append(prep(h, d, nof(d, s), s % NSL))
                        if r0 >= GRP and DBG["b1_stage"] >= 3:
                            for d in range(2):
                                gens.append(scan2(h, d, range(r0 - GRP, min(r0, NCHR))))
                        run_rr(gens)
                P.emit()
            return P.nops

        nops = {}
        with nc.sbuf_tensor("QmT0", [128, 2, S], BF16) as qm0:
            qm_box[0] = qm0
            if want("A0"):
                nops["A0"] = phase_A(0)
            if want("B0"):
                nops["B0"] = phase_B0()
            if want("C0"):
                nops["C0"] = phase_C(0)
        if want("D0"):
            nops["D0"] = phase_D(0)
        with nc.sbuf_tensor("QmT1", [128, 2, S], BF16) as qm1:
            qm_box[0] = qm1
            if want("A1"):
                nops["A1"] = phase_A(1)
            if want("B1"):
                nops["B1"] = phase_B1()
            if want("C1"):
                nops["C1"] = phase_C(1)
        if want("D1"):
            nops["D1"] = phase_D(1)
    return nc, nops


def make_in_maps(inputs):
    f = lambda a: np.ascontiguousarray(np.asarray(a, dtype=np.float32))
    consts = make_consts()
    biasg = gather_bias(f(inputs["rel_bias"]))

    def pc(v):
        return f(v).reshape(8, 128).T
    gains = np.stack([pc(inputs["norm_mix_pre"][0]), pc(inputs["norm_mix_pre"][1]),
                      pc(inputs["norm_ffn_pre"][0]), pc(inputs["norm_ffn_pre"][1]),
                      pc(inputs["mem_norm"][0]), pc(inputs["mem_norm"][1])], axis=1)
    shared = {
        "biasg": biasg, "consts": consts, "gains_pc": f(gains),
        "att_w_in": f(inputs["att_w_in"][0]), "att_w_out": f(inputs["att_w_out"][0]),
        "dn_w_in": f(inputs["dn_w_in"][0]), "dn_convT": f(np.asarray(inputs["dn_conv"][0]).T),
        "dn_a_log": f(inputs["dn_a_log"][0]).reshape(12), "dn_dt_bias": f(inputs["dn_dt_bias"][0]).reshape(12),
        "dn_out_norm": f(inputs["dn_out_norm"][0]), "dn_w_out": f(inputs["dn_w_out"][0]),
        "mem_w_kv": f(inputs["mem_w_kv"]), "norm_mix_post": f(inputs["norm_mix_post"]),
        "norm_ffn_post": f(inputs["norm_ffn_post"]), "ffn_wgu": f(inputs["ffn_w_gate_up"]),
        "ffn_wd": f(inputs["ffn_w_down"]),
    }
    x = f(inputs["x"])
    mem = f(inputs["mem"])
    maps = []
    for b in range(8):
        m = dict(shared)
        m["x"] = x[b]
        m["mem"] = mem[b]
        maps.append(m)
    return maps


def kernel(**inputs):
    nc, _ = build()
    maps = make_in_maps(inputs)
    res = run_bass_kernel_spmd(nc, maps, core_ids=list(range(8)))
    return np.stack([np.asarray(r["y"], dtype=np.float32) for r in res.results], axis=0)
```
